# Optimizing a Trainium2 kernel written in Bass

```python
import jax, jax.numpy as jnp
from jax import lax
import numpy as np

D_MODEL = 1024
BATCH = 4
SEQ = 4096
DEPTH = 1
DEC_BATCH = 32
DEC_SEQ = 8
PAST_LEN = 8192
PAGE_SIZE = 128

SB_HEADS = 8
SB_HEAD_DIM = 64
SB_WIDTH = SB_HEADS * SB_HEAD_DIM
Q_BLOCK = 128
SB_BIAS_HI = -6.0
SB_BIAS_LO = -11.0
RET_HEADS = 4
RET_QK_DIM = 128
RET_V_DIM = 256
RET_QK_WIDTH = RET_HEADS * RET_QK_DIM
RET_V_WIDTH = RET_HEADS * RET_V_DIM
RET_CHUNK = 128
ROPE_BASE = 10000.0
D_FF = 2816
NORM_EPS = 1e-6
POOL_NUM = 5
POOL_DEN = 4
IN_WIDTH = 3 * SB_WIDTH + 2 * RET_QK_WIDTH + 2 * RET_V_WIDTH + 2 * D_MODEL

kernel_name = 'stickbreak_retention_hybrid_step'

F32 = jnp.float32


def rms_norm(x, g):
    xf = x.astype(F32)
    y = xf * lax.rsqrt(jnp.mean(xf * xf, axis=-1, keepdims=True) + NORM_EPS)
    return (y * g.astype(F32)).astype(x.dtype)


def swiglu(x, w_gu, w_down):
    g, u = jnp.split(x @ w_gu, 2, axis=-1)
    return (jax.nn.silu(g) * u) @ w_down


def rope(x, pos):
    half = x.shape[-1] // 2
    freq = ROPE_BASE ** (-jnp.arange(half, dtype=F32) / half)
    ang = pos.astype(F32)[:, None] * freq[None, :]
    cos = jnp.cos(ang)[None, :, None, :]
    sin = jnp.sin(ang)[None, :, None, :]
    xf = x.astype(F32)
    x1, x2 = xf[..., :half], xf[..., half:]
    return jnp.concatenate([x1 * cos - x2 * sin, x2 * cos + x1 * sin], axis=-1).astype(x.dtype)


def sb_block(q, k, v, q_pos, bias):
    z = jnp.einsum('bqhd,bkhd->bhqk', q.astype(F32), k.astype(F32)) * (SB_HEAD_DIM ** -0.5)
    z = z + bias.astype(F32)[None, :, None, None]
    k_pos = jnp.arange(k.shape[1])
    mask = k_pos[None, :] < q_pos[:, None]
    log_stay = jnp.where(mask, jax.nn.log_sigmoid(-z), 0.0)
    tail = lax.cumsum(log_stay, axis=3, reverse=True) - log_stay
    a = jnp.where(mask, jnp.exp(jax.nn.log_sigmoid(z) + tail), 0.0)
    return jnp.einsum('bhqk,bkhd->bqhd', a, v.astype(F32)).astype(q.dtype)


def stick_breaking(q, k, v, q_start, bias):
    B, Tq, H, D = q.shape
    qb = Q_BLOCK if Tq % Q_BLOCK == 0 else Tq
    nb = Tq // qb
    q_blocks = q.reshape(B, nb, qb, H, D).transpose(1, 0, 2, 3, 4)
    pos = (q_start + jnp.arange(Tq)).reshape(nb, qb)
    out = lax.map(lambda a: sb_block(a[0], k, v, a[1], bias), (q_blocks, pos))
    return out.transpose(1, 0, 2, 3, 4).reshape(B, Tq, H, D)


def retention(q, k, v, state):
    B, T, H, DK = q.shape
    DV = v.shape[-1]
    C = RET_CHUNK if T % RET_CHUNK == 0 else T
    n = T // C
    log_gamma = jnp.log1p(-jnp.exp2(-5.0 - jnp.arange(H, dtype=F32)))
    idx = jnp.arange(C, dtype=F32)
    diff = idx[:, None] - idx[None, :]
    decay_in = jnp.where(diff >= 0, jnp.exp(log_gamma[:, None, None] * jnp.maximum(diff, 0.0)), 0.0)
    decay_q = jnp.exp(log_gamma[:, None] * (idx + 1.0))[None, :, :, None]
    decay_k = jnp.exp(log_gamma[:, None] * (C - 1.0 - idx))[None, :, :, None]
    decay_c = jnp.exp(log_gamma * C)[None, :, None, None]

    def chunks(a):
        return a.astype(F32).reshape(B, n, C, H, a.shape[-1]).transpose(1, 0, 3, 2, 4)

    qc, kc, vc = chunks(q), chunks(k * (DK ** -0.5)), chunks(v)

    def step(S, inp):
        qi, ki, vi = inp
        inner = jnp.einsum('bhcd,bhsd->bhcs', qi, ki) * decay_in[None]
        o = jnp.einsum('bhcs,bhsv->bhcv', inner, vi) + jnp.einsum('bhcd,bhdv->bhcv', qi, S) * decay_q
        S = S * decay_c + jnp.einsum('bhsd,bhsv->bhdv', ki * decay_k, vi)
        return S, o

    S, o = lax.scan(step, state.astype(F32), (qc, kc, vc))
    o = o.transpose(1, 0, 3, 2, 4).reshape(B, T, H, DV)
    return o.astype(v.dtype), S.astype(state.dtype)


def mixer(u, k_past, v_past, ret_state, q_start, w_in, sb_bias, ret_gn_g, w_sb_out, w_ret_out, w_o):
    B, T, _ = u.shape
    sizes = [SB_WIDTH] * 3 + [RET_QK_WIDTH] * 2 + [RET_V_WIDTH] * 2 + [D_MODEL] * 2
    cuts = [int(c) for c in np.cumsum(sizes)[:-1]]
    q_sb, k_sb, v_sb, q_r, k_r, v_r, g_r, a_sb, a_r = jnp.split(u @ w_in, cuts, axis=-1)
    pos = q_start + jnp.arange(T)
    q_sb = q_sb.reshape(B, T, SB_HEADS, SB_HEAD_DIM)
    k_sb = k_sb.reshape(B, T, SB_HEADS, SB_HEAD_DIM)
    v_sb = v_sb.reshape(B, T, SB_HEADS, SB_HEAD_DIM)
    if k_past is None:
        keys, vals = k_sb, v_sb
    else:
        keys = jnp.concatenate([k_past.astype(k_sb.dtype), k_sb], axis=1)
        vals = jnp.concatenate([v_past.astype(v_sb.dtype), v_sb], axis=1)
    o_sb = stick_breaking(q_sb, keys, vals, q_start, sb_bias).reshape(B, T, SB_WIDTH)
    q_r = rope(q_r.reshape(B, T, RET_HEADS, RET_QK_DIM), pos)
    k_r = rope(k_r.reshape(B, T, RET_HEADS, RET_QK_DIM), pos)
    v_r = v_r.reshape(B, T, RET_HEADS, RET_V_DIM)
    o_r, new_state = retention(q_r, k_r, v_r, ret_state)
    o_rf = o_r.astype(F32)
    o_rf = o_rf * lax.rsqrt(jnp.mean(o_rf * o_rf, axis=-1, keepdims=True) + NORM_EPS)
    o_r = (o_rf.reshape(B, T, RET_V_WIDTH) * ret_gn_g.astype(F32)).astype(u.dtype)
    o_r = jax.nn.silu(g_r) * o_r
    m = jax.nn.sigmoid(a_sb) * (o_sb @ w_sb_out) + jax.nn.sigmoid(a_r) * (o_r @ w_ret_out)
    return m @ w_o, k_sb, v_sb, new_state


def layer(x, k_past, v_past, ret_state, q_start, p):
    (g_f1_pre, w_f1_gu, w_f1_down, g_f1_post, g_m_pre, w_in, sb_bias, ret_gn_g, w_sb_out, w_ret_out, w_o,
     g_m_post, g_f2_pre, w_f2_gu, w_f2_down, g_f2_post) = p
    h = x + 0.5 * rms_norm(swiglu(rms_norm(x, g_f1_pre), w_f1_gu, w_f1_down), g_f1_post)
    mix, k_new, v_new, s_new = mixer(rms_norm(h, g_m_pre), k_past, v_past, ret_state, q_start,
                                     w_in, sb_bias, ret_gn_g, w_sb_out, w_ret_out, w_o)
    h = h + rms_norm(mix, g_m_post)
    h = h + 0.5 * rms_norm(swiglu(rms_norm(h, g_f2_pre), w_f2_gu, w_f2_down), g_f2_post)
    return h, k_new, v_new, s_new


def setup_inputs(seed: int = 0) -> dict:
    key = jax.random.key(seed)
    ks = jax.random.split(key, 32)
    n_pages = PAST_LEN // PAGE_SIZE
    n_used = DEC_BATCH * n_pages
    n_pool = (n_used * POOL_NUM) // POOL_DEN

    def w(k, shape, fan_in):
        return jax.random.normal(k, shape, F32) * (fan_in ** -0.5)

    def gain(k, dim):
        return 1.0 + 0.02 * jax.random.normal(k, (DEPTH, dim), F32)

    page_table = jax.random.permutation(ks[5], n_pool)[:n_used].reshape(DEC_BATCH, n_pages).astype(jnp.int32)
    sb_bias = (jnp.linspace(SB_BIAS_HI, SB_BIAS_LO, SB_HEADS, dtype=F32)[None, :]
               + 0.1 * jax.random.normal(ks[21], (DEPTH, SB_HEADS), F32))
    return {
        'x_prompt': jax.random.normal(ks[0], (BATCH, SEQ, D_MODEL), F32),
        'x_sample': jax.random.normal(ks[1], (DEC_BATCH, DEC_SEQ, D_MODEL), F32),
        'cache_k': jax.random.normal(ks[2], (DEPTH, n_pool, PAGE_SIZE, SB_HEADS, SB_HEAD_DIM), F32),
        'cache_v': jax.random.normal(ks[3], (DEPTH, n_pool, PAGE_SIZE, SB_HEADS, SB_HEAD_DIM), F32),
        'state_ret': jax.random.normal(ks[4], (DEPTH, DEC_BATCH, RET_HEADS, RET_QK_DIM, RET_V_DIM), F32),
        'page_table': page_table,
        'g_ffn1_pre': gain(ks[6], D_MODEL),
        'w_ffn1_gu': w(ks[7], (DEPTH, D_MODEL, 2 * D_FF), D_MODEL),
        'w_ffn1_down': w(ks[8], (DEPTH, D_FF, D_MODEL), D_FF),
        'g_ffn1_post': gain(ks[9], D_MODEL),
        'g_mix_pre': gain(ks[10], D_MODEL),
        'w_in': w(ks[11], (DEPTH, D_MODEL, IN_WIDTH), D_MODEL),
        'sb_bias': sb_bias,
        'ret_gn_g': gain(ks[12], RET_V_WIDTH),
        'w_sb_out': w(ks[13], (DEPTH, SB_WIDTH, D_MODEL), SB_WIDTH),
        'w_ret_out': w(ks[14], (DEPTH, RET_V_WIDTH, D_MODEL), RET_V_WIDTH),
        'w_o': w(ks[15], (DEPTH, D_MODEL, D_MODEL), D_MODEL),
        'g_mix_post': gain(ks[16], D_MODEL),
        'g_ffn2_pre': gain(ks[17], D_MODEL),
        'w_ffn2_gu': w(ks[18], (DEPTH, D_MODEL, 2 * D_FF), D_MODEL),
        'w_ffn2_down': w(ks[19], (DEPTH, D_FF, D_MODEL), D_FF),
        'g_ffn2_post': gain(ks[20], D_MODEL),
    }


def reference(x_prompt, x_sample, cache_k, cache_v, state_ret, page_table,
              g_ffn1_pre, w_ffn1_gu, w_ffn1_down, g_ffn1_post, g_mix_pre, w_in, sb_bias, ret_gn_g,
              w_sb_out, w_ret_out, w_o, g_mix_post, g_ffn2_pre, w_ffn2_gu, w_ffn2_down, g_ffn2_post):
    n_seq, n_pages = page_table.shape
    past_len = n_pages * PAGE_SIZE
    hp, hs = x_prompt, x_sample
    kp_l, vp_l, sp_l, ks_l, vs_l, ss_l = [], [], [], [], [], []
    for l in range(DEPTH):
        p = (g_ffn1_pre[l], w_ffn1_gu[l], w_ffn1_down[l], g_ffn1_post[l], g_mix_pre[l], w_in[l],
             sb_bias[l], ret_gn_g[l], w_sb_out[l], w_ret_out[l], w_o[l], g_mix_post[l], g_ffn2_pre[l],
             w_ffn2_gu[l], w_ffn2_down[l], g_ffn2_post[l])
        s0 = jnp.zeros((hp.shape[0], RET_HEADS, RET_QK_DIM, RET_V_DIM), state_ret.dtype)
        hp, kp, vp, sp = layer(hp, None, None, s0, 0, p)
        k_past = cache_k[l][page_table].reshape(n_seq, past_len, SB_HEADS, SB_HEAD_DIM)
        v_past = cache_v[l][page_table].reshape(n_seq, past_len, SB_HEADS, SB_HEAD_DIM)
        hs, kn, vn, sn = layer(hs, k_past, v_past, state_ret[l], past_len, p)
        kp_l.append(kp); vp_l.append(vp); sp_l.append(sp)
        ks_l.append(kn); vs_l.append(vn); ss_l.append(sn)
    return (hp, hs, jnp.stack(kp_l), jnp.stack(vp_l), jnp.stack(sp_l),
            jnp.stack(ks_l), jnp.stack(vs_l), jnp.stack(ss_l))
```

```python
import numpy as np
import ml_dtypes
import concourse.bass as bass
import concourse.mybir as mybir
from concourse.bass_utils import run_bass_kernel_spmd

F32 = mybir.dt.float32
BF16 = mybir.dt.bfloat16
I32 = mybir.dt.int32
AF = mybir.ActivationFunctionType
ALU = mybir.AluOpType

D = 1024
DFF = 2816
NCH = 22
SEQ = 4096
HALF = 2048
TN = 512
NT = HALF // TN
SB_H = 8
RET_H = 4
PAGE = 128
NPAGES = 64
NPOOL = 2560
SEQ_PER_CORE = 4
DEC = 8
EPS = 1e-6
NSLOT = 6
SLOT_E = 2048

W_IN_COLS = dict(q_sb=0, k_sb=512, v_sb=1024, q_r=1536, k_r=2048, v_r=2560, g_r=3584, a_sb=4608, a_r=5632)


class Sem:
    __slots__ = ("h", "count")

    def __init__(self, h):
        self.h = h
        self.count = 0


class Eng:
    def __init__(self, name, h, sem):
        self.name = name
        self.h = h
        self.sem = sem
        self.waited = {}


class Buf:
    __slots__ = ("name", "lo", "hi", "lastw", "reads", "dsem", "ovl", "ver", "space")

    def __init__(self, name, space, lo=0, hi=0):
        self.name = name
        self.space = space
        self.lo = lo
        self.hi = hi
        self.lastw = None
        self.reads = {}
        self.dsem = None
        self.ovl = None
        self.ver = -1


class V:
    __slots__ = ("ap", "buf")

    def __init__(self, ap, buf):
        self.ap = ap
        self.buf = buf

    def __getitem__(self, idx):
        return V(self.ap[idx], self.buf)

    def re(self, pat, **kw):
        return V(self.ap.rearrange(pat, **kw), self.buf)

    def bc(self, dt):
        return V(self.ap.bitcast(dt), self.buf)


class Tile:
    def __init__(self, T, name, shape, dt, off):
        self.T = T
        self.name = name
        self.shape = list(shape)
        self.isz = 2 if dt == BF16 else 4
        self.off = off
        self.t = T.nc.alloc_sbuf_tensor_at(name, list(shape), dt, offset=off)
        self.free = self.shape[1:]
        self.strides = []
        s = 1
        for d in reversed(self.free):
            self.strides.insert(0, s)
            s *= d
        self.nbytes = s * self.isz
        self.bufs = {}

    def __call__(self, *idx):
        lo = 0
        hi = 0
        full = []
        for k, d in enumerate(self.free):
            if k < len(idx):
                i = idx[k]
                if isinstance(i, int):
                    a, b = i, i + 1
                    full.append(i)
                else:
                    a = 0 if i.start is None else i.start
                    b = d if i.stop is None else i.stop
                    full.append(slice(a, b))
            else:
                a, b = 0, d
                full.append(slice(0, d))
            lo += a * self.strides[k]
            hi += (b - 1) * self.strides[k]
        hi += 1
        key = (lo, hi)
        b = self.bufs.get(key)
        if b is None:
            b = Buf(f"{self.name}{key}", "sb", self.off + lo * self.isz, self.off + hi * self.isz)
            self.bufs[key] = b
            self.T.sb_bufs.append(b)
            self.T.sb_ver += 1
        ap = self.t[tuple([slice(None)] + full)]
        return V(ap, b)


class Tracker:
    def __init__(self, nc):
        self.nc = nc
        self.dry = False
        self.sb_bufs = []
        self.sb_ver = 0
        self.all_sems = []
        mk = lambda n: self.new_sem(n)
        self.PE = Eng("pe", nc.tensor, mk("s_pe"))
        self.ACT = Eng("act", nc.scalar, mk("s_act"))
        self.DVE = Eng("dve", nc.vector, mk("s_dve"))
        self.POOL = Eng("pool", nc.gpsimd, mk("s_pool"))
        self.SP = Eng("sp", nc.sync, None)
        self.engs = [self.PE, self.ACT, self.DVE, self.POOL, self.SP]
        self.out_stamps = []
        self.cursor = (nc.sbuf_base + 63) // 64 * 64
        self.top = nc.sbuf_top
        self.pe_open = False
        self.ninst = 0

    def new_sem(self, name):
        s = Sem(self.nc.alloc_semaphore(name))
        self.all_sems.append(s)
        return s

    def alloc(self, name, shape, dt, at=None):
        isz = 2 if dt == BF16 else 4
        nb = int(np.prod(shape[1:])) * isz
        nb = (nb + 63) // 64 * 64
        if at is None:
            at = self.cursor
            self.cursor += nb
            assert self.cursor <= self.top, f"SBUF overflow at {name}: {self.cursor} > {self.top}"
        return Tile(self, name, shape, dt, at)

    def dram_v(self, ap, name="d"):
        return V(ap, Buf(name, "dram"))

    def _ovl(self, b):
        if b.space == "psum":
            if b.ovl is None:
                b.ovl = [o for o in self.ps_bufs if o.lo < b.hi and b.lo < o.hi]
            return b.ovl
        if b.space != "sb":
            return (b,)
        if b.ver != self.sb_ver:
            b.ovl = [o for o in self.sb_bufs if o.lo < b.hi and b.lo < o.hi]
            b.ver = self.sb_ver
        return b.ovl

    def _deps(self, eng, reads, writes):
        need = {}
        pe_sem = self.PE.sem
        is_pe = eng is self.PE
        for v in reads:
            for b in self._ovl(v.buf):
                st = b.lastw
                if st is not None:
                    if not (is_pe and st[0] is pe_sem) and need.get(st[0], 0) < st[1]:
                        need[st[0]] = st[1]
                if b.space == "psum":
                    for sem, val in b.reads.items():
                        if sem is not eng.sem and need.get(sem, 0) < val:
                            need[sem] = val
        for v in writes:
            for b in self._ovl(v.buf):
                st = b.lastw
                if st is not None:
                    if not (is_pe and st[0] is pe_sem) and need.get(st[0], 0) < st[1]:
                        need[st[0]] = st[1]
                for sem, val in b.reads.items():
                    if not (is_pe and sem is pe_sem) and need.get(sem, 0) < val:
                        need[sem] = val
        for sem, val in need.items():
            if eng.waited.get(sem, 0) < val:
                eng.h.wait_ge(sem.h, val)
                eng.waited[sem] = val
                self.ninst += 1

    def _mark(self, stamp, reads, writes):
        sem, val = stamp
        for v in writes:
            b = v.buf
            b.lastw = stamp
            b.reads = {}
        for v in reads:
            b = v.buf
            if b.reads.get(sem, 0) < val:
                b.reads[sem] = val

    def _emit(self, eng, fn, reads, writes):
        if self.dry:
            return
        reads = [r for r in reads if isinstance(r, V)]
        self._deps(eng, reads, writes)
        inst = fn()
        eng.sem.count += 1
        inst.then_inc(eng.sem.h, 1)
        self.ninst += 1
        self._mark((eng.sem, eng.sem.count), reads, writes)

    def barrier(self):
        if self.dry:
            return
        for e in self.engs:
            for s in self.all_sems:
                if s.count > 0 and e.waited.get(s, 0) < s.count:
                    e.h.wait_ge(s.h, s.count)
                    e.waited[s] = s.count

    def mm(self, out, lhsT, rhs, start, stop, last):
        if self.dry:
            return
        PE = self.PE
        stamp = (PE.sem, PE.sem.count + 1)
        self._deps(PE, [lhsT, rhs], [out])
        inst = self.nc.tensor.matmul(out.ap, lhsT=lhsT.ap, rhs=rhs.ap, start=start, stop=stop,
                                     skip_group_check=True)
        self.ninst += 1
        self._mark(stamp, [lhsT, rhs], [out])
        self.pe_open = True
        if last:
            inst.then_inc(PE.sem.h, 1)
            PE.sem.count += 1
            self.pe_open = False

    def tr(self, out, in_, ident, last):
        if self.dry:
            return
        PE = self.PE
        stamp = (PE.sem, PE.sem.count + 1)
        self._deps(PE, [in_, ident], [out])
        inst = self.nc.tensor.transpose(out=out.ap, in_=in_.ap, identity=ident.ap)
        self.ninst += 1
        self._mark(stamp, [in_, ident], [out])
        self.pe_open = True
        if last:
            inst.then_inc(PE.sem.h, 1)
            PE.sem.count += 1
            self.pe_open = False

    def act(self, out, in_, func, bias=None, scale=None, accum=None):
        kw = {}
        if bias is not None:
            kw["bias"] = bias.ap if isinstance(bias, V) else bias
        if scale is not None:
            kw["scale"] = scale.ap if isinstance(scale, V) else scale
        writes = [out]
        if accum is not None:
            kw["accum_out"] = accum.ap
            writes.append(accum)
        self._emit(self.ACT, lambda: self.nc.scalar.activation(out=out.ap, in_=in_.ap, func=func, **kw),
                   [in_, bias, scale], writes)

    def _veng(self, e):
        return (self.DVE, self.nc.vector) if e == "dve" else (self.POOL, self.nc.gpsimd)

    def tt(self, e, out, in0, in1, op):
        E, h = self._veng(e)
        self._emit(E, lambda: h.tensor_tensor(out=out.ap, in0=in0.ap, in1=in1.ap, op=op), [in0, in1], [out])

    def ts(self, e, out, in0, s1, s2, op0, op1=None):
        E, h = self._veng(e)
        a1 = s1.ap if isinstance(s1, V) else s1
        a2 = s2.ap if isinstance(s2, V) else s2
        if op1 is None:
            fn = lambda: h.tensor_scalar(out=out.ap, in0=in0.ap, scalar1=a1, scalar2=None, op0=op0)
        else:
            fn = lambda: h.tensor_scalar(out=out.ap, in0=in0.ap, scalar1=a1, scalar2=a2, op0=op0, op1=op1)
        self._emit(E, fn, [in0, s1, s2], [out])

    def stt(self, out, in0, scalar, in1, op0, op1):
        sc = scalar.ap if isinstance(scalar, V) else scalar
        self._emit(self.DVE, lambda: self.nc.vector.scalar_tensor_tensor(
            out=out.ap, in0=in0.ap, scalar=sc, in1=in1.ap, op0=op0, op1=op1), [in0, scalar, in1], [out])

    def copy(self, e, out, in_):
        if e == "act":
            self.act(out, in_, AF.Copy)
            return
        E, h = self._veng(e)
        self._emit(E, lambda: h.tensor_copy(out=out.ap, in_=in_.ap), [in_], [out])

    def recip(self, out, in_):
        self._emit(self.DVE, lambda: self.nc.vector.reciprocal(out=out.ap, in_=in_.ap), [in_], [out])

    def memset(self, out, val):
        self._emit(self.DVE, lambda: self.nc.vector.memset(out.ap, val), [], [out])

    def _dsem(self, b):
        if b.dsem is None:
            b.dsem = self.new_sem(f"s_dma{len(self.all_sems)}")
        return b.dsem

    def dma(self, out, in_, q=None, is_output=False, nonctg=False):
        if self.dry:
            return
        q = q or self.SP
        side = out if out.buf.space == "sb" else in_
        sem = self._dsem(side.buf)
        self._deps(q, [in_], [out])
        if nonctg:
            with self.nc.allow_non_contiguous_dma(reason="tiny strided constant load"):
                inst = q.h.dma_start(out=out.ap, in_=in_.ap)
        else:
            inst = q.h.dma_start(out=out.ap, in_=in_.ap)
        inst.then_inc(sem.h, 16)
        sem.count += 16
        self.ninst += 1
        st = (sem, sem.count)
        self._mark(st, [in_], [out])
        if is_output:
            self.out_stamps.append(st)

    def gather(self, out, table_ap, idx):
        if self.dry:
            return
        q = self.POOL
        sem = self._dsem(out.buf)
        self._deps(q, [idx], [out])
        inst = self.nc.gpsimd.indirect_dma_start(
            out=out.ap, out_offset=None, in_=table_ap,
            in_offset=bass.IndirectOffsetOnAxis(ap=idx.ap, axis=0))
        inst.then_inc(sem.h, 16)
        sem.count += 16
        self.ninst += 1
        self._mark((sem, sem.count), [idx], [out])

    def finish(self):
        if self.dry:
            return
        assert not self.pe_open
        sp = self.SP
        for sem, val in self.out_stamps:
            if sp.waited.get(sem, 0) < val:
                sp.h.wait_ge(sem.h, val)
                sp.waited[sem] = val


def weight_blocks(dr):
    blocks = []

    def pview(W2d, r0, kc):
        return W2d[r0:r0 + kc * 128, :].rearrange("(k p) n -> p k n", p=128)

    for f in (1, 2):
        gu = dr[f"w_ffn{f}_gu"][0]
        dn = dr[f"w_ffn{f}_down"][0]
        gv = pview(gu, 0, 8)
        for c in range(NCH):
            blocks.append(((f"f{f}gu", c), 8, 256,
                           [(gv[:, :, c * 128:(c + 1) * 128], 0, 128),
                            (gv[:, :, DFF + c * 128:DFF + (c + 1) * 128], 128, 128)]))
        for g in range(NCH // 2):
            dv = pview(dn, g * 256, 2)
            blocks.append(((f"f{f}dn", g), 2, 1024, [(dv[:, :, :], 0, 1024)]))
    win = pview(dr["w_in"][0], 0, 8)
    for name, nblk in (("q_sb", 2), ("k_sb", 2), ("v_sb", 2), ("q_r", 2), ("k_r", 2), ("v_r", 4), ("g_r", 4),
                       ("a_sb", 4), ("a_r", 4)):
        c0 = W_IN_COLS[name]
        for j in range(nblk):
            blocks.append(((name, j), 8, 256, [(win[:, :, c0 + j * 256:c0 + (j + 1) * 256], 0, 256)]))
    for name in ("q_r", "k_r"):
        c0 = W_IN_COLS[name]
        for j in range(2):
            pcs = []
            for cc in range(2):
                h0 = c0 + (2 * j + cc) * 128
                pcs.append((win[:, :, h0 + 64:h0 + 128], cc * 128, 64))
                pcs.append((win[:, :, h0:h0 + 64], cc * 128 + 64, 64))
            blocks.append(((name + "_sw", j), 8, 256, pcs))
    sbo = pview(dr["w_sb_out"][0], 0, 4)
    for j in range(2):
        blocks.append((("sbo", j), 4, 512, [(sbo[:, :, j * 512:(j + 1) * 512], 0, 512)]))
    reto = pview(dr["w_ret_out"][0], 0, 8)
    wo = pview(dr["w_o"][0], 0, 8)
    for j in range(4):
        blocks.append((("reto", j), 8, 256, [(reto[:, :, j * 256:(j + 1) * 256], 0, 256)]))
        blocks.append((("wo", j), 8, 256, [(wo[:, :, j * 256:(j + 1) * 256], 0, 256)]))
    return blocks


class WStream:
    def __init__(self, T, slots, scr_views, plan):
        self.T = T
        self.slots = slots
        self.scr = scr_views
        self.plan = plan
        self.rec = []
        self.i = 0
        self.loaded = 0

    def get(self, key, hold=0):
        T = self.T
        if T.dry:
            self.rec.append(key)
            _, kc, bw = self.scr[key]
            sl = self.slots[0]
            return V(sl.t[:, 0:kc * bw].rearrange("p (k n) -> p k n", k=kc), sl().buf)
        i = self.i
        assert self.plan[i] == key, (i, self.plan[i], key)
        upto = min(len(self.plan) - 1, i + NSLOT - 1 - hold)
        upto = max(upto, i)
        while self.loaded <= upto:
            j = self.loaded
            k = self.plan[j]
            src, kc, bw = self.scr[k]
            sl = self.slots[j % NSLOT]
            T.dma(sl(slice(0, kc * bw)), src)
            self.loaded += 1
        self.i += 1
        _, kc, bw = self.scr[key]
        sl = self.slots[i % NSLOT]
        v = sl(slice(0, kc * bw))
        return V(v.ap.rearrange("p (k n) -> p k n", k=kc), v.buf)


def gammas():
    h = np.arange(RET_H, dtype=np.float64)
    return np.exp(np.log1p(-np.exp2(-5.0 - h)))


def host_constants():
    c = {}
    c["ident_bf"] = np.eye(128, dtype=np.float32).astype(ml_dtypes.bfloat16)
    c["ident_f"] = np.eye(128, dtype=np.float32)
    j = np.arange(128)[:, None]
    k = np.arange(128)[None, :]
    c["triS"] = (j > k).astype(np.float32).astype(ml_dtypes.bfloat16)
    c["triC"] = (j <= k).astype(np.float32).astype(ml_dtypes.bfloat16)
    q = np.arange(512)[None, :]
    m = np.stack([((np.arange(128)[:, None] + 128 * o) < q) for o in range(4)], axis=1)
    c["masks"] = m.astype(np.float32).astype(ml_dtypes.bfloat16)
    m8 = np.zeros((128, 64), np.float32)
    for hh in range(8):
        for qq in range(8):
            m8[:8, hh * 8 + qq] = (np.arange(8) < qq)
    c["mask8"] = m8
    g = gammas()
    s = np.arange(128, dtype=np.float64)
    sc = 128.0 ** -0.5
    din = np.zeros((128, 4, 128), np.float64)
    for hh in range(4):
        din[:, hh, :] = (g[hh] ** (-(s[:, None] + 1.0))) * (s[None, :] >= s[:, None]) * sc
    c["din"] = din.astype(np.float32)
    din8 = np.zeros((128, 4, 8), np.float64)
    s8 = np.arange(8, dtype=np.float64)
    for hh in range(4):
        din8[:8, hh, :] = (g[hh] ** (-(s8[:, None] + 1.0))) * (s8[None, :] >= s8[:, None]) * sc
    c["din8"] = din8.astype(np.float32)
    rt = np.zeros((128, 12), np.float64)
    rt8 = np.zeros((128, 12), np.float64)
    for hh in range(4):
        rt[:, hh] = g[hh] ** (127.0 - s) * sc
        rt[:, 4 + hh] = g[hh] ** (s + 1.0)
        rt[:, 8 + hh] = g[hh] ** (2.0 * (s + 1.0)) / 256.0
        rt8[:8, hh] = g[hh] ** (7.0 - s8) * sc
        rt8[:8, 4 + hh] = g[hh] ** (s8 + 1.0)
        rt8[:8, 8 + hh] = g[hh] ** (2.0 * (s8 + 1.0)) / 256.0
    c["rtab"] = rt.astype(np.float32)
    c["rtab8"] = rt8.astype(np.float32)
    c["pidx"] = np.arange(128, dtype=np.float32)[:, None]
    return c


def rope_tables(pos):
    half = 64
    freq = (10000.0 ** (-np.arange(half, dtype=np.float32) / half)).astype(np.float32)
    ang = pos.astype(np.float32)[None, :] * freq[:, None]
    cos = np.cos(ang).astype(np.float32)
    sin = np.sin(ang).astype(np.float32)
    return (np.concatenate([cos, cos], 0).astype(np.float32),
            np.concatenate([-sin, sin], 0).astype(np.float32))


_DRAM_IN = [
    ("x_own", [HALF, D], F32), ("x_pre", [HALF, D], F32), ("x_smp", [SEQ_PER_CORE * DEC, D], F32),
    ("cache_kv", [NPOOL * PAGE, 1024], F32),
    ("state_in", [SEQ_PER_CORE, RET_H, 128, 256], F32), ("ptab", [SEQ_PER_CORE, NPAGES], I32),
    ("flag", [128, 1], F32),
    ("g_ffn1_pre", [1, D], F32), ("w_ffn1_gu", [1, D, 2 * DFF], F32), ("w_ffn1_down", [1, DFF, D], F32),
    ("g_ffn1_post", [1, D], F32), ("g_mix_pre", [1, D], F32), ("w_in", [1, D, 6656], F32),
    ("sb_bias", [1, SB_H], F32), ("ret_gn_g", [1, D], F32), ("w_sb_out", [1, 512, D], F32),
    ("w_ret_out", [1, D, D], F32), ("w_o", [1, D, D], F32), ("g_mix_post", [1, D], F32),
    ("g_ffn2_pre", [1, D], F32), ("w_ffn2_gu", [1, D, 2 * DFF], F32), ("w_ffn2_down", [1, DFF, D], F32),
    ("g_ffn2_post", [1, D], F32),
    ("ident_bf", [128, 128], BF16), ("ident_f", [128, 128], F32), ("triS", [128, 128], BF16),
    ("triC", [128, 128], BF16), ("masks", [128, 4, 512], BF16), ("mask8", [128, 64], F32),
    ("din", [128, 4, 128], F32), ("din8", [128, 4, 8], F32), ("rtab", [128, 12], F32), ("rtab8", [128, 12], F32),
    ("pidx", [128, 1], F32),
    ("rope_cos_own", [128, HALF], F32), ("rope_sin_own", [128, HALF], F32),
    ("rope_cos_pre", [128, HALF], F32), ("rope_sin_pre", [128, HALF], F32),
    ("rope_cos_smp", [128, 32], F32), ("rope_sin_smp", [128, 32], F32),
]
_DRAM_OUT = [
    ("y_own", [HALF, D], F32), ("y_smp", [32, D], F32), ("k_rows", [HALF, 512], F32), ("v_rows", [HALF, 512], F32),
    ("ret_state", [RET_H, 128, 256], F32), ("ks_rows", [32, 512], F32), ("vs_rows", [32, 512], F32),
    ("ret_state_s", [SEQ_PER_CORE, RET_H, 128, 256], F32),
]


def build(stage=99):
    nc = bass.Bass("TRN2", target_bir_lowering=False)
    dr = {}
    for name, shape, dt in _DRAM_IN:
        if name == "cache_kv" and stage < 9:
            continue
        dr[name] = nc.dram_tensor(name, shape, dt, kind="ExternalInput").ap()
    for name, shape, dt in _DRAM_OUT:
        dr[name] = nc.dram_tensor(name, shape, dt, kind="ExternalOutput").ap()
    T = Tracker(nc)
    blocks = weight_blocks(dr)
    tot = sum(128 * kc * bw for _, kc, bw, _ in blocks)
    scr = nc.dram_tensor("wscr", [tot], BF16, kind="Internal").ap()
    scr_views = {}
    off = 0
    for key, kc, bw, _ in blocks:
        n = 128 * kc * bw
        scr_views[key] = (T.dram_v(scr[off:off + n].rearrange("(p m) -> p m", p=128), f"scr{key}"), kc, bw)
        off += n

    PSP = [nc.alloc_psum_tensor(f"psp{i}", [128, 1024], F32) for i in range(4)]
    PS = [V(PSP[i // 2][:, (i % 2) * 512:(i % 2 + 1) * 512], Buf(f"ps{i}", "psum", i, i + 1)) for i in range(8)]
    PS2 = [V(PSP[i][:], Buf(f"psp{i}", "psum", 2 * i, 2 * i + 2)) for i in range(4)]
    T.ps_bufs = [v.buf for v in PS] + [v.buf for v in PS2]
    dv = lambda name: T.dram_v(dr[name], name)

    base0 = T.cursor
    stg = [T.alloc(f"stg{i}", [128, SLOT_E], F32) for i in range(3)]
    wb = [T.alloc(f"wb{i}", [128, SLOT_E], BF16) for i in range(3)]
    cast_engs = ["act", "dve", "pool"]
    def p0_load(bi):
        key, kc, bw, pieces = blocks[bi]
        sv = stg[bi % 3](slice(0, kc * bw))
        s3 = V(sv.ap.rearrange("p (k n) -> p k n", k=kc), sv.buf)
        for src, c0, wd in pieces:
            T.dma(s3[:, :, c0:c0 + wd], T.dram_v(src, "w"))

    p0_load(0)
    p0_load(1)
    for bi, (key, kc, bw, pieces) in enumerate(blocks):
        if bi + 2 < len(blocks):
            p0_load(bi + 2)
        sv = stg[bi % 3](slice(0, kc * bw))
        wv = wb[bi % 3](slice(0, kc * bw))
        T.copy(cast_engs[bi % 3], wv, sv)
        T.dma(scr_views[key][0], wv)
    T.barrier()

    T.cursor = base0
    A = T.alloc
    X = A("X", [128, 4, D], F32)
    xnb = [A(f"xnb{i}", [128, D], BF16) for i in range(2)]
    xT = A("xT", [128, 8, TN], BF16)
    uT = A("uT", [128, 8, TN], BF16)
    QsT = A("QsT", [128, 4, TN], BF16, at=xT.off)
    osT = A("osT", [128, 4, TN], BF16, at=xT.off + 4 * TN * 2)
    hT = A("hT", [128, NCH, TN], BF16)
    T.cursor += 24 * 1024 - hT.nbytes
    orT = A("orT", [128, 8, TN], BF16, at=hT.off)
    mT = A("mT", [128, 8, TN], BF16, at=hT.off + 8192)
    Vr = A("Vr", [128, 4, D], BF16, at=hT.off + 16384)
    ring = [A(f"wr{i}", [128, SLOT_E], BF16) for i in range(NSLOT)]
    KsT = A("KsT", [128, 4, SEQ], BF16)
    Vsb = A("Vsb", [128, 32, 512], BF16)
    S = A("S", [128, 4, 256], F32)
    Sbf = A("Sbf", [128, 4, 256], BF16)
    Kdec = [A(f"Kdec{i}", [128, 4, 128], BF16) for i in range(2)]
    ropec = A("ropec", [128, TN], F32)
    ropes = A("ropes", [128, TN], F32)
    gcur = A("gcur", [128, D], F32)
    gng = A("gng", [128, D], F32)
    gpT = A("gpT", [128, 3, 8], F32)
    ident_bf = A("ident_bf", [128, 128], BF16)
    ident_f = A("ident_f", [128, 128], F32)
    triS = A("triS", [128, 128], BF16)
    triC = A("triC", [128, 128], BF16)
    triSP = A("triSP", [128, 128], BF16)
    triCP = A("triCP", [128, 128], BF16)
    masks = A("masks", [128, 4, 512], BF16)
    mask8 = A("mask8", [128, 64], F32)
    din = A("din", [128, 4, 128], F32)
    din8 = A("din8", [128, 4, 8], F32)
    rtab = A("rtab", [128, 12], F32)
    rtab8 = A("rtab8", [128, 12], F32)
    pidx = A("pidx", [128, 1], F32)
    flag = A("flag", [128, 1], F32)
    nb = A("nb", [128, 8], F32)
    nb64 = A("nb64", [128, 8, 8], F32)
    sbb = A("sbb", [128, 8], F32)
    small = A("small", [128, 64], F32)
    ptt = A("ptt", [128, SEQ_PER_CORE * NPAGES], I32)
    pgi = A("pgi", [128, SEQ_PER_CORE * NPAGES], I32)
    KsTs = A("KsTs", [128, 4, 32], BF16)
    TEMP0 = T.cursor
    TEMP_SZ = T.top - TEMP0
    assert TEMP_SZ >= 26 * 1024, TEMP_SZ

    _tt = {}

    def TT(name, off, shape, dt):
        key = (name, off, tuple(shape), dt == BF16)
        t = _tt.get(key)
        if t is None:
            isz = 2 if dt == BF16 else 4
            nbts = int(np.prod(shape[1:])) * isz
            assert TEMP0 + off + nbts <= T.top, ("TEMP overflow", name, TEMP0 + off + nbts - T.top)
            t = Tile(T, f"tmp_{name}_{off}", shape, dt, TEMP0 + off)
            _tt[key] = t
        return t
    K1 = 1024

    for tl, name in ((ident_bf, "ident_bf"), (ident_f, "ident_f"), (triS, "triS"), (triC, "triC"), (masks, "masks"),
                     (mask8, "mask8"), (din, "din"), (din8, "din8"), (rtab, "rtab"), (rtab8, "rtab8"),
                     (pidx, "pidx"), (flag, "flag")):
        T.dma(tl(), dv(name))
    T.dma(gng(), V(dr["ret_gn_g"][0:1, :].partition_broadcast(128), Buf("g", "dram")))
    for i, name in enumerate(("g_ffn1_pre", "g_mix_pre", "g_ffn2_pre")):
        T.dma(gpT(i), V(dr[name].rearrange("o (c p) -> p (o c)", p=128), Buf("g", "dram")), nonctg=True)
    T.dma(sbb(), V(dr["sb_bias"][0:1, :].partition_broadcast(128), Buf("g", "dram")))
    T.ts("dve", nb(), sbb(), -1.0, None, ALU.mult)
    T.copy("dve", nb64(), V(nb().ap.unsqueeze(2).to_broadcast([128, 8, 8]), nb().buf))
    T.ts("dve", triSP(), triS(), flag(), None, ALU.mult)
    T.ts("dve", triCP(), triC(), flag(), None, ALU.mult)
    T.dma(ptt(), V(dr["ptab"].rearrange("s j -> (s j)").rearrange("(o n) -> o n", o=1).partition_broadcast(128),
                   Buf("g", "dram")))
    T.ts("dve", pgi(), ptt(), 128.0, pidx(), ALU.mult, ALU.add)
    T.memset(S(), 0.0)
    T.memset(Sbf(), 0.0)

    W = WStream(T, ring, scr_views, None)
    small_i = [0]

    def sm(n=1):
        i = small_i[0]
        if i + n > 64:
            i = 0
        small_i[0] = i + n
        return small(slice(i, i + n))

    def rstd_of(ssv, rows, scale, n=1):
        rt = sm(n)
        T.act(rt[:rows], ssv[:rows], AF.Sqrt, bias=EPS, scale=scale)
        rs = sm(n)
        T.recip(rs[:rows], rt[:rows])
        return rs

    def norm_transpose(blocks_, gidx, outT, psbank, joff):
        junk = TT("junkN", joff * K1, [128, D], BF16)
        for bi, (tb, col0, rows) in enumerate(blocks_):
            ss = sm()
            T.act(junk()[:rows], X(tb)[:rows], AF.Square, accum=ss[:rows])
            rs = rstd_of(ss, rows, 1.0 / D)
            xb = xnb[bi % 2]
            T.ts("dve", xb()[:rows], X(tb)[:rows], rs[:rows], None, ALU.mult)
            ps = PS[psbank + bi % 2]
            psb = V(ps.ap.bitcast(BF16).rearrange("p (c t) -> p c t", c=8), ps.buf)
            for c in range(8):
                T.tr(psb[:, c, 0:rows], xb(slice(c * 128, (c + 1) * 128))[:rows], ident_bf()[:rows, :rows],
                     last=(c == 7))
            ov = outT(slice(0, 8), slice(col0, col0 + rows))
            T.tt("dve", ov, psb[:, :, 0:rows],
                 V(gpT(gidx).ap.unsqueeze(2).to_broadcast([128, 8, rows]), gpT(gidx).buf), ALU.mult)

    def postnorm_residual(psA, psB, tb, rows, half_scale, base, slot):
        junk = TT("pjunk", base * K1, [128, 512], BF16)
        ss = sm(2)
        T.act(junk()[:rows], psA[:rows], AF.Square, accum=ss[:rows, 0:1])
        T.act(junk()[:rows], psB[:rows], AF.Square, accum=ss[:rows, 1:2])
        st = sm()
        T.tt("dve", st[:rows], ss[:rows, 0:1], ss[:rows, 1:2], ALU.add)
        rs = rstd_of(st, rows, 1.0 / D)
        t = TT("pt", (base + 2 + 4 * slot) * K1, [128, D], F32)
        T.stt(t(slice(0, 512))[:rows], psA[:rows], rs[:rows], gcur(slice(0, 512))[:rows], ALU.mult, ALU.mult)
        T.stt(t(slice(512, 1024))[:rows], psB[:rows], rs[:rows], gcur(slice(512, 1024))[:rows], ALU.mult, ALU.mult)
        if half_scale:
            T.stt(X(tb)[:rows], t()[:rows], 0.5, X(tb)[:rows], ALU.mult, ALU.add)
        else:
            T.tt("pool", X(tb)[:rows], t()[:rows], X(tb)[:rows], ALU.add)

    def ffn(f, N, tokblocks, gidx, gname):
        T.dma(gcur(), V(dr[gname][0:1, :].partition_broadcast(128), Buf("g", "dram")))
        norm_transpose(tokblocks, gidx, xT, 6, 0)
        sg = [TT("sg", (2 + 2 * i) * K1, [128, TN], F32) for i in range(2)]
        for c in range(NCH):
            w = W.get((f"f{f}gu", c))
            b0 = (c % 3) * 2
            psg, psu = PS[b0], PS[b0 + 1]
            for kc in range(8):
                T.mm(psg[:, :N], w[:, kc, 0:128], xT(kc, slice(0, N)), kc == 0, kc == 7, kc == 7)
            for kc in range(8):
                T.mm(psu[:, :N], w[:, kc, 128:256], xT(kc, slice(0, N)), kc == 0, kc == 7, kc == 7)
            s = sg[c % 2]
            T.act(s(slice(0, N)), psg[:, :N], AF.Silu)
            T.tt("dve", hT(c, slice(0, N)), s(slice(0, N)), psu[:, :N], ALU.mult)
        passes = [tokblocks[i:i + 2] for i in range(0, len(tokblocks), 2)]
        for pi, pb in enumerate(passes):
            bank0 = (pi % 2) * 4
            for g in range(NCH // 2):
                w = W.get((f"f{f}dn", g))
                for kk in range(2):
                    kc = 2 * g + kk
                    for bi, (tb, col0, rows) in enumerate(pb):
                        for j in range(2):
                            T.mm(PS[bank0 + 2 * bi + j][:rows, :], hT(kc, slice(col0, col0 + rows)),
                                 w[:, kk, j * 512:(j + 1) * 512], kc == 0, kc == NCH - 1,
                                 kk == 1 and bi == len(pb) - 1 and j == 1)
            for bi, (tb, col0, rows) in enumerate(pb):
                postnorm_residual(PS[bank0 + 2 * bi], PS[bank0 + 2 * bi + 1], tb, rows, True, 6, bi)

    def proj_fm(wname, nblk, N, src, evac, bank0=0, sw=None):
        for j in range(nblk):
            w = W.get((wname, j))
            w2 = W.get((sw, j), hold=1) if sw else None
            for cc in range(2):
                c = 2 * j + cc
                ps = PS[bank0 + (c % 2) * 2]
                for kc in range(8):
                    T.mm(ps[:, :N], w[:, kc, cc * 128:(cc + 1) * 128], src(kc), kc == 0, kc == 7, kc == 7)
                ps2 = None
                if sw:
                    ps2 = PS[bank0 + (c % 2) * 2 + 1]
                    for kc in range(8):
                        T.mm(ps2[:, :N], w2[:, kc, cc * 128:(cc + 1) * 128], src(kc), kc == 0, kc == 7, kc == 7)
                evac(c, ps, ps2)

    def proj_tm(wname, jlist, tokblocks, src_cols, banks, col_of):
        for j in jlist:
            w = W.get((wname, j))
            for bi, (tb, col0, rows) in enumerate(tokblocks):
                ps = banks(bi, j)
                c0 = col_of(j)
                for kc in range(8):
                    T.mm(ps[:rows, c0:c0 + 256], src_cols(kc, col0, rows), w[:, kc, :], kc == 0, kc == 7, kc == 7)

    def rope_evac(dst):
        def f(c, ps, ps2, N):
            t1 = TT("rope1", 18 * K1, [128, TN], F32)
            t2 = TT("rope2", 20 * K1, [128, TN], F32)
            T.tt("dve", t1(slice(0, N)), ps[:, :N], ropec(slice(0, N)), ALU.mult)
            T.tt("dve", t2(slice(0, N)), ps2[:, :N], ropes(slice(0, N)), ALU.mult)
            T.tt("pool", dst(c, slice(0, N)), t1(slice(0, N)), t2(slice(0, N)), ALU.add)
        return f

    def retention_block(rows, qv, kv, vr, din_t, rt_t, dc, do_out, sg_v, or_dst, psbase):
        C = rows
        if do_out:
            psI = PS[psbase]
            for h in range(4):
                T.mm(psI[:C, h * C:(h + 1) * C], kv(h), qv(h), True, True, h == 3)
            inT = TT("inT", 10 * K1, [128, 4, 128], BF16)
            T.tt("dve", inT(slice(0, 4), slice(0, C))[:C],
                 V(psI.ap[:C, 0:4 * C].rearrange("p (h c) -> p h c", h=4), psI.buf),
                 din_t(slice(0, 4), slice(0, C))[:C], ALU.mult)
            pso = [PS[psbase + 1], PS[psbase + 2]]
            for h in range(4):
                o = pso[h // 2][:C, (h % 2) * 256:(h % 2 + 1) * 256]
                T.mm(o, inT(h, slice(0, C))[:C], vr[:C, h * 256:(h + 1) * 256], True, False, False)
                T.mm(o, qv(h), Sbf(h), False, True, h % 2 == 1)
        kd = Kdec[0]
        psk = PS[psbase + 3]
        pskb = V(psk.ap.bitcast(BF16).rearrange("p (c t) -> p c t", c=8), psk.buf)
        for h in range(4):
            T.tr(pskb[:C, h, :], kv(h), ident_bf(), last=(h == 3))
        for h in range(4):
            T.ts("dve", kd(h)[:C], pskb[:C, h, :], rt_t(slice(h, h + 1))[:C], None, ALU.mult)
        pss = [PS[psbase + 4], PS[psbase + 5]]
        for h in range(4):
            T.mm(pss[h // 2][:, (h % 2) * 256:(h % 2 + 1) * 256], kd(h)[:C], vr[:C, h * 256:(h + 1) * 256],
                 True, True, h % 2 == 1)
        if do_out:
            junk = TT("rjunk", 11 * K1, [128, 256], BF16)
            ss = sm(4)
            for h in range(4):
                T.act(junk()[:C], pso[h // 2][:C, (h % 2) * 256:(h % 2 + 1) * 256], AF.Square,
                      accum=ss[:C, h:h + 1])
            s2 = sm(4)
            T.tt("dve", s2[:C], ss[:C], rt_t(slice(8, 12))[:C], ALU.mult)
            rs = rstd_of(s2, C, 1.0, n=4)
            fac = sm(4)
            T.tt("dve", fac[:C], rs[:C], rt_t(slice(4, 8))[:C], ALU.mult)
            tn = TT("tn", 12 * K1, [128, D], F32)
            for h in range(4):
                T.stt(tn(slice(h * 256, (h + 1) * 256))[:C], pso[h // 2][:C, (h % 2) * 256:(h % 2 + 1) * 256],
                      fac[:C, h:h + 1], gng(slice(h * 256, (h + 1) * 256))[:C], ALU.mult, ALU.mult)
            orb = TT("orb", 16 * K1, [128, D], BF16)
            T.tt("pool", orb()[:C], tn()[:C], sg_v[:C], ALU.mult)
            pst = PS[psbase]
            pstb = V(pst.ap.bitcast(BF16).rearrange("p (c t) -> p c t", c=8), pst.buf)
            for c in range(8):
                T.tr(pstb[:, c, 0:C], orb(slice(c * 128, (c + 1) * 128))[:C], ident_bf()[:C, :C], last=(c == 7))
            or_dst(pstb[:, :, 0:C])
        for h in range(4):
            T.stt(S(h), S(h), float(dc[h]), pss[h // 2][:, (h % 2) * 256:(h % 2 + 1) * 256], ALU.mult, ALU.add)
        T.copy("pool", Sbf(), S())

    g = gammas()
    dc128 = g ** 128.0
    dc8 = g ** 8.0

    aL = [Tile(T, f"aL{i}", [128, 2, TN], BF16, mT.off + i * 2048) for i in range(4)]
    aA = [Tile(T, f"aA{i}", [128, 2, TN], BF16, Vr.off + i * 2048) for i in range(2)]

    def sb_attention_prompt(t_own, core_blocks_before):
        E = [TT("aE", (4 * i) * K1, [128, 2, TN], F32) for i in range(3)]
        Xt = [TT("aX", (12 + 4 * i) * K1, [128, 2, TN], F32) for i in range(4)]
        L = aL
        Aa = aA
        qblk0 = core_blocks_before + t_own * 4
        kb_last = qblk0 + 3
        units = [(pair, kb) for pair in range(4) for kb in range(kb_last, -1, -1)]
        n = len(units)

        def bview(v2, shape):
            return V(v2.ap.rearrange("p (j q) -> p j q", j=2), v2.buf)

        def stA(u):
            pair, kb = units[u]
            zb = 2 + u % 2
            for j in range(2):
                r0 = j * 64
                T.mm(PS[2 * zb + j], KsT(pair, slice(kb * 128, (kb + 1) * 128))[r0:r0 + 64],
                     QsT(pair)[r0:r0 + 64], True, True, True)
            e = E[u % 3]
            x = Xt[u % 4]
            nbp = nb(slice(2 * pair, 2 * pair + 2))
            T.stt(x(), bview(PS2[zb], None), -0.125,
                  V(nbp.ap.unsqueeze(2).to_broadcast([128, 2, TN]), nbp.buf), ALU.mult, ALU.add)
            T.act(e(), x(), AF.Exp)
            T.act(e(), e(), AF.Ln, bias=1.0)
            T.tt("pool", L[u % 4](0), x(0), e(0), ALU.subtract)
            T.tt("dve", L[u % 4](1), x(1), e(1), ALU.subtract)
            if kb >= qblk0:
                mk = masks(kb - qblk0)
                T.tt("pool", L[u % 4](), L[u % 4](), V(mk.ap.unsqueeze(1).to_broadcast([128, 2, TN]), mk.buf),
                     ALU.mult)

        def stB(u):
            pair, kb = units[u]
            pre = kb < 16
            for j in range(2):
                T.mm(PS[2 + j], (triSP if pre else triS)(), L[u % 4](j), kb == kb_last, True, True)
            T.tt("dve", Xt[u % 4](), bview(PS2[1], None), E[u % 3](), ALU.subtract)

        def stC(u):
            pair, kb = units[u]
            pre = kb < 16
            if kb > 0:
                for j in range(2):
                    T.mm(PS[2 + j], (triCP if pre else triC)(), L[u % 4](j), False, True, True)
            a = Aa[u % 2]
            T.act(a(), Xt[u % 4](), AF.Exp)
            if kb >= qblk0:
                mk = masks(kb - qblk0)
                T.tt("pool", a(), a(), V(mk.ap.unsqueeze(1).to_broadcast([128, 2, TN]), mk.buf), ALU.mult)

        def stD(u):
            pair, kb = units[u]
            pso = PS[pair % 2]
            for j in range(2):
                h = 2 * pair + j
                T.mm(pso[j * 64:(j + 1) * 64, :], Vsb(kb, slice(h * 64, (h + 1) * 64)), Aa[u % 2](j),
                     kb == kb_last, kb == 0, True)
            if kb == 0:
                T.copy("act", osT(pair), pso)

        for step in range(n + 4):
            if step < n:
                stA(step)
            if 0 <= step - 3 < n:
                stC(step - 3)
            if 0 <= step - 2 < n:
                stB(step - 2)
            if 0 <= step - 4 < n:
                stD(step - 4)

    def load_rope(cos_name, sin_name, c0, N):
        T.dma(ropec(slice(0, N)), V(dr[cos_name][:, c0:c0 + N], Buf("g", "dram")))
        T.dma(ropes(slice(0, N)), V(dr[sin_name][:, c0:c0 + N], Buf("g", "dram")))

    FULL = [(tb, tb * 128, 128) for tb in range(4)]

    def prefix_tile(t):
        T.dma(V(X.t[:], X().buf), V(dr["x_pre"][t * TN:(t + 1) * TN, :].rearrange("(b p) d -> p b d", p=128),
                                    Buf("g", "dram")))
        load_rope("rope_cos_pre", "rope_sin_pre", t * TN, TN)
        ffn(1, TN, FULL, 0, "g_ffn1_post")
        norm_transpose(FULL, 1, uT, 6, 8)
        src = lambda kc: uT(kc)
        proj_fm("k_sb", 2, TN, src,
                lambda c, ps, ps2: T.copy("act", KsT(c, slice(t * TN, (t + 1) * TN)), ps))
        proj_tm("v_sb", [0, 1], FULL, lambda kc, col0, rows: uT(kc, slice(col0, col0 + rows)),
                lambda bi, j: PS[4 + bi], lambda j: j * 256)
        for bi in range(4):
            T.ts("dve", Vsb(t * 4 + bi), PS[4 + bi], flag(), None, ALU.mult)
        re = rope_evac(lambda c, sl: KrT(c, sl))
        proj_fm("k_r", 2, TN, src, lambda c, ps, ps2: re(c, ps, ps2, TN), sw="k_r_sw")
        for hf in range(2):
            proj_tm("v_r", [2 * hf, 2 * hf + 1], FULL, lambda kc, col0, rows: uT(kc, slice(col0, col0 + rows)),
                    lambda bi, j: PS[4 * (hf % 2) + bi], lambda j: (j % 2) * 256)
            for bi in range(4):
                T.ts("dve", Vr(bi, slice(hf * 512, (hf + 1) * 512)), PS[4 * (hf % 2) + bi], flag(), None, ALU.mult)
        for tb in range(4):
            retention_block(128, None, lambda h: KrT(h, slice(tb * 128, (tb + 1) * 128)), Vr(tb),
                            din, rtab, dc128, False, None, None, 0)

    def own_tile(t, core_blocks_before, N=TN, tokblocks=FULL, sample=False):
        if not sample:
            T.dma(V(X.t[:], X().buf), V(dr["x_own"][t * TN:(t + 1) * TN, :].rearrange("(b p) d -> p b d", p=128),
                                        Buf("g", "dram")))
            load_rope("rope_cos_own", "rope_sin_own", t * TN, TN)
        else:
            T.dma(X(0)[:32], dv("x_smp"))
            load_rope("rope_cos_smp", "rope_sin_smp", 0, 32)
        ffn(1, N, tokblocks, 0, "g_ffn1_post")
        norm_transpose(tokblocks, 1, uT, 6, 8)
        src = lambda kc: uT(kc, slice(0, N))
        srcc = lambda kc, col0, rows: uT(kc, slice(col0, col0 + rows))
        kpos = (core_blocks_before * 128 + t * TN) if not sample else 0
        proj_fm("q_sb", 2, N, src, lambda c, ps, ps2: T.copy("act", QsT(c, slice(0, N)), ps[:, :N]))
        if not sample:
            kdst = lambda c: KsT(c, slice(kpos, kpos + N))
        else:
            kdst = lambda c: KsTs(c, slice(0, N))
        for j in range(2):
            w = W.get(("k_sb", j))
            for cc in range(2):
                c = 2 * j + cc
                ps = PS[(c % 2)]
                for kc in range(8):
                    T.mm(ps[:, :N], w[:, kc, cc * 128:(cc + 1) * 128], src(kc), kc == 0, kc == 7, kc == 7)
                T.copy("act", kdst(c), ps[:, :N])
            for bi, (tb, col0, rows) in enumerate(tokblocks):
                ps = PS[4 + bi]
                for kc in range(8):
                    T.mm(ps[:rows, j * 256:(j + 1) * 256], srcc(kc, col0, rows), w[:, kc, :], kc == 0, kc == 7, kc == 7)
        for bi, (tb, col0, rows) in enumerate(tokblocks):
            ko = TT("ko", (10 + 2 * (bi % 2)) * K1, [128, 512], F32)
            T.copy("dve", ko()[:rows], PS[4 + bi][:rows])
            if not sample:
                T.dma(T.dram_v(dr["k_rows"][t * TN + col0:t * TN + col0 + rows, :], "o"), ko()[:rows], is_output=True)
            else:
                T.dma(T.dram_v(dr["ks_rows"][:, :], "o"), ko()[:rows], is_output=True)
        vblocks = tokblocks if not sample else [(0, s * 8, 8) for s in range(4)]
        proj_tm("v_sb", [0, 1], vblocks, srcc, lambda bi, j: PS[bi], lambda j: j * 256)
        for bi, (tb, col0, rows) in enumerate(vblocks):
            if not sample:
                vo = TT("vo", (14 + 2 * (bi % 2)) * K1, [128, 512], F32)
                T.copy("dve", vo()[:rows], PS[bi][:rows])
                T.dma(T.dram_v(dr["v_rows"][t * TN + col0:t * TN + col0 + rows, :], "o"), vo()[:rows], is_output=True)
                T.copy("act", Vsb((kpos // 128) + bi), vo())
            else:
                T.copy("dve", Vnew[bi]()[:8], PS[bi][:8])
                T.dma(T.dram_v(dr["vs_rows"][bi * 8:(bi + 1) * 8, :], "o"), Vnew[bi]()[:8], is_output=True)
                T.copy("act", Vnb[bi]()[:8], Vnew[bi]()[:8])
        re_q = rope_evac(lambda c, sl: QrT(c, sl))
        proj_fm("q_r", 2, N, src, lambda c, ps, ps2: re_q(c, ps, ps2, N), bank0=4, sw="q_r_sw")
        re_k = rope_evac(lambda c, sl: KrT(c, sl))
        proj_fm("k_r", 2, N, src, lambda c, ps, ps2: re_k(c, ps, ps2, N), bank0=0, sw="k_r_sw")
        for hf in range(2):
            proj_tm("v_r", [2 * hf, 2 * hf + 1], vblocks, srcc,
                    lambda bi, j: PS[4 * (hf % 2) + bi], lambda j: (j % 2) * 256)
            for bi, (tb, col0, rows) in enumerate(vblocks):
                T.copy("act", Vr(bi, slice(hf * 512, (hf + 1) * 512))[:rows], PS[4 * (hf % 2) + bi][:rows])
        wg = [W.get(("g_r", j), hold=j) for j in range(4)]
        for bi, (tb, col0, rows) in enumerate(vblocks):
            psg = [PS[6], PS[7]]
            for j in range(4):
                for kc in range(8):
                    T.mm(psg[j // 2][:rows, (j % 2) * 256:(j % 2 + 1) * 256], srcc(kc, col0, rows), wg[j][:, kc, :],
                         kc == 0, kc == 7, kc == 7)
            sgr = TT("sgr", 18 * K1, [128, D], F32)
            T.act(sgr(slice(0, 512))[:rows], psg[0][:rows], AF.Silu)
            T.act(sgr(slice(512, 1024))[:rows], psg[1][:rows], AF.Silu)
            if sample:
                T.dma(V(S.t[:], S().buf), V(dr["state_in"][bi].rearrange("h d v -> d h v"), Buf("g", "dram")))
                T.copy("pool", Sbf(), S())
            ordst = lambda src_v, col0=col0, rows=rows: T.copy(
                "dve", orT(slice(0, 8), slice(col0, col0 + rows)), src_v)
            retention_block(rows, lambda h: QrT(h, slice(col0, col0 + rows)),
                            lambda h: KrT(h, slice(col0, col0 + rows)), Vr(bi),
                            din8 if sample else din, rtab8 if sample else rtab, dc8 if sample else dc128,
                            True, sgr(), ordst, 0)
            if sample:
                T.dma(T.dram_v(dr["ret_state_s"][bi].rearrange("h d v -> d h v"), "o"), V(S.t[:], S().buf),
                      is_output=True)
        if not sample and t == NT - 1:
            T.dma(T.dram_v(dr["ret_state"].rearrange("h d v -> d h v"), "o"), V(S.t[:], S().buf), is_output=True)
        if not sample:
            sb_attention_prompt(t, core_blocks_before)
        else:
            sb_attention_sample()
        st_ = [[TT("mg", ((a * 4 + b) * 2) * K1, [128, TN], F32) for b in range(4)] for a in range(2)]
        for p in range(4):
            wa = W.get(("a_sb", p))
            wr_ = W.get(("a_r", p), hold=1)
            wro = W.get(("reto", p), hold=2)
            wso = W.get(("sbo", p // 2), hold=3)
            for cc in range(2):
                fc = 2 * p + cc
                b0 = (fc % 2) * 4
                for kc in range(8):
                    T.mm(PS[b0][:, :N], wa[:, kc, cc * 128:(cc + 1) * 128], src(kc), kc == 0, kc == 7, kc == 7)
                for kc in range(8):
                    T.mm(PS[b0 + 1][:, :N], wr_[:, kc, cc * 128:(cc + 1) * 128], src(kc), kc == 0, kc == 7, kc == 7)
                for kc in range(4):
                    T.mm(PS[b0 + 2][:, :N], wso[:, kc, (fc % 4) * 128:(fc % 4 + 1) * 128], osT(kc, slice(0, N)),
                         kc == 0, kc == 3, kc == 3)
                for kc in range(8):
                    T.mm(PS[b0 + 3][:, :N], wro[:, kc, cc * 128:(cc + 1) * 128], orT(kc, slice(0, N)),
                         kc == 0, kc == 7, kc == 7)
                s1, t1, s2, t2 = [x(slice(0, N)) for x in st_[fc % 2]]
                T.act(s1, PS[b0][:, :N], AF.Sigmoid)
                T.tt("dve", t1, s1, PS[b0 + 2][:, :N], ALU.mult)
                T.act(s2, PS[b0 + 1][:, :N], AF.Sigmoid)
                T.tt("dve", t2, s2, PS[b0 + 3][:, :N], ALU.mult)
                T.tt("pool", mT(fc, slice(0, N)), t1, t2, ALU.add)
        T.dma(gcur(), V(dr["g_mix_post"][0:1, :].partition_broadcast(128), Buf("g", "dram")))
        passes = [tokblocks[i:i + 2] for i in range(0, len(tokblocks), 2)]
        for pi, pb in enumerate(passes):
            bank0 = (pi % 2) * 4
            for j in range(4):
                w = W.get(("wo", j))
                for bi, (tb, col0, rows) in enumerate(pb):
                    for kc in range(8):
                        T.mm(PS[bank0 + 2 * bi + j // 2][:rows, (j % 2) * 256:(j % 2 + 1) * 256],
                             mT(kc, slice(col0, col0 + rows)), w[:, kc, :], kc == 0, kc == 7, kc == 7)
            for bi, (tb, col0, rows) in enumerate(pb):
                postnorm_residual(PS[bank0 + 2 * bi], PS[bank0 + 2 * bi + 1], tb, rows, False, 16, bi)
        ffn(2, N, tokblocks, 2, "g_ffn2_post")
        if not sample:
            T.dma(T.dram_v(dr["y_own"][t * TN:(t + 1) * TN, :].rearrange("(b p) d -> p b d", p=128), "o"),
                  V(X.t[:], X().buf), is_output=True)
        else:
            T.dma(T.dram_v(dr["y_smp"][:, :], "o"), X(0)[:32], is_output=True)

    QrT = TT("QrT", 0, [128, 4, TN], BF16)
    KrT = TT("KrT", 4 * K1, [128, 4, TN], BF16)
    SB0 = KsT.off
    Vnew = [Tile(T, f"Vnew{i}", [128, 512], F32, SB0 + i * 2048) for i in range(4)]
    KVpg = [Tile(T, f"KVpg{i}", [128, 1024], F32, SB0 + 8192 + i * 4096) for i in range(6)]
    KTp = [Tile(T, f"KTp{i}", [128, 4, 128], BF16, SB0 + 32768 + i * 1024) for i in range(3)]
    Kpb = [Tile(T, f"Kpb{i}", [128, 512], BF16, SB0 + 44032 + i * 1024) for i in range(4)]
    Vpb = [Tile(T, f"Vpb{i}", [128, 512], BF16, SB0 + 48128 + i * 1024) for i in range(8)]
    Vnb = [Tile(T, f"Vnb{i}", [128, 512], BF16, SB0 + 56320 + i * 1024) for i in range(4)]
    SR = 6
    sE = [Tile(T, f"sE{i}", [128, 64], F32, SB0 + 36864 + i * 1024) for i in range(SR)]
    sX = [Tile(T, f"sX{i}", [128, 64], F32, SB0 + 36864 + i * 1024 + 256) for i in range(SR)]
    sA = [Tile(T, f"sA{i}", [128, 64], BF16, SB0 + 36864 + i * 1024 + 512) for i in range(SR)]
    sL = [Tile(T, f"sL{i}", [128, 64], BF16, SB0 + 36864 + i * 1024 + 768) for i in range(SR)]

    def sb_attention_sample():
        ckv_ap = dr["cache_kv"]
        nb64v = V(nb64().ap.rearrange("p a b -> p (a b)"), nb64().buf)
        for s_ in range(SEQ_PER_CORE):
            units = ["new"] + list(range(NPAGES - 1, -1, -1))
            n = len(units)
            psO = PS[0]
            psT = PS[6]
            qcols = slice(s_ * 8, s_ * 8 + 8)

            def gath(u):
                if u >= n or units[u] == "new":
                    return
                j = units[u]
                col = s_ * NPAGES + j
                T.gather(KVpg[u % 6](), ckv_ap, pgi(slice(col, col + 1)))

            def cast(u):
                if u >= n or units[u] == "new":
                    return
                T.copy("act", Kpb[u % 4](), KVpg[u % 6](slice(0, 512)))
                T.copy("dve", Vpb[u % 8](), KVpg[u % 6](slice(512, 1024)))

            def S0(u):
                if units[u] == "new":
                    return
                psK = PS[2 + u % 2]
                pkb = V(psK.ap.bitcast(BF16).rearrange("p (c t) -> p c t", c=8), psK.buf)
                for c in range(4):
                    T.tr(pkb[:, c, :], Kpb[u % 4](slice(c * 128, (c + 1) * 128)), ident_bf(), c == 3)

            def S1(u):
                new = units[u] == "new"
                R_ = 8 if new else 128
                psZe = PS[1] if u % 2 == 0 else PS[4]
                psZo = PS[5 + 2 * (u % 2)]
                if not new:
                    psK = PS[2 + u % 2]
                    kt = KTp[u % 3]
                    pkb = V(psK.ap.bitcast(BF16).rearrange("p (c t) -> p c t", c=8), psK.buf)
                    T.copy("dve", kt(), pkb[:, 0:4, :])
                for h in range(8):
                    r0 = (h % 2) * 64
                    if new:
                        lhs = KsTs(h // 2, qcols)[r0:r0 + 64]
                    else:
                        lhs = kt(h // 2)[r0:r0 + 64]
                    pz = psZe if h % 2 == 0 else psZo
                    T.mm(pz[:R_, (h // 2) * 8:(h // 2 + 1) * 8], lhs, QsT(h // 2, qcols)[r0:r0 + 64],
                         True, True, h >= 6)

            def S2(u):
                new = units[u] == "new"
                R_ = 8 if new else 128
                psZe = PS[1] if u % 2 == 0 else PS[4]
                psZo = PS[5 + 2 * (u % 2)]
                i = u % SR
                sx4 = V(sX[i]().ap.rearrange("p (c r q) -> p c r q", c=4, r=2), sX[i]().buf)
                nb4 = V(nb64().ap.rearrange("p (c r) q -> p c r q", c=4), nb64().buf)
                for par, pz in ((0, psZe), (1, psZo)):
                    T.stt(sx4[:R_, :, par, :], V(pz.ap[:R_, 0:32].rearrange("p (c q) -> p c q", c=4), pz.buf),
                          -0.125, nb4[:R_, :, par, :], ALU.mult, ALU.add)
                T.act(sE[i]()[:R_], sX[i]()[:R_], AF.Exp)
                T.act(sE[i]()[:R_], sE[i]()[:R_], AF.Ln, bias=1.0)
                T.tt("dve", sL[i]()[:R_], sX[i]()[:R_], sE[i]()[:R_], ALU.subtract)
                if new:
                    T.tt("dve", sL[i]()[:R_], sL[i]()[:R_], mask8()[:R_], ALU.mult)

            def S3(u):
                new = units[u] == "new"
                R_ = 8 if new else 128
                i = u % SR
                T.mm(psT[:, 0:64], triS()[:R_, :], sL[i]()[:R_], u == 0, True, True)
                T.tt("dve", sX[i]()[:R_], psT[:R_, 0:64], sE[i]()[:R_], ALU.subtract)

            def S4(u):
                new = units[u] == "new"
                R_ = 8 if new else 128
                i = u % SR
                if u < n - 1:
                    T.mm(psT[:, 0:64], triC()[:R_, :], sL[i]()[:R_], False, True, True)
                T.act(sA[i]()[:R_], sX[i]()[:R_], AF.Exp)
                if new:
                    T.tt("dve", sA[i]()[:R_], sA[i]()[:R_], mask8()[:R_], ALU.mult)
                vt = Vnb[s_] if new else Vpb[u % 8]
                for h in range(8):
                    r0 = (h % 2) * 64
                    T.mm(psO[r0:r0 + 64, h * 8:(h + 1) * 8], vt(slice(h * 64, (h + 1) * 64))[:R_],
                         sA[i](slice(h * 8, (h + 1) * 8))[:R_], u == 0 and h < 2, u == n - 1, h == 7)

            gath(1)
            gath(2)
            gath(3)
            for step in range(n + 4):
                gath(step + 4)
                cast(step + 1)
                if step < n:
                    S0(step)
                if 0 <= step - 1 < n:
                    S1(step - 1)
                if 0 <= step - 2 < n:
                    S2(step - 2)
                if 0 <= step - 3 < n:
                    S3(step - 3)
                if 0 <= step - 4 < n:
                    S4(step - 4)
            po = V(psO.ap[:, 0:64].rearrange("p (c r q) -> p c r q", c=4, r=2), psO.buf)
            T.copy("dve", osT(slice(0, 4), qcols)[0:64], po[0:64, :, 0, :])
            T.copy("act", osT(slice(0, 4), qcols)[64:128], po[64:128, :, 1, :])

    def run(Tdry):
        T.dry = Tdry
        W.i = 0
        W.loaded = 0
        for t in range(min(NT, stage)):
            prefix_tile(t)
        for t in range(min(NT, stage - 4)):
            own_tile(t, 16)
        if stage >= 9:
            own_tile(0, 0, N=32, tokblocks=[(0, 0, 32)], sample=True)

    run(True)
    W.plan = list(W.rec)
    small_i[0] = 0
    run(False)
    T.finish()
    return nc, T


_CACHE = {}
STAGE = 9


def kernel(**inputs):
    inputs = {k: np.asarray(v) for k, v in inputs.items()}
    if "nc" not in _CACHE:
        _CACHE["nc"] = build(STAGE)[0]
    nc = _CACHE["nc"]
    consts = host_constants()
    xp = inputs["x_prompt"]
    xs = inputs["x_sample"]
    ckv = None
    if STAGE >= 9:
        ckv = np.concatenate([np.asarray(inputs["cache_k"][0]).reshape(NPOOL * PAGE, 512),
                              np.asarray(inputs["cache_v"][0]).reshape(NPOOL * PAGE, 512)], axis=1)
    pos_first = np.arange(0, HALF)
    cos_s, sin_s = rope_tables(np.tile(8192 + np.arange(DEC), SEQ_PER_CORE))
    in_maps = []
    for c in range(8):
        b, half = c // 2, c % 2
        m = {}
        m["x_own"] = np.ascontiguousarray(xp[b, half * HALF:(half + 1) * HALF])
        m["x_pre"] = np.ascontiguousarray(xp[b, 0:HALF])
        m["x_smp"] = np.ascontiguousarray(xs[4 * c:4 * c + 4].reshape(32, D))
        if STAGE >= 9:
            m["cache_kv"] = ckv
        m["state_in"] = np.ascontiguousarray(inputs["state_ret"][0, 4 * c:4 * c + 4])
        m["ptab"] = np.ascontiguousarray(inputs["page_table"][4 * c:4 * c + 4]).astype(np.int32)
        m["flag"] = np.full((128, 1), float(half), np.float32)
        for k in ("g_ffn1_pre", "w_ffn1_gu", "w_ffn1_down", "g_ffn1_post", "g_mix_pre", "w_in", "sb_bias", "ret_gn_g",
                  "w_sb_out", "w_ret_out", "w_o", "g_mix_post", "g_ffn2_pre", "w_ffn2_gu", "w_ffn2_down",
                  "g_ffn2_post"):
            m[k] = inputs[k]
        m.update(consts)
        co, so = rope_tables(half * HALF + pos_first)
        cp, sp_ = rope_tables(pos_first)
        m["rope_cos_own"], m["rope_sin_own"] = co, so
        m["rope_cos_pre"], m["rope_sin_pre"] = cp, sp_
        m["rope_cos_smp"], m["rope_sin_smp"] = cos_s, sin_s
        in_maps.append(m)
    res = run_bass_kernel_spmd(nc, in_maps, core_ids=list(range(8)))
    r = res.results
    y_prompt = np.stack([np.concatenate([r[2 * b]["y_own"], r[2 * b + 1]["y_own"]], 0) for b in range(4)], 0)
    y_sample = np.concatenate([r[c]["y_smp"].reshape(4, DEC, D) for c in range(8)], 0)
    k_rows = np.stack([np.concatenate([r[2 * b]["k_rows"], r[2 * b + 1]["k_rows"]], 0) for b in range(4)], 0)
    v_rows = np.stack([np.concatenate([r[2 * b]["v_rows"], r[2 * b + 1]["v_rows"]], 0) for b in range(4)], 0)
    k_rows = k_rows.reshape(1, 4, SEQ, SB_H, 64)
    v_rows = v_rows.reshape(1, 4, SEQ, SB_H, 64)
    ret_p = np.stack([r[2 * b + 1]["ret_state"] for b in range(4)], 0)[None]
    ks = np.concatenate([r[c]["ks_rows"].reshape(4, DEC, SB_H, 64) for c in range(8)], 0)[None]
    vs = np.concatenate([r[c]["vs_rows"].reshape(4, DEC, SB_H, 64) for c in range(8)], 0)[None]
    ret_s = np.concatenate([r[c]["ret_state_s"] for c in range(8)], 0)[None]
    f = lambda a: np.ascontiguousarray(a, dtype=np.float32)
    return (f(y_prompt), f(y_sample), f(k_rows), f(v_rows), f(ret_p), f(ks), f(vs), f(ret_s))
```

```python
import numpy as np
import ml_dtypes
import concourse.bass as bass
import concourse.mybir as mybir
from concourse.bass_utils import run_bass_kernel_spmd

F32 = mybir.dt.float32
BF16 = mybir.dt.bfloat16
I32 = mybir.dt.int32
AF = mybir.ActivationFunctionType
ALU = mybir.AluOpType

D = 1024
DFF = 2816
NCH = 22
SEQ = 4096
HALF = 2048
TN = 512
NT = HALF // TN
SB_H = 8
RET_H = 4
PAGE = 128
NPAGES = 64
NPOOL = 2560
SEQ_PER_CORE = 4
DEC = 8
EPS = 1e-6
NSLOT = 6
SLOT_E = 2048

W_IN_COLS = dict(q_sb=0, k_sb=512, v_sb=1024, q_r=1536, k_r=2048, v_r=2560, g_r=3584, a_sb=4608, a_r=5632)


class Sem:
    __slots__ = ("h", "count")

    def __init__(self, h):
        self.h = h
        self.count = 0


class Eng:
    def __init__(self, name, h, sem):
        self.name = name
        self.h = h
        self.sem = sem
        self.waited = {}


class Buf:
    __slots__ = ("name", "lo", "hi", "lastw", "reads", "dsem", "ovl", "ver", "space")

    def __init__(self, name, space, lo=0, hi=0):
        self.name = name
        self.space = space
        self.lo = lo
        self.hi = hi
        self.lastw = None
        self.reads = {}
        self.dsem = None
        self.ovl = None
        self.ver = -1


class V:
    __slots__ = ("ap", "buf")

    def __init__(self, ap, buf):
        self.ap = ap
        self.buf = buf

    def __getitem__(self, idx):
        return V(self.ap[idx], self.buf)

    def re(self, pat, **kw):
        return V(self.ap.rearrange(pat, **kw), self.buf)

    def bc(self, dt):
        return V(self.ap.bitcast(dt), self.buf)


class Tile:
    def __init__(self, T, name, shape, dt, off):
        self.T = T
        self.name = name
        self.shape = list(shape)
        self.isz = 2 if dt == BF16 else 4
        self.off = off
        self.t = T.nc.alloc_sbuf_tensor_at(name, list(shape), dt, offset=off)
        self.free = self.shape[1:]
        self.strides = []
        s = 1
        for d in reversed(self.free):
            self.strides.insert(0, s)
            s *= d
        self.nbytes = s * self.isz
        self.bufs = {}

    def __call__(self, *idx):
        lo = 0
        hi = 0
        full = []
        for k, d in enumerate(self.free):
            if k < len(idx):
                i = idx[k]
                if isinstance(i, int):
                    a, b = i, i + 1
                    full.append(i)
                else:
                    a = 0 if i.start is None else i.start
                    b = d if i.stop is None else i.stop
                    full.append(slice(a, b))
            else:
                a, b = 0, d
                full.append(slice(0, d))
            lo += a * self.strides[k]
            hi += (b - 1) * self.strides[k]
        hi += 1
        key = (lo, hi)
        b = self.bufs.get(key)
        if b is None:
            b = Buf(f"{self.name}{key}", "sb", self.off + lo * self.isz, self.off + hi * self.isz)
            self.bufs[key] = b
            self.T.sb_bufs.append(b)
            self.T.sb_ver += 1
        ap = self.t[tuple([slice(None)] + full)]
        return V(ap, b)


class Tracker:
    def __init__(self, nc):
        self.nc = nc
        self.dry = False
        self.sb_bufs = []
        self.sb_ver = 0
        self.all_sems = []
        mk = lambda n: self.new_sem(n)
        self.PE = Eng("pe", nc.tensor, mk("s_pe"))
        self.ACT = Eng("act", nc.scalar, mk("s_act"))
        self.DVE = Eng("dve", nc.vector, mk("s_dve"))
        self.POOL = Eng("pool", nc.gpsimd, mk("s_pool"))
        self.SP = Eng("sp", nc.sync, None)
        self.engs = [self.PE, self.ACT, self.DVE, self.POOL, self.SP]
        self.out_stamps = []
        self.cursor = (nc.sbuf_base + 63) // 64 * 64
        self.top = nc.sbuf_top
        self.pe_open = False
        self.ninst = 0

    def new_sem(self, name):
        s = Sem(self.nc.alloc_semaphore(name))
        self.all_sems.append(s)
        return s

    def alloc(self, name, shape, dt, at=None):
        isz = 2 if dt == BF16 else 4
        nb = int(np.prod(shape[1:])) * isz
        nb = (nb + 63) // 64 * 64
        if at is None:
            at = self.cursor
            self.cursor += nb
            assert self.cursor <= self.top, f"SBUF overflow at {name}: {self.cursor} > {self.top}"
        return Tile(self, name, shape, dt, at)

    def dram_v(self, ap, name="d"):
        return V(ap, Buf(name, "dram"))

    def _ovl(self, b):
        if b.space != "sb":
            return (b,)
        if b.ver != self.sb_ver:
            b.ovl = [o for o in self.sb_bufs if o.lo < b.hi and b.lo < o.hi]
            b.ver = self.sb_ver
        return b.ovl

    def _deps(self, eng, reads, writes):
        need = {}
        pe_sem = self.PE.sem
        is_pe = eng is self.PE
        for v in reads:
            for b in self._ovl(v.buf):
                st = b.lastw
                if st is not None:
                    if not (is_pe and st[0] is pe_sem) and need.get(st[0], 0) < st[1]:
                        need[st[0]] = st[1]
                if b.space == "psum":
                    for sem, val in b.reads.items():
                        if sem is not eng.sem and need.get(sem, 0) < val:
                            need[sem] = val
        for v in writes:
            for b in self._ovl(v.buf):
                st = b.lastw
                if st is not None:
                    if not (is_pe and st[0] is pe_sem) and need.get(st[0], 0) < st[1]:
                        need[st[0]] = st[1]
                for sem, val in b.reads.items():
                    if not (is_pe and sem is pe_sem) and need.get(sem, 0) < val:
                        need[sem] = val
        for sem, val in need.items():
            if eng.waited.get(sem, 0) < val:
                eng.h.wait_ge(sem.h, val)
                eng.waited[sem] = val
                self.ninst += 1

    def _mark(self, stamp, reads, writes):
        sem, val = stamp
        for v in writes:
            b = v.buf
            b.lastw = stamp
            b.reads = {}
        for v in reads:
            b = v.buf
            if b.reads.get(sem, 0) < val:
                b.reads[sem] = val

    def _emit(self, eng, fn, reads, writes):
        if self.dry:
            return
        reads = [r for r in reads if isinstance(r, V)]
        self._deps(eng, reads, writes)
        inst = fn()
        eng.sem.count += 1
        inst.then_inc(eng.sem.h, 1)
        self.ninst += 1
        self._mark((eng.sem, eng.sem.count), reads, writes)

    def barrier(self):
        if self.dry:
            return
        for e in self.engs:
            for s in self.all_sems:
                if s.count > 0 and e.waited.get(s, 0) < s.count:
                    e.h.wait_ge(s.h, s.count)
                    e.waited[s] = s.count

    def mm(self, out, lhsT, rhs, start, stop, last):
        if self.dry:
            return
        PE = self.PE
        stamp = (PE.sem, PE.sem.count + 1)
        self._deps(PE, [lhsT, rhs], [out])
        inst = self.nc.tensor.matmul(out.ap, lhsT=lhsT.ap, rhs=rhs.ap, start=start, stop=stop,
                                     skip_group_check=True)
        self.ninst += 1
        self._mark(stamp, [lhsT, rhs], [out])
        self.pe_open = True
        if last:
            inst.then_inc(PE.sem.h, 1)
            PE.sem.count += 1
            self.pe_open = False

    def tr(self, out, in_, ident, last):
        if self.dry:
            return
        PE = self.PE
        stamp = (PE.sem, PE.sem.count + 1)
        self._deps(PE, [in_, ident], [out])
        inst = self.nc.tensor.transpose(out=out.ap, in_=in_.ap, identity=ident.ap)
        self.ninst += 1
        self._mark(stamp, [in_, ident], [out])
        self.pe_open = True
        if last:
            inst.then_inc(PE.sem.h, 1)
            PE.sem.count += 1
            self.pe_open = False

    def act(self, out, in_, func, bias=None, scale=None, accum=None):
        kw = {}
        if bias is not None:
            kw["bias"] = bias.ap if isinstance(bias, V) else bias
        if scale is not None:
            kw["scale"] = scale.ap if isinstance(scale, V) else scale
        writes = [out]
        if accum is not None:
            kw["accum_out"] = accum.ap
            writes.append(accum)
        self._emit(self.ACT, lambda: self.nc.scalar.activation(out=out.ap, in_=in_.ap, func=func, **kw),
                   [in_, bias, scale], writes)

    def _veng(self, e):
        return (self.DVE, self.nc.vector) if e == "dve" else (self.POOL, self.nc.gpsimd)

    def tt(self, e, out, in0, in1, op):
        E, h = self._veng(e)
        self._emit(E, lambda: h.tensor_tensor(out=out.ap, in0=in0.ap, in1=in1.ap, op=op), [in0, in1], [out])

    def ts(self, e, out, in0, s1, s2, op0, op1=None):
        E, h = self._veng(e)
        a1 = s1.ap if isinstance(s1, V) else s1
        a2 = s2.ap if isinstance(s2, V) else s2
        if op1 is None:
            fn = lambda: h.tensor_scalar(out=out.ap, in0=in0.ap, scalar1=a1, scalar2=None, op0=op0)
        else:
            fn = lambda: h.tensor_scalar(out=out.ap, in0=in0.ap, scalar1=a1, scalar2=a2, op0=op0, op1=op1)
        self._emit(E, fn, [in0, s1, s2], [out])

    def stt(self, out, in0, scalar, in1, op0, op1):
        sc = scalar.ap if isinstance(scalar, V) else scalar
        self._emit(self.DVE, lambda: self.nc.vector.scalar_tensor_tensor(
            out=out.ap, in0=in0.ap, scalar=sc, in1=in1.ap, op0=op0, op1=op1), [in0, scalar, in1], [out])

    def copy(self, e, out, in_):
        if e == "act":
            self.act(out, in_, AF.Copy)
            return
        E, h = self._veng(e)
        self._emit(E, lambda: h.tensor_copy(out=out.ap, in_=in_.ap), [in_], [out])

    def recip(self, out, in_):
        self._emit(self.DVE, lambda: self.nc.vector.reciprocal(out=out.ap, in_=in_.ap), [in_], [out])

    def memset(self, out, val):
        self._emit(self.DVE, lambda: self.nc.vector.memset(out.ap, val), [], [out])

    def _dsem(self, b):
        if b.dsem is None:
            b.dsem = self.new_sem(f"s_dma{len(self.all_sems)}")
        return b.dsem

    def dma(self, out, in_, q=None, is_output=False, nonctg=False):
        if self.dry:
            return
        q = q or self.SP
        side = out if out.buf.space == "sb" else in_
        sem = self._dsem(side.buf)
        self._deps(q, [in_], [out])
        if nonctg:
            with self.nc.allow_non_contiguous_dma(reason="tiny strided constant load"):
                inst = q.h.dma_start(out=out.ap, in_=in_.ap)
        else:
            inst = q.h.dma_start(out=out.ap, in_=in_.ap)
        inst.then_inc(sem.h, 16)
        sem.count += 16
        self.ninst += 1
        st = (sem, sem.count)
        self._mark(st, [in_], [out])
        if is_output:
            self.out_stamps.append(st)

    def gather(self, out, table_ap, idx):
        if self.dry:
            return
        q = self.POOL
        sem = self._dsem(out.buf)
        self._deps(q, [idx], [out])
        inst = self.nc.gpsimd.indirect_dma_start(
            out=out.ap, out_offset=None, in_=table_ap,
            in_offset=bass.IndirectOffsetOnAxis(ap=idx.ap, axis=0))
        inst.then_inc(sem.h, 16)
        sem.count += 16
        self.ninst += 1
        self._mark((sem, sem.count), [idx], [out])

    def finish(self):
        if self.dry:
            return
        assert not self.pe_open
        sp = self.SP
        for sem, val in self.out_stamps:
            if sp.waited.get(sem, 0) < val:
                sp.h.wait_ge(sem.h, val)
                sp.waited[sem] = val


def weight_blocks(dr):
    blocks = []

    def pview(W2d, r0, kc):
        return W2d[r0:r0 + kc * 128, :].rearrange("(k p) n -> p k n", p=128)

    for f in (1, 2):
        gu = dr[f"w_ffn{f}_gu"][0]
        dn = dr[f"w_ffn{f}_down"][0]
        gv = pview(gu, 0, 8)
        for c in range(NCH):
            blocks.append(((f"f{f}gu", c), 8, 256,
                           [(gv[:, :, c * 128:(c + 1) * 128], 0, 128),
                            (gv[:, :, DFF + c * 128:DFF + (c + 1) * 128], 128, 128)]))
        for g in range(NCH // 2):
            dv = pview(dn, g * 256, 2)
            blocks.append(((f"f{f}dn", g), 2, 1024, [(dv[:, :, :], 0, 1024)]))
    win = pview(dr["w_in"][0], 0, 8)
    for name, nblk in (("q_sb", 2), ("k_sb", 2), ("v_sb", 2), ("q_r", 2), ("k_r", 2), ("v_r", 4), ("g_r", 4),
                       ("a_sb", 4), ("a_r", 4)):
        c0 = W_IN_COLS[name]
        for j in range(nblk):
            blocks.append(((name, j), 8, 256, [(win[:, :, c0 + j * 256:c0 + (j + 1) * 256], 0, 256)]))
    for name in ("q_r", "k_r"):
        c0 = W_IN_COLS[name]
        for j in range(2):
            pcs = []
            for cc in range(2):
                h0 = c0 + (2 * j + cc) * 128
                pcs.append((win[:, :, h0 + 64:h0 + 128], cc * 128, 64))
                pcs.append((win[:, :, h0:h0 + 64], cc * 128 + 64, 64))
            blocks.append(((name + "_sw", j), 8, 256, pcs))
    sbo = pview(dr["w_sb_out"][0], 0, 4)
    for j in range(2):
        blocks.append((("sbo", j), 4, 512, [(sbo[:, :, j * 512:(j + 1) * 512], 0, 512)]))
    reto = pview(dr["w_ret_out"][0], 0, 8)
    wo = pview(dr["w_o"][0], 0, 8)
    for j in range(4):
        blocks.append((("reto", j), 8, 256, [(reto[:, :, j * 256:(j + 1) * 256], 0, 256)]))
        blocks.append((("wo", j), 8, 256, [(wo[:, :, j * 256:(j + 1) * 256], 0, 256)]))
    return blocks


class WStream:
    def __init__(self, T, slots, scr_views, plan):
        self.T = T
        self.slots = slots
        self.scr = scr_views
        self.plan = plan
        self.rec = []
        self.i = 0
        self.loaded = 0

    def get(self, key, hold=0):
        T = self.T
        if T.dry:
            self.rec.append(key)
            _, kc, bw = self.scr[key]
            sl = self.slots[0]
            return V(sl.t[:, 0:kc * bw].rearrange("p (k n) -> p k n", k=kc), sl().buf)
        i = self.i
        assert self.plan[i] == key, (i, self.plan[i], key)
        upto = min(len(self.plan) - 1, i + NSLOT - 1 - hold)
        upto = max(upto, i)
        while self.loaded <= upto:
            j = self.loaded
            k = self.plan[j]
            src, kc, bw = self.scr[k]
            sl = self.slots[j % NSLOT]
            T.dma(sl(slice(0, kc * bw)), src)
            self.loaded += 1
        self.i += 1
        _, kc, bw = self.scr[key]
        sl = self.slots[i % NSLOT]
        v = sl(slice(0, kc * bw))
        return V(v.ap.rearrange("p (k n) -> p k n", k=kc), v.buf)


def gammas():
    h = np.arange(RET_H, dtype=np.float64)
    return np.exp(np.log1p(-np.exp2(-5.0 - h)))


def host_constants():
    c = {}
    c["ident_bf"] = np.eye(128, dtype=np.float32).astype(ml_dtypes.bfloat16)
    c["ident_f"] = np.eye(128, dtype=np.float32)
    j = np.arange(128)[:, None]
    k = np.arange(128)[None, :]
    c["triS"] = (j > k).astype(np.float32).astype(ml_dtypes.bfloat16)
    c["triC"] = (j <= k).astype(np.float32).astype(ml_dtypes.bfloat16)
    q = np.arange(512)[None, :]
    m = np.stack([((np.arange(128)[:, None] + 128 * o) < q) for o in range(4)], axis=1)
    c["masks"] = m.astype(np.float32).astype(ml_dtypes.bfloat16)
    m8 = np.zeros((128, 64), np.float32)
    for hh in range(8):
        for qq in range(8):
            m8[:8, hh * 8 + qq] = (np.arange(8) < qq)
    c["mask8"] = m8
    g = gammas()
    s = np.arange(128, dtype=np.float64)
    sc = 128.0 ** -0.5
    din = np.zeros((128, 4, 128), np.float64)
    for hh in range(4):
        din[:, hh, :] = (g[hh] ** (-(s[:, None] + 1.0))) * (s[None, :] >= s[:, None]) * sc
    c["din"] = din.astype(np.float32)
    din8 = np.zeros((128, 4, 8), np.float64)
    s8 = np.arange(8, dtype=np.float64)
    for hh in range(4):
        din8[:8, hh, :] = (g[hh] ** (-(s8[:, None] + 1.0))) * (s8[None, :] >= s8[:, None]) * sc
    c["din8"] = din8.astype(np.float32)
    rt = np.zeros((128, 12), np.float64)
    rt8 = np.zeros((128, 12), np.float64)
    for hh in range(4):
        rt[:, hh] = g[hh] ** (127.0 - s) * sc
        rt[:, 4 + hh] = g[hh] ** (s + 1.0)
        rt[:, 8 + hh] = g[hh] ** (2.0 * (s + 1.0)) / 256.0
        rt8[:8, hh] = g[hh] ** (7.0 - s8) * sc
        rt8[:8, 4 + hh] = g[hh] ** (s8 + 1.0)
        rt8[:8, 8 + hh] = g[hh] ** (2.0 * (s8 + 1.0)) / 256.0
    c["rtab"] = rt.astype(np.float32)
    c["rtab8"] = rt8.astype(np.float32)
    c["pidx"] = np.arange(128, dtype=np.float32)[:, None]
    return c


def rope_tables(pos):
    half = 64
    freq = (10000.0 ** (-np.arange(half, dtype=np.float32) / half)).astype(np.float32)
    ang = pos.astype(np.float32)[None, :] * freq[:, None]
    cos = np.cos(ang).astype(np.float32)
    sin = np.sin(ang).astype(np.float32)
    return (np.concatenate([cos, cos], 0).astype(np.float32),
            np.concatenate([-sin, sin], 0).astype(np.float32))


_DRAM_IN = [
    ("x_own", [HALF, D], F32), ("x_pre", [HALF, D], F32), ("x_smp", [SEQ_PER_CORE * DEC, D], F32),
    ("cache_kv", [NPOOL * PAGE, 1024], F32),
    ("state_in", [SEQ_PER_CORE, RET_H, 128, 256], F32), ("ptab", [SEQ_PER_CORE, NPAGES], I32),
    ("flag", [128, 1], F32),
    ("g_ffn1_pre", [1, D], F32), ("w_ffn1_gu", [1, D, 2 * DFF], F32), ("w_ffn1_down", [1, DFF, D], F32),
    ("g_ffn1_post", [1, D], F32), ("g_mix_pre", [1, D], F32), ("w_in", [1, D, 6656], F32),
    ("sb_bias", [1, SB_H], F32), ("ret_gn_g", [1, D], F32), ("w_sb_out", [1, 512, D], F32),
    ("w_ret_out", [1, D, D], F32), ("w_o", [1, D, D], F32), ("g_mix_post", [1, D], F32),
    ("g_ffn2_pre", [1, D], F32), ("w_ffn2_gu", [1, D, 2 * DFF], F32), ("w_ffn2_down", [1, DFF, D], F32),
    ("g_ffn2_post", [1, D], F32),
    ("ident_bf", [128, 128], BF16), ("ident_f", [128, 128], F32), ("triS", [128, 128], BF16),
    ("triC", [128, 128], BF16), ("masks", [128, 4, 512], BF16), ("mask8", [128, 64], F32),
    ("din", [128, 4, 128], F32), ("din8", [128, 4, 8], F32), ("rtab", [128, 12], F32), ("rtab8", [128, 12], F32),
    ("pidx", [128, 1], F32),
    ("rope_cos_own", [128, HALF], F32), ("rope_sin_own", [128, HALF], F32),
    ("rope_cos_pre", [128, HALF], F32), ("rope_sin_pre", [128, HALF], F32),
    ("rope_cos_smp", [128, 32], F32), ("rope_sin_smp", [128, 32], F32),
]
_DRAM_OUT = [
    ("y_own", [HALF, D], F32), ("y_smp", [32, D], F32), ("k_rows", [HALF, 512], F32), ("v_rows", [HALF, 512], F32),
    ("ret_state", [RET_H, 128, 256], F32), ("ks_rows", [32, 512], F32), ("vs_rows", [32, 512], F32),
    ("ret_state_s", [SEQ_PER_CORE, RET_H, 128, 256], F32),
]


def build(stage=99):
    nc = bass.Bass("TRN2", target_bir_lowering=False)
    dr = {}
    for name, shape, dt in _DRAM_IN:
        if name == "cache_kv" and stage < 9:
            continue
        dr[name] = nc.dram_tensor(name, shape, dt, kind="ExternalInput").ap()
    for name, shape, dt in _DRAM_OUT:
        dr[name] = nc.dram_tensor(name, shape, dt, kind="ExternalOutput").ap()
    T = Tracker(nc)
    blocks = weight_blocks(dr)
    tot = sum(128 * kc * bw for _, kc, bw, _ in blocks)
    scr = nc.dram_tensor("wscr", [tot], BF16, kind="Internal").ap()
    scr_views = {}
    off = 0
    for key, kc, bw, _ in blocks:
        n = 128 * kc * bw
        scr_views[key] = (T.dram_v(scr[off:off + n].rearrange("(p m) -> p m", p=128), f"scr{key}"), kc, bw)
        off += n

    PSb = [nc.alloc_psum_tensor(f"ps{i}", [128, 512], F32) for i in range(8)]
    PS = [V(PSb[i][:], Buf(f"ps{i}", "psum")) for i in range(8)]
    dv = lambda name: T.dram_v(dr[name], name)

    base0 = T.cursor
    stg = [T.alloc(f"stg{i}", [128, SLOT_E], F32) for i in range(3)]
    wb = [T.alloc(f"wb{i}", [128, SLOT_E], BF16) for i in range(3)]
    cast_engs = ["act", "dve", "pool"]
    def p0_load(bi):
        key, kc, bw, pieces = blocks[bi]
        sv = stg[bi % 3](slice(0, kc * bw))
        s3 = V(sv.ap.rearrange("p (k n) -> p k n", k=kc), sv.buf)
        for src, c0, wd in pieces:
            T.dma(s3[:, :, c0:c0 + wd], T.dram_v(src, "w"))

    p0_load(0)
    p0_load(1)
    for bi, (key, kc, bw, pieces) in enumerate(blocks):
        if bi + 2 < len(blocks):
            p0_load(bi + 2)
        sv = stg[bi % 3](slice(0, kc * bw))
        wv = wb[bi % 3](slice(0, kc * bw))
        T.copy(cast_engs[bi % 3], wv, sv)
        T.dma(scr_views[key][0], wv)
    T.barrier()

    T.cursor = base0
    A = T.alloc
    X = A("X", [128, 4, D], F32)
    xnb = [A(f"xnb{i}", [128, D], BF16) for i in range(2)]
    xT = A("xT", [128, 8, TN], BF16)
    uT = A("uT", [128, 8, TN], BF16)
    QsT = A("QsT", [128, 4, TN], BF16, at=xT.off)
    osT = A("osT", [128, 4, TN], BF16, at=xT.off + 4 * TN * 2)
    hT = A("hT", [128, NCH, TN], BF16)
    T.cursor += 24 * 1024 - hT.nbytes
    orT = A("orT", [128, 8, TN], BF16, at=hT.off)
    mT = A("mT", [128, 8, TN], BF16, at=hT.off + 8192)
    Vr = A("Vr", [128, 4, D], BF16, at=hT.off + 16384)
    ring = [A(f"wr{i}", [128, SLOT_E], BF16) for i in range(NSLOT)]
    KsT = A("KsT", [128, 4, SEQ], BF16)
    Vsb = A("Vsb", [128, 32, 512], BF16)
    S = A("S", [128, 4, 256], F32)
    Sbf = A("Sbf", [128, 4, 256], BF16)
    Kdec = [A(f"Kdec{i}", [128, 4, 128], BF16) for i in range(2)]
    ropec = A("ropec", [128, TN], F32)
    ropes = A("ropes", [128, TN], F32)
    gcur = A("gcur", [128, D], F32)
    gng = A("gng", [128, D], F32)
    gpT = A("gpT", [128, 3, 8], F32)
    ident_bf = A("ident_bf", [128, 128], BF16)
    ident_f = A("ident_f", [128, 128], F32)
    triS = A("triS", [128, 128], BF16)
    triC = A("triC", [128, 128], BF16)
    triSP = A("triSP", [128, 128], BF16)
    triCP = A("triCP", [128, 128], BF16)
    masks = A("masks", [128, 4, 512], BF16)
    mask8 = A("mask8", [128, 64], F32)
    din = A("din", [128, 4, 128], F32)
    din8 = A("din8", [128, 4, 8], F32)
    rtab = A("rtab", [128, 12], F32)
    rtab8 = A("rtab8", [128, 12], F32)
    pidx = A("pidx", [128, 1], F32)
    flag = A("flag", [128, 1], F32)
    nb = A("nb", [128, 8], F32)
    nb64 = A("nb64", [128, 8, 8], F32)
    sbb = A("sbb", [128, 8], F32)
    small = A("small", [128, 64], F32)
    ptt = A("ptt", [128, SEQ_PER_CORE * NPAGES], I32)
    pgi = A("pgi", [128, SEQ_PER_CORE * NPAGES], I32)
    KsTs = A("KsTs", [128, 4, 32], BF16)
    TEMP0 = T.cursor
    TEMP_SZ = T.top - TEMP0
    assert TEMP_SZ >= 26 * 1024, TEMP_SZ

    _tt = {}

    def TT(name, off, shape, dt):
        key = (name, off, tuple(shape), dt == BF16)
        t = _tt.get(key)
        if t is None:
            isz = 2 if dt == BF16 else 4
            nbts = int(np.prod(shape[1:])) * isz
            assert TEMP0 + off + nbts <= T.top, ("TEMP overflow", name, TEMP0 + off + nbts - T.top)
            t = Tile(T, f"tmp_{name}_{off}", shape, dt, TEMP0 + off)
            _tt[key] = t
        return t
    K1 = 1024

    for tl, name in ((ident_bf, "ident_bf"), (ident_f, "ident_f"), (triS, "triS"), (triC, "triC"), (masks, "masks"),
                     (mask8, "mask8"), (din, "din"), (din8, "din8"), (rtab, "rtab"), (rtab8, "rtab8"),
                     (pidx, "pidx"), (flag, "flag")):
        T.dma(tl(), dv(name))
    T.dma(gng(), V(dr["ret_gn_g"][0:1, :].partition_broadcast(128), Buf("g", "dram")))
    for i, name in enumerate(("g_ffn1_pre", "g_mix_pre", "g_ffn2_pre")):
        T.dma(gpT(i), V(dr[name].rearrange("o (c p) -> p (o c)", p=128), Buf("g", "dram")), nonctg=True)
    T.dma(sbb(), V(dr["sb_bias"][0:1, :].partition_broadcast(128), Buf("g", "dram")))
    T.ts("dve", nb(), sbb(), -1.0, None, ALU.mult)
    T.copy("dve", nb64(), V(nb().ap.unsqueeze(2).to_broadcast([128, 8, 8]), nb().buf))
    T.ts("dve", triSP(), triS(), flag(), None, ALU.mult)
    T.ts("dve", triCP(), triC(), flag(), None, ALU.mult)
    T.dma(ptt(), V(dr["ptab"].rearrange("s j -> (s j)").rearrange("(o n) -> o n", o=1).partition_broadcast(128),
                   Buf("g", "dram")))
    T.ts("dve", pgi(), ptt(), 128.0, pidx(), ALU.mult, ALU.add)
    T.memset(S(), 0.0)
    T.memset(Sbf(), 0.0)

    W = WStream(T, ring, scr_views, None)
    small_i = [0]

    def sm(n=1):
        i = small_i[0]
        if i + n > 64:
            i = 0
        small_i[0] = i + n
        return small(slice(i, i + n))

    def rstd_of(ssv, rows, scale, n=1):
        rt = sm(n)
        T.act(rt[:rows], ssv[:rows], AF.Sqrt, bias=EPS, scale=scale)
        rs = sm(n)
        T.recip(rs[:rows], rt[:rows])
        return rs

    def norm_transpose(blocks_, gidx, outT, psbank, joff):
        junk = TT("junkN", joff * K1, [128, D], BF16)
        for bi, (tb, col0, rows) in enumerate(blocks_):
            ss = sm()
            T.act(junk()[:rows], X(tb)[:rows], AF.Square, accum=ss[:rows])
            rs = rstd_of(ss, rows, 1.0 / D)
            xb = xnb[bi % 2]
            T.ts("dve", xb()[:rows], X(tb)[:rows], rs[:rows], None, ALU.mult)
            ps = PS[psbank + bi % 2]
            psb = V(ps.ap.bitcast(BF16).rearrange("p (c t) -> p c t", c=8), ps.buf)
            for c in range(8):
                T.tr(psb[:, c, 0:rows], xb(slice(c * 128, (c + 1) * 128))[:rows], ident_bf()[:rows, :rows],
                     last=(c == 7))
            ov = outT(slice(0, 8), slice(col0, col0 + rows))
            T.tt("dve", ov, psb[:, :, 0:rows],
                 V(gpT(gidx).ap.unsqueeze(2).to_broadcast([128, 8, rows]), gpT(gidx).buf), ALU.mult)

    def postnorm_residual(psA, psB, tb, rows, half_scale, base, slot):
        junk = TT("pjunk", base * K1, [128, 512], BF16)
        ss = sm(2)
        T.act(junk()[:rows], psA[:rows], AF.Square, accum=ss[:rows, 0:1])
        T.act(junk()[:rows], psB[:rows], AF.Square, accum=ss[:rows, 1:2])
        st = sm()
        T.tt("dve", st[:rows], ss[:rows, 0:1], ss[:rows, 1:2], ALU.add)
        rs = rstd_of(st, rows, 1.0 / D)
        t = TT("pt", (base + 2 + 4 * slot) * K1, [128, D], F32)
        T.stt(t(slice(0, 512))[:rows], psA[:rows], rs[:rows], gcur(slice(0, 512))[:rows], ALU.mult, ALU.mult)
        T.stt(t(slice(512, 1024))[:rows], psB[:rows], rs[:rows], gcur(slice(512, 1024))[:rows], ALU.mult, ALU.mult)
        if half_scale:
            T.stt(X(tb)[:rows], t()[:rows], 0.5, X(tb)[:rows], ALU.mult, ALU.add)
        else:
            T.tt("pool", X(tb)[:rows], t()[:rows], X(tb)[:rows], ALU.add)

    def ffn(f, N, tokblocks, gidx, gname):
        T.dma(gcur(), V(dr[gname][0:1, :].partition_broadcast(128), Buf("g", "dram")))
        norm_transpose(tokblocks, gidx, xT, 6, 0)
        sg = [TT("sg", (2 + 2 * i) * K1, [128, TN], F32) for i in range(2)]
        for c in range(NCH):
            w = W.get((f"f{f}gu", c))
            b0 = (c % 3) * 2
            psg, psu = PS[b0], PS[b0 + 1]
            for kc in range(8):
                T.mm(psg[:, :N], w[:, kc, 0:128], xT(kc, slice(0, N)), kc == 0, kc == 7, kc == 7)
            for kc in range(8):
                T.mm(psu[:, :N], w[:, kc, 128:256], xT(kc, slice(0, N)), kc == 0, kc == 7, kc == 7)
            s = sg[c % 2]
            T.act(s(slice(0, N)), psg[:, :N], AF.Silu)
            T.tt("dve", hT(c, slice(0, N)), s(slice(0, N)), psu[:, :N], ALU.mult)
        passes = [tokblocks[i:i + 2] for i in range(0, len(tokblocks), 2)]
        for pi, pb in enumerate(passes):
            bank0 = (pi % 2) * 4
            for g in range(NCH // 2):
                w = W.get((f"f{f}dn", g))
                for kk in range(2):
                    kc = 2 * g + kk
                    for bi, (tb, col0, rows) in enumerate(pb):
                        for j in range(2):
                            T.mm(PS[bank0 + 2 * bi + j][:rows, :], hT(kc, slice(col0, col0 + rows)),
                                 w[:, kk, j * 512:(j + 1) * 512], kc == 0, kc == NCH - 1,
                                 kk == 1 and bi == len(pb) - 1 and j == 1)
            for bi, (tb, col0, rows) in enumerate(pb):
                postnorm_residual(PS[bank0 + 2 * bi], PS[bank0 + 2 * bi + 1], tb, rows, True, 6, bi)

    def proj_fm(wname, nblk, N, src, evac, bank0=0, sw=None):
        for j in range(nblk):
            w = W.get((wname, j))
            w2 = W.get((sw, j), hold=1) if sw else None
            for cc in range(2):
                c = 2 * j + cc
                ps = PS[bank0 + (c % 2) * 2]
                for kc in range(8):
                    T.mm(ps[:, :N], w[:, kc, cc * 128:(cc + 1) * 128], src(kc), kc == 0, kc == 7, kc == 7)
                ps2 = None
                if sw:
                    ps2 = PS[bank0 + (c % 2) * 2 + 1]
                    for kc in range(8):
                        T.mm(ps2[:, :N], w2[:, kc, cc * 128:(cc + 1) * 128], src(kc), kc == 0, kc == 7, kc == 7)
                evac(c, ps, ps2)

    def proj_tm(wname, jlist, tokblocks, src_cols, banks, col_of):
        for j in jlist:
            w = W.get((wname, j))
            for bi, (tb, col0, rows) in enumerate(tokblocks):
                ps = banks(bi, j)
                c0 = col_of(j)
                for kc in range(8):
                    T.mm(ps[:rows, c0:c0 + 256], src_cols(kc, col0, rows), w[:, kc, :], kc == 0, kc == 7, kc == 7)

    def rope_evac(dst):
        def f(c, ps, ps2, N):
            t1 = TT("rope1", 18 * K1, [128, TN], F32)
            t2 = TT("rope2", 20 * K1, [128, TN], F32)
            T.tt("dve", t1(slice(0, N)), ps[:, :N], ropec(slice(0, N)), ALU.mult)
            T.tt("dve", t2(slice(0, N)), ps2[:, :N], ropes(slice(0, N)), ALU.mult)
            T.tt("pool", dst(c, slice(0, N)), t1(slice(0, N)), t2(slice(0, N)), ALU.add)
        return f

    def retention_block(rows, qv, kv, vr, din_t, rt_t, dc, do_out, sg_v, or_dst, psbase):
        C = rows
        if do_out:
            psI = PS[psbase]
            for h in range(4):
                T.mm(psI[:C, h * C:(h + 1) * C], kv(h), qv(h), True, True, h == 3)
            inT = TT("inT", 10 * K1, [128, 4, 128], BF16)
            T.tt("dve", inT(slice(0, 4), slice(0, C))[:C],
                 V(psI.ap[:C, 0:4 * C].rearrange("p (h c) -> p h c", h=4), psI.buf),
                 din_t(slice(0, 4), slice(0, C))[:C], ALU.mult)
            pso = [PS[psbase + 1], PS[psbase + 2]]
            for h in range(4):
                o = pso[h // 2][:C, (h % 2) * 256:(h % 2 + 1) * 256]
                T.mm(o, inT(h, slice(0, C))[:C], vr[:C, h * 256:(h + 1) * 256], True, False, False)
                T.mm(o, qv(h), Sbf(h), False, True, h % 2 == 1)
        kd = Kdec[0]
        psk = PS[psbase + 3]
        pskb = V(psk.ap.bitcast(BF16).rearrange("p (c t) -> p c t", c=8), psk.buf)
        for h in range(4):
            T.tr(pskb[:C, h, :], kv(h), ident_bf(), last=(h == 3))
        for h in range(4):
            T.ts("dve", kd(h)[:C], pskb[:C, h, :], rt_t(slice(h, h + 1))[:C], None, ALU.mult)
        pss = [PS[psbase + 4], PS[psbase + 5]]
        for h in range(4):
            T.mm(pss[h // 2][:, (h % 2) * 256:(h % 2 + 1) * 256], kd(h)[:C], vr[:C, h * 256:(h + 1) * 256],
                 True, True, h % 2 == 1)
        if do_out:
            junk = TT("rjunk", 11 * K1, [128, 256], BF16)
            ss = sm(4)
            for h in range(4):
                T.act(junk()[:C], pso[h // 2][:C, (h % 2) * 256:(h % 2 + 1) * 256], AF.Square,
                      accum=ss[:C, h:h + 1])
            s2 = sm(4)
            T.tt("dve", s2[:C], ss[:C], rt_t(slice(8, 12))[:C], ALU.mult)
            rs = rstd_of(s2, C, 1.0, n=4)
            fac = sm(4)
            T.tt("dve", fac[:C], rs[:C], rt_t(slice(4, 8))[:C], ALU.mult)
            tn = TT("tn", 12 * K1, [128, D], F32)
            for h in range(4):
                T.stt(tn(slice(h * 256, (h + 1) * 256))[:C], pso[h // 2][:C, (h % 2) * 256:(h % 2 + 1) * 256],
                      fac[:C, h:h + 1], gng(slice(h * 256, (h + 1) * 256))[:C], ALU.mult, ALU.mult)
            orb = TT("orb", 16 * K1, [128, D], BF16)
            T.tt("pool", orb()[:C], tn()[:C], sg_v[:C], ALU.mult)
            pst = PS[psbase]
            pstb = V(pst.ap.bitcast(BF16).rearrange("p (c t) -> p c t", c=8), pst.buf)
            for c in range(8):
                T.tr(pstb[:, c, 0:C], orb(slice(c * 128, (c + 1) * 128))[:C], ident_bf()[:C, :C], last=(c == 7))
            or_dst(pstb[:, :, 0:C])
        for h in range(4):
            T.stt(S(h), S(h), float(dc[h]), pss[h // 2][:, (h % 2) * 256:(h % 2 + 1) * 256], ALU.mult, ALU.add)
        T.copy("pool", Sbf(), S())

    g = gammas()
    dc128 = g ** 128.0
    dc8 = g ** 8.0

    aL = [Tile(T, f"aL{i}", [128, TN], BF16, mT.off + i * 1024) for i in range(6)]
    aA = [Tile(T, f"aA{i}", [128, TN], BF16, mT.off + 6144 + i * 1024) for i in range(2)]

    def sb_attention_prompt(t_own, core_blocks_before):
        R = 6
        E = [TT("aE", (2 * i) * K1, [128, TN], F32) for i in range(R)]
        Xt = [TT("aX", (12 + 2 * i) * K1, [128, TN], F32) for i in range(R)]
        L = aL
        Aa = aA
        qblk0 = core_blocks_before + t_own * 4
        kb_last = qblk0 + 3
        units = [(pair, h, kb) for pair in range(4) for kb in range(kb_last, -1, -1)
                 for h in (2 * pair, 2 * pair + 1)]
        n = len(units)

        def stA(u):
            pair, h, kb = units[u]
            r0 = (h % 2) * 64
            psz = PS[4 + u % 4]
            T.mm(psz, KsT(pair, slice(kb * 128, (kb + 1) * 128))[r0:r0 + 64],
                 QsT(pair)[r0:r0 + 64], True, True, True)
            e = E[u % R]
            T.ts("dve", Xt[u % R](), psz, -0.125, nb(slice(h, h + 1)), ALU.mult, ALU.add)
            T.act(e(), Xt[u % R](), AF.Exp)
            T.act(e(), e(), AF.Ln, bias=1.0)
            T.tt("pool", L[u % R](), Xt[u % R](), e(), ALU.subtract)
            if kb >= qblk0:
                T.tt("pool", L[u % R](), L[u % R](), masks(kb - qblk0), ALU.mult)

        def stB(u):
            pair, h, kb = units[u]
            pst = PS[2 + h % 2]
            pre = kb < 16
            T.mm(pst, (triSP if pre else triS)(), L[u % R](), kb == kb_last, True, True)
            T.tt("dve", Xt[u % R](), pst, E[u % R](), ALU.subtract)

        def stC(u):
            pair, h, kb = units[u]
            pst = PS[2 + h % 2]
            pre = kb < 16
            if kb > 0:
                T.mm(pst, (triCP if pre else triC)(), L[u % R](), False, True, True)
            a = Aa[u % 2]
            T.act(a(), Xt[u % R](), AF.Exp)
            if kb >= qblk0:
                T.tt("pool", a(), a(), masks(kb - qblk0), ALU.mult)

        def stD(u):
            pair, h, kb = units[u]
            r0 = (h % 2) * 64
            pso = PS[pair % 2]
            T.mm(pso[r0:r0 + 64, :], Vsb(kb, slice(h * 64, (h + 1) * 64)), Aa[u % 2](), kb == kb_last, kb == 0, True)
            if kb == 0 and h % 2 == 1:
                T.copy("act", osT(pair), pso)

        for step in range(n + 5):
            if step < n:
                stA(step)
            if 0 <= step - 3 < n:
                stB(step - 3)
            if 0 <= step - 4 < n:
                stC(step - 4)
            if 0 <= step - 5 < n:
                stD(step - 5)

    def load_rope(cos_name, sin_name, c0, N):
        T.dma(ropec(slice(0, N)), V(dr[cos_name][:, c0:c0 + N], Buf("g", "dram")))
        T.dma(ropes(slice(0, N)), V(dr[sin_name][:, c0:c0 + N], Buf("g", "dram")))

    FULL = [(tb, tb * 128, 128) for tb in range(4)]

    def prefix_tile(t):
        T.dma(V(X.t[:], X().buf), V(dr["x_pre"][t * TN:(t + 1) * TN, :].rearrange("(b p) d -> p b d", p=128),
                                    Buf("g", "dram")))
        load_rope("rope_cos_pre", "rope_sin_pre", t * TN, TN)
        ffn(1, TN, FULL, 0, "g_ffn1_post")
        norm_transpose(FULL, 1, uT, 6, 8)
        src = lambda kc: uT(kc)
        proj_fm("k_sb", 2, TN, src,
                lambda c, ps, ps2: T.copy("act", KsT(c, slice(t * TN, (t + 1) * TN)), ps))
        proj_tm("v_sb", [0, 1], FULL, lambda kc, col0, rows: uT(kc, slice(col0, col0 + rows)),
                lambda bi, j: PS[4 + bi], lambda j: j * 256)
        for bi in range(4):
            T.ts("dve", Vsb(t * 4 + bi), PS[4 + bi], flag(), None, ALU.mult)
        re = rope_evac(lambda c, sl: KrT(c, sl))
        proj_fm("k_r", 2, TN, src, lambda c, ps, ps2: re(c, ps, ps2, TN), sw="k_r_sw")
        for hf in range(2):
            proj_tm("v_r", [2 * hf, 2 * hf + 1], FULL, lambda kc, col0, rows: uT(kc, slice(col0, col0 + rows)),
                    lambda bi, j: PS[4 * (hf % 2) + bi], lambda j: (j % 2) * 256)
            for bi in range(4):
                T.ts("dve", Vr(bi, slice(hf * 512, (hf + 1) * 512)), PS[4 * (hf % 2) + bi], flag(), None, ALU.mult)
        for tb in range(4):
            retention_block(128, None, lambda h: KrT(h, slice(tb * 128, (tb + 1) * 128)), Vr(tb),
                            din, rtab, dc128, False, None, None, 0)

    def own_tile(t, core_blocks_before, N=TN, tokblocks=FULL, sample=False):
        if not sample:
            T.dma(V(X.t[:], X().buf), V(dr["x_own"][t * TN:(t + 1) * TN, :].rearrange("(b p) d -> p b d", p=128),
                                        Buf("g", "dram")))
            load_rope("rope_cos_own", "rope_sin_own", t * TN, TN)
        else:
            T.dma(X(0)[:32], dv("x_smp"))
            load_rope("rope_cos_smp", "rope_sin_smp", 0, 32)
        ffn(1, N, tokblocks, 0, "g_ffn1_post")
        norm_transpose(tokblocks, 1, uT, 6, 8)
        src = lambda kc: uT(kc, slice(0, N))
        srcc = lambda kc, col0, rows: uT(kc, slice(col0, col0 + rows))
        kpos = (core_blocks_before * 128 + t * TN) if not sample else 0
        proj_fm("q_sb", 2, N, src, lambda c, ps, ps2: T.copy("act", QsT(c, slice(0, N)), ps[:, :N]))
        if not sample:
            kdst = lambda c: KsT(c, slice(kpos, kpos + N))
        else:
            kdst = lambda c: KsTs(c, slice(0, N))
        for j in range(2):
            w = W.get(("k_sb", j))
            for cc in range(2):
                c = 2 * j + cc
                ps = PS[(c % 2)]
                for kc in range(8):
                    T.mm(ps[:, :N], w[:, kc, cc * 128:(cc + 1) * 128], src(kc), kc == 0, kc == 7, kc == 7)
                T.copy("act", kdst(c), ps[:, :N])
            for bi, (tb, col0, rows) in enumerate(tokblocks):
                ps = PS[4 + bi]
                for kc in range(8):
                    T.mm(ps[:rows, j * 256:(j + 1) * 256], srcc(kc, col0, rows), w[:, kc, :], kc == 0, kc == 7, kc == 7)
        for bi, (tb, col0, rows) in enumerate(tokblocks):
            ko = TT("ko", (10 + 2 * (bi % 2)) * K1, [128, 512], F32)
            T.copy("dve", ko()[:rows], PS[4 + bi][:rows])
            if not sample:
                T.dma(T.dram_v(dr["k_rows"][t * TN + col0:t * TN + col0 + rows, :], "o"), ko()[:rows], is_output=True)
            else:
                T.dma(T.dram_v(dr["ks_rows"][:, :], "o"), ko()[:rows], is_output=True)
        vblocks = tokblocks if not sample else [(0, s * 8, 8) for s in range(4)]
        proj_tm("v_sb", [0, 1], vblocks, srcc, lambda bi, j: PS[bi], lambda j: j * 256)
        for bi, (tb, col0, rows) in enumerate(vblocks):
            if not sample:
                vo = TT("vo", (14 + 2 * (bi % 2)) * K1, [128, 512], F32)
                T.copy("dve", vo()[:rows], PS[bi][:rows])
                T.dma(T.dram_v(dr["v_rows"][t * TN + col0:t * TN + col0 + rows, :], "o"), vo()[:rows], is_output=True)
                T.copy("act", Vsb((kpos // 128) + bi), vo())
            else:
                T.copy("dve", Vnew[bi]()[:8], PS[bi][:8])
                T.dma(T.dram_v(dr["vs_rows"][bi * 8:(bi + 1) * 8, :], "o"), Vnew[bi]()[:8], is_output=True)
                T.copy("act", Vnb[bi]()[:8], Vnew[bi]()[:8])
        re_q = rope_evac(lambda c, sl: QrT(c, sl))
        proj_fm("q_r", 2, N, src, lambda c, ps, ps2: re_q(c, ps, ps2, N), bank0=4, sw="q_r_sw")
        re_k = rope_evac(lambda c, sl: KrT(c, sl))
        proj_fm("k_r", 2, N, src, lambda c, ps, ps2: re_k(c, ps, ps2, N), bank0=0, sw="k_r_sw")
        for hf in range(2):
            proj_tm("v_r", [2 * hf, 2 * hf + 1], vblocks, srcc,
                    lambda bi, j: PS[4 * (hf % 2) + bi], lambda j: (j % 2) * 256)
            for bi, (tb, col0, rows) in enumerate(vblocks):
                T.copy("act", Vr(bi, slice(hf * 512, (hf + 1) * 512))[:rows], PS[4 * (hf % 2) + bi][:rows])
        wg = [W.get(("g_r", j), hold=j) for j in range(4)]
        for bi, (tb, col0, rows) in enumerate(vblocks):
            psg = [PS[6], PS[7]]
            for j in range(4):
                for kc in range(8):
                    T.mm(psg[j // 2][:rows, (j % 2) * 256:(j % 2 + 1) * 256], srcc(kc, col0, rows), wg[j][:, kc, :],
                         kc == 0, kc == 7, kc == 7)
            sgr = TT("sgr", 18 * K1, [128, D], F32)
            T.act(sgr(slice(0, 512))[:rows], psg[0][:rows], AF.Silu)
            T.act(sgr(slice(512, 1024))[:rows], psg[1][:rows], AF.Silu)
            if sample:
                T.dma(V(S.t[:], S().buf), V(dr["state_in"][bi].rearrange("h d v -> d h v"), Buf("g", "dram")))
                T.copy("pool", Sbf(), S())
            ordst = lambda src_v, col0=col0, rows=rows: T.copy(
                "dve", orT(slice(0, 8), slice(col0, col0 + rows)), src_v)
            retention_block(rows, lambda h: QrT(h, slice(col0, col0 + rows)),
                            lambda h: KrT(h, slice(col0, col0 + rows)), Vr(bi),
                            din8 if sample else din, rtab8 if sample else rtab, dc8 if sample else dc128,
                            True, sgr(), ordst, 0)
            if sample:
                T.dma(T.dram_v(dr["ret_state_s"][bi].rearrange("h d v -> d h v"), "o"), V(S.t[:], S().buf),
                      is_output=True)
        if not sample and t == NT - 1:
            T.dma(T.dram_v(dr["ret_state"].rearrange("h d v -> d h v"), "o"), V(S.t[:], S().buf), is_output=True)
        if not sample:
            sb_attention_prompt(t, core_blocks_before)
        else:
            sb_attention_sample()
        st_ = [[TT("mg", ((a * 4 + b) * 2) * K1, [128, TN], F32) for b in range(4)] for a in range(2)]
        for p in range(4):
            wa = W.get(("a_sb", p))
            wr_ = W.get(("a_r", p), hold=1)
            wro = W.get(("reto", p), hold=2)
            wso = W.get(("sbo", p // 2), hold=3)
            for cc in range(2):
                fc = 2 * p + cc
                b0 = (fc % 2) * 4
                for kc in range(8):
                    T.mm(PS[b0][:, :N], wa[:, kc, cc * 128:(cc + 1) * 128], src(kc), kc == 0, kc == 7, kc == 7)
                for kc in range(8):
                    T.mm(PS[b0 + 1][:, :N], wr_[:, kc, cc * 128:(cc + 1) * 128], src(kc), kc == 0, kc == 7, kc == 7)
                for kc in range(4):
                    T.mm(PS[b0 + 2][:, :N], wso[:, kc, (fc % 4) * 128:(fc % 4 + 1) * 128], osT(kc, slice(0, N)),
                         kc == 0, kc == 3, kc == 3)
                for kc in range(8):
                    T.mm(PS[b0 + 3][:, :N], wro[:, kc, cc * 128:(cc + 1) * 128], orT(kc, slice(0, N)),
                         kc == 0, kc == 7, kc == 7)
                s1, t1, s2, t2 = [x(slice(0, N)) for x in st_[fc % 2]]
                T.act(s1, PS[b0][:, :N], AF.Sigmoid)
                T.tt("dve", t1, s1, PS[b0 + 2][:, :N], ALU.mult)
                T.act(s2, PS[b0 + 1][:, :N], AF.Sigmoid)
                T.tt("dve", t2, s2, PS[b0 + 3][:, :N], ALU.mult)
                T.tt("pool", mT(fc, slice(0, N)), t1, t2, ALU.add)
        T.dma(gcur(), V(dr["g_mix_post"][0:1, :].partition_broadcast(128), Buf("g", "dram")))
        passes = [tokblocks[i:i + 2] for i in range(0, len(tokblocks), 2)]
        for pi, pb in enumerate(passes):
            bank0 = (pi % 2) * 4
            for j in range(4):
                w = W.get(("wo", j))
                for bi, (tb, col0, rows) in enumerate(pb):
                    for kc in range(8):
                        T.mm(PS[bank0 + 2 * bi + j // 2][:rows, (j % 2) * 256:(j % 2 + 1) * 256],
                             mT(kc, slice(col0, col0 + rows)), w[:, kc, :], kc == 0, kc == 7, kc == 7)
            for bi, (tb, col0, rows) in enumerate(pb):
                postnorm_residual(PS[bank0 + 2 * bi], PS[bank0 + 2 * bi + 1], tb, rows, False, 16, bi)
        ffn(2, N, tokblocks, 2, "g_ffn2_post")
        if not sample:
            T.dma(T.dram_v(dr["y_own"][t * TN:(t + 1) * TN, :].rearrange("(b p) d -> p b d", p=128), "o"),
                  V(X.t[:], X().buf), is_output=True)
        else:
            T.dma(T.dram_v(dr["y_smp"][:, :], "o"), X(0)[:32], is_output=True)

    QrT = TT("QrT", 0, [128, 4, TN], BF16)
    KrT = TT("KrT", 4 * K1, [128, 4, TN], BF16)
    SB0 = KsT.off
    Vnew = [Tile(T, f"Vnew{i}", [128, 512], F32, SB0 + i * 2048) for i in range(4)]
    KVpg = [Tile(T, f"KVpg{i}", [128, 1024], F32, SB0 + 8192 + i * 4096) for i in range(6)]
    KTp = [Tile(T, f"KTp{i}", [128, 4, 128], BF16, SB0 + 32768 + i * 1024) for i in range(3)]
    Kpb = [Tile(T, f"Kpb{i}", [128, 512], BF16, SB0 + 44032 + i * 1024) for i in range(4)]
    Vpb = [Tile(T, f"Vpb{i}", [128, 512], BF16, SB0 + 48128 + i * 1024) for i in range(8)]
    Vnb = [Tile(T, f"Vnb{i}", [128, 512], BF16, SB0 + 56320 + i * 1024) for i in range(4)]
    SR = 6
    sE = [Tile(T, f"sE{i}", [128, 64], F32, SB0 + 36864 + i * 1024) for i in range(SR)]
    sX = [Tile(T, f"sX{i}", [128, 64], F32, SB0 + 36864 + i * 1024 + 256) for i in range(SR)]
    sA = [Tile(T, f"sA{i}", [128, 64], BF16, SB0 + 36864 + i * 1024 + 512) for i in range(SR)]
    sL = [Tile(T, f"sL{i}", [128, 64], BF16, SB0 + 36864 + i * 1024 + 768) for i in range(SR)]

    def sb_attention_sample():
        ckv_ap = dr["cache_kv"]
        nb64v = V(nb64().ap.rearrange("p a b -> p (a b)"), nb64().buf)
        for s_ in range(SEQ_PER_CORE):
            units = ["new"] + list(range(NPAGES - 1, -1, -1))
            n = len(units)
            psO = PS[0]
            psT = PS[6]
            qcols = slice(s_ * 8, s_ * 8 + 8)

            def gath(u):
                if u >= n or units[u] == "new":
                    return
                j = units[u]
                col = s_ * NPAGES + j
                T.gather(KVpg[u % 6](), ckv_ap, pgi(slice(col, col + 1)))

            def cast(u):
                if u >= n or units[u] == "new":
                    return
                T.copy("act", Kpb[u % 4](), KVpg[u % 6](slice(0, 512)))
                T.copy("dve", Vpb[u % 8](), KVpg[u % 6](slice(512, 1024)))

            def S0(u):
                if units[u] == "new":
                    return
                psK = PS[2 + u % 2]
                pkb = V(psK.ap.bitcast(BF16).rearrange("p (c t) -> p c t", c=8), psK.buf)
                for c in range(4):
                    T.tr(pkb[:, c, :], Kpb[u % 4](slice(c * 128, (c + 1) * 128)), ident_bf(), c == 3)

            def S1(u):
                new = units[u] == "new"
                R_ = 8 if new else 128
                psZe = PS[1] if u % 2 == 0 else PS[4]
                psZo = PS[5 + 2 * (u % 2)]
                if not new:
                    psK = PS[2 + u % 2]
                    kt = KTp[u % 3]
                    pkb = V(psK.ap.bitcast(BF16).rearrange("p (c t) -> p c t", c=8), psK.buf)
                    T.copy("dve", kt(), pkb[:, 0:4, :])
                for h in range(8):
                    r0 = (h % 2) * 64
                    if new:
                        lhs = KsTs(h // 2, qcols)[r0:r0 + 64]
                    else:
                        lhs = kt(h // 2)[r0:r0 + 64]
                    pz = psZe if h % 2 == 0 else psZo
                    T.mm(pz[:R_, (h // 2) * 8:(h // 2 + 1) * 8], lhs, QsT(h // 2, qcols)[r0:r0 + 64],
                         True, True, h >= 6)

            def S2(u):
                new = units[u] == "new"
                R_ = 8 if new else 128
                psZe = PS[1] if u % 2 == 0 else PS[4]
                psZo = PS[5 + 2 * (u % 2)]
                i = u % SR
                sx4 = V(sX[i]().ap.rearrange("p (c r q) -> p c r q", c=4, r=2), sX[i]().buf)
                nb4 = V(nb64().ap.rearrange("p (c r) q -> p c r q", c=4), nb64().buf)
                for par, pz in ((0, psZe), (1, psZo)):
                    T.stt(sx4[:R_, :, par, :], V(pz.ap[:R_, 0:32].rearrange("p (c q) -> p c q", c=4), pz.buf),
                          -0.125, nb4[:R_, :, par, :], ALU.mult, ALU.add)
                T.act(sE[i]()[:R_], sX[i]()[:R_], AF.Exp)
                T.act(sE[i]()[:R_], sE[i]()[:R_], AF.Ln, bias=1.0)
                T.tt("dve", sL[i]()[:R_], sX[i]()[:R_], sE[i]()[:R_], ALU.subtract)
                if new:
                    T.tt("dve", sL[i]()[:R_], sL[i]()[:R_], mask8()[:R_], ALU.mult)

            def S3(u):
                new = units[u] == "new"
                R_ = 8 if new else 128
                i = u % SR
                T.mm(psT[:, 0:64], triS()[:R_, :], sL[i]()[:R_], u == 0, True, True)
                T.tt("dve", sX[i]()[:R_], psT[:R_, 0:64], sE[i]()[:R_], ALU.subtract)

            def S4(u):
                new = units[u] == "new"
                R_ = 8 if new else 128
                i = u % SR
                if u < n - 1:
                    T.mm(psT[:, 0:64], triC()[:R_, :], sL[i]()[:R_], False, True, True)
                T.act(sA[i]()[:R_], sX[i]()[:R_], AF.Exp)
                if new:
                    T.tt("dve", sA[i]()[:R_], sA[i]()[:R_], mask8()[:R_], ALU.mult)
                vt = Vnb[s_] if new else Vpb[u % 8]
                for h in range(8):
                    r0 = (h % 2) * 64
                    T.mm(psO[r0:r0 + 64, h * 8:(h + 1) * 8], vt(slice(h * 64, (h + 1) * 64))[:R_],
                         sA[i](slice(h * 8, (h + 1) * 8))[:R_], u == 0 and h < 2, u == n - 1, h == 7)

            gath(1)
            gath(2)
            gath(3)
            for step in range(n + 4):
                gath(step + 4)
                cast(step + 1)
                if step < n:
                    S0(step)
                if 0 <= step - 1 < n:
                    S1(step - 1)
                if 0 <= step - 2 < n:
                    S2(step - 2)
                if 0 <= step - 3 < n:
                    S3(step - 3)
                if 0 <= step - 4 < n:
                    S4(step - 4)
            po = V(psO.ap[:, 0:64].rearrange("p (c r q) -> p c r q", c=4, r=2), psO.buf)
            T.copy("dve", osT(slice(0, 4), qcols)[0:64], po[0:64, :, 0, :])
            T.copy("act", osT(slice(0, 4), qcols)[64:128], po[64:128, :, 1, :])

    def run(Tdry):
        T.dry = Tdry
        W.i = 0
        W.loaded = 0
        for t in range(min(NT, stage)):
            prefix_tile(t)
        for t in range(min(NT, stage - 4)):
            own_tile(t, 16)
        if stage >= 9:
            own_tile(0, 0, N=32, tokblocks=[(0, 0, 32)], sample=True)

    run(True)
    W.plan = list(W.rec)
    small_i[0] = 0
    run(False)
    T.finish()
    return nc, T


_CACHE = {}
STAGE = 9


def kernel(**inputs):
    inputs = {k: np.asarray(v) for k, v in inputs.items()}
    if "nc" not in _CACHE:
        _CACHE["nc"] = build(STAGE)[0]
    nc = _CACHE["nc"]
    consts = host_constants()
    xp = inputs["x_prompt"]
    xs = inputs["x_sample"]
    ckv = None
    if STAGE >= 9:
        ckv = np.concatenate([np.asarray(inputs["cache_k"][0]).reshape(NPOOL * PAGE, 512),
                              np.asarray(inputs["cache_v"][0]).reshape(NPOOL * PAGE, 512)], axis=1)
    pos_first = np.arange(0, HALF)
    cos_s, sin_s = rope_tables(np.tile(8192 + np.arange(DEC), SEQ_PER_CORE))
    in_maps = []
    for c in range(8):
        b, half = c // 2, c % 2
        m = {}
        m["x_own"] = np.ascontiguousarray(xp[b, half * HALF:(half + 1) * HALF])
        m["x_pre"] = np.ascontiguousarray(xp[b, 0:HALF])
        m["x_smp"] = np.ascontiguousarray(xs[4 * c:4 * c + 4].reshape(32, D))
        if STAGE >= 9:
            m["cache_kv"] = ckv
        m["state_in"] = np.ascontiguousarray(inputs["state_ret"][0, 4 * c:4 * c + 4])
        m["ptab"] = np.ascontiguousarray(inputs["page_table"][4 * c:4 * c + 4]).astype(np.int32)
        m["flag"] = np.full((128, 1), float(half), np.float32)
        for k in ("g_ffn1_pre", "w_ffn1_gu", "w_ffn1_down", "g_ffn1_post", "g_mix_pre", "w_in", "sb_bias", "ret_gn_g",
                  "w_sb_out", "w_ret_out", "w_o", "g_mix_post", "g_ffn2_pre", "w_ffn2_gu", "w_ffn2_down",
                  "g_ffn2_post"):
            m[k] = inputs[k]
        m.update(consts)
        co, so = rope_tables(half * HALF + pos_first)
        cp, sp_ = rope_tables(pos_first)
        m["rope_cos_own"], m["rope_sin_own"] = co, so
        m["rope_cos_pre"], m["rope_sin_pre"] = cp, sp_
        m["rope_cos_smp"], m["rope_sin_smp"] = cos_s, sin_s
        in_maps.append(m)
    res = run_bass_kernel_spmd(nc, in_maps, core_ids=list(range(8)))
    r = res.results
    y_prompt = np.stack([np.concatenate([r[2 * b]["y_own"], r[2 * b + 1]["y_own"]], 0) for b in range(4)], 0)
    y_sample = np.concatenate([r[c]["y_smp"].reshape(4, DEC, D) for c in range(8)], 0)
    k_rows = np.stack([np.concatenate([r[2 * b]["k_rows"], r[2 * b + 1]["k_rows"]], 0) for b in range(4)], 0)
    v_rows = np.stack([np.concatenate([r[2 * b]["v_rows"], r[2 * b + 1]["v_rows"]], 0) for b in range(4)], 0)
    k_rows = k_rows.reshape(1, 4, SEQ, SB_H, 64)
    v_rows = v_rows.reshape(1, 4, SEQ, SB_H, 64)
    ret_p = np.stack([r[2 * b + 1]["ret_state"] for b in range(4)], 0)[None]
    ks = np.concatenate([r[c]["ks_rows"].reshape(4, DEC, SB_H, 64) for c in range(8)], 0)[None]
    vs = np.concatenate([r[c]["vs_rows"].reshape(4, DEC, SB_H, 64) for c in range(8)], 0)[None]
    ret_s = np.concatenate([r[c]["ret_state_s"] for c in range(8)], 0)[None]
    f = lambda a: np.ascontiguousarray(a, dtype=np.float32)
    return (f(y_prompt), f(y_sample), f(k_rows), f(v_rows), f(ret_p), f(ks), f(vs), f(ret_s))
```

```python
import numpy as np
import ml_dtypes
import concourse.bass as bass
import concourse.mybir as mybir
from concourse.bass_utils import run_bass_kernel_spmd

F32 = mybir.dt.float32
BF16 = mybir.dt.bfloat16
I32 = mybir.dt.int32
AF = mybir.ActivationFunctionType
ALU = mybir.AluOpType

D = 1024
DFF = 2816
NCH = 22
SEQ = 4096
HALF = 2048
TN = 512
NT = HALF // TN
SB_H = 8
RET_H = 4
PAGE = 128
NPAGES = 64
NPOOL = 2560
SEQ_PER_CORE = 4
DEC = 8
EPS = 1e-6
NSLOT = 6
SLOT_E = 2048

W_IN_COLS = dict(q_sb=0, k_sb=512, v_sb=1024, q_r=1536, k_r=2048, v_r=2560, g_r=3584, a_sb=4608, a_r=5632)


class Sem:
    __slots__ = ("h", "count")

    def __init__(self, h):
        self.h = h
        self.count = 0


class Eng:
    def __init__(self, name, h, sem):
        self.name = name
        self.h = h
        self.sem = sem
        self.waited = {}


class Buf:
    __slots__ = ("name", "lo", "hi", "lastw", "reads", "dsem", "ovl", "ver", "space")

    def __init__(self, name, space, lo=0, hi=0):
        self.name = name
        self.space = space
        self.lo = lo
        self.hi = hi
        self.lastw = None
        self.reads = {}
        self.dsem = None
        self.ovl = None
        self.ver = -1


class V:
    __slots__ = ("ap", "buf")

    def __init__(self, ap, buf):
        self.ap = ap
        self.buf = buf

    def __getitem__(self, idx):
        return V(self.ap[idx], self.buf)

    def re(self, pat, **kw):
        return V(self.ap.rearrange(pat, **kw), self.buf)

    def bc(self, dt):
        return V(self.ap.bitcast(dt), self.buf)


class Tile:
    def __init__(self, T, name, shape, dt, off):
        self.T = T
        self.name = name
        self.shape = list(shape)
        self.isz = 2 if dt == BF16 else 4
        self.off = off
        self.t = T.nc.alloc_sbuf_tensor_at(name, list(shape), dt, offset=off)
        self.free = self.shape[1:]
        self.strides = []
        s = 1
        for d in reversed(self.free):
            self.strides.insert(0, s)
            s *= d
        self.nbytes = s * self.isz
        self.bufs = {}

    def __call__(self, *idx):
        lo = 0
        hi = 0
        full = []
        for k, d in enumerate(self.free):
            if k < len(idx):
                i = idx[k]
                if isinstance(i, int):
                    a, b = i, i + 1
                    full.append(i)
                else:
                    a = 0 if i.start is None else i.start
                    b = d if i.stop is None else i.stop
                    full.append(slice(a, b))
            else:
                a, b = 0, d
                full.append(slice(0, d))
            lo += a * self.strides[k]
            hi += (b - 1) * self.strides[k]
        hi += 1
        key = (lo, hi)
        b = self.bufs.get(key)
        if b is None:
            b = Buf(f"{self.name}{key}", "sb", self.off + lo * self.isz, self.off + hi * self.isz)
            self.bufs[key] = b
            self.T.sb_bufs.append(b)
            self.T.sb_ver += 1
        ap = self.t[tuple([slice(None)] + full)]
        return V(ap, b)


class Tracker:
    def __init__(self, nc):
        self.nc = nc
        self.dry = False
        self.sb_bufs = []
        self.sb_ver = 0
        self.all_sems = []
        mk = lambda n: self.new_sem(n)
        self.PE = Eng("pe", nc.tensor, mk("s_pe"))
        self.ACT = Eng("act", nc.scalar, mk("s_act"))
        self.DVE = Eng("dve", nc.vector, mk("s_dve"))
        self.POOL = Eng("pool", nc.gpsimd, mk("s_pool"))
        self.SP = Eng("sp", nc.sync, None)
        self.engs = [self.PE, self.ACT, self.DVE, self.POOL, self.SP]
        self.out_stamps = []
        self.cursor = (nc.sbuf_base + 63) // 64 * 64
        self.top = nc.sbuf_top
        self.pe_open = False
        self.ninst = 0

    def new_sem(self, name):
        s = Sem(self.nc.alloc_semaphore(name))
        self.all_sems.append(s)
        return s

    def alloc(self, name, shape, dt, at=None):
        isz = 2 if dt == BF16 else 4
        nb = int(np.prod(shape[1:])) * isz
        nb = (nb + 63) // 64 * 64
        if at is None:
            at = self.cursor
            self.cursor += nb
            assert self.cursor <= self.top, f"SBUF overflow at {name}: {self.cursor} > {self.top}"
        return Tile(self, name, shape, dt, at)

    def dram_v(self, ap, name="d"):
        return V(ap, Buf(name, "dram"))

    def _ovl(self, b):
        if b.space != "sb":
            return (b,)
        if b.ver != self.sb_ver:
            b.ovl = [o for o in self.sb_bufs if o.lo < b.hi and b.lo < o.hi]
            b.ver = self.sb_ver
        return b.ovl

    def _deps(self, eng, reads, writes):
        need = {}
        pe_sem = self.PE.sem
        is_pe = eng is self.PE
        for v in reads:
            for b in self._ovl(v.buf):
                st = b.lastw
                if st is not None:
                    if not (is_pe and st[0] is pe_sem) and need.get(st[0], 0) < st[1]:
                        need[st[0]] = st[1]
                if b.space == "psum":
                    for sem, val in b.reads.items():
                        if sem is not eng.sem and need.get(sem, 0) < val:
                            need[sem] = val
        for v in writes:
            for b in self._ovl(v.buf):
                st = b.lastw
                if st is not None:
                    if not (is_pe and st[0] is pe_sem) and need.get(st[0], 0) < st[1]:
                        need[st[0]] = st[1]
                for sem, val in b.reads.items():
                    if not (is_pe and sem is pe_sem) and need.get(sem, 0) < val:
                        need[sem] = val
        for sem, val in need.items():
            if eng.waited.get(sem, 0) < val:
                eng.h.wait_ge(sem.h, val)
                eng.waited[sem] = val
                self.ninst += 1

    def _mark(self, stamp, reads, writes):
        sem, val = stamp
        for v in writes:
            b = v.buf
            b.lastw = stamp
            b.reads = {}
        for v in reads:
            b = v.buf
            if b.reads.get(sem, 0) < val:
                b.reads[sem] = val

    def _emit(self, eng, fn, reads, writes):
        if self.dry:
            return
        reads = [r for r in reads if isinstance(r, V)]
        self._deps(eng, reads, writes)
        inst = fn()
        eng.sem.count += 1
        inst.then_inc(eng.sem.h, 1)
        self.ninst += 1
        self._mark((eng.sem, eng.sem.count), reads, writes)

    def barrier(self):
        if self.dry:
            return
        for e in self.engs:
            for s in self.all_sems:
                if s.count > 0 and e.waited.get(s, 0) < s.count:
                    e.h.wait_ge(s.h, s.count)
                    e.waited[s] = s.count

    def mm(self, out, lhsT, rhs, start, stop, last):
        if self.dry:
            return
        PE = self.PE
        stamp = (PE.sem, PE.sem.count + 1)
        self._deps(PE, [lhsT, rhs], [out])
        inst = self.nc.tensor.matmul(out.ap, lhsT=lhsT.ap, rhs=rhs.ap, start=start, stop=stop,
                                     skip_group_check=True)
        self.ninst += 1
        self._mark(stamp, [lhsT, rhs], [out])
        self.pe_open = True
        if last:
            inst.then_inc(PE.sem.h, 1)
            PE.sem.count += 1
            self.pe_open = False

    def tr(self, out, in_, ident, last):
        if self.dry:
            return
        PE = self.PE
        stamp = (PE.sem, PE.sem.count + 1)
        self._deps(PE, [in_, ident], [out])
        inst = self.nc.tensor.transpose(out=out.ap, in_=in_.ap, identity=ident.ap)
        self.ninst += 1
        self._mark(stamp, [in_, ident], [out])
        self.pe_open = True
        if last:
            inst.then_inc(PE.sem.h, 1)
            PE.sem.count += 1
            self.pe_open = False

    def act(self, out, in_, func, bias=None, scale=None, accum=None):
        kw = {}
        if bias is not None:
            kw["bias"] = bias.ap if isinstance(bias, V) else bias
        if scale is not None:
            kw["scale"] = scale.ap if isinstance(scale, V) else scale
        writes = [out]
        if accum is not None:
            kw["accum_out"] = accum.ap
            writes.append(accum)
        self._emit(self.ACT, lambda: self.nc.scalar.activation(out=out.ap, in_=in_.ap, func=func, **kw),
                   [in_, bias, scale], writes)

    def _veng(self, e):
        return (self.DVE, self.nc.vector) if e == "dve" else (self.POOL, self.nc.gpsimd)

    def tt(self, e, out, in0, in1, op):
        E, h = self._veng(e)
        self._emit(E, lambda: h.tensor_tensor(out=out.ap, in0=in0.ap, in1=in1.ap, op=op), [in0, in1], [out])

    def ts(self, e, out, in0, s1, s2, op0, op1=None):
        E, h = self._veng(e)
        a1 = s1.ap if isinstance(s1, V) else s1
        a2 = s2.ap if isinstance(s2, V) else s2
        if op1 is None:
            fn = lambda: h.tensor_scalar(out=out.ap, in0=in0.ap, scalar1=a1, scalar2=None, op0=op0)
        else:
            fn = lambda: h.tensor_scalar(out=out.ap, in0=in0.ap, scalar1=a1, scalar2=a2, op0=op0, op1=op1)
        self._emit(E, fn, [in0, s1, s2], [out])

    def stt(self, out, in0, scalar, in1, op0, op1):
        sc = scalar.ap if isinstance(scalar, V) else scalar
        self._emit(self.DVE, lambda: self.nc.vector.scalar_tensor_tensor(
            out=out.ap, in0=in0.ap, scalar=sc, in1=in1.ap, op0=op0, op1=op1), [in0, scalar, in1], [out])

    def copy(self, e, out, in_):
        if e == "act":
            self.act(out, in_, AF.Copy)
            return
        E, h = self._veng(e)
        self._emit(E, lambda: h.tensor_copy(out=out.ap, in_=in_.ap), [in_], [out])

    def recip(self, out, in_):
        self._emit(self.DVE, lambda: self.nc.vector.reciprocal(out=out.ap, in_=in_.ap), [in_], [out])

    def memset(self, out, val):
        self._emit(self.DVE, lambda: self.nc.vector.memset(out.ap, val), [], [out])

    def _dsem(self, b):
        if b.dsem is None:
            b.dsem = self.new_sem(f"s_dma{len(self.all_sems)}")
        return b.dsem

    def dma(self, out, in_, q=None, is_output=False, nonctg=False):
        if self.dry:
            return
        q = q or self.SP
        side = out if out.buf.space == "sb" else in_
        sem = self._dsem(side.buf)
        self._deps(q, [in_], [out])
        if nonctg:
            with self.nc.allow_non_contiguous_dma(reason="tiny strided constant load"):
                inst = q.h.dma_start(out=out.ap, in_=in_.ap)
        else:
            inst = q.h.dma_start(out=out.ap, in_=in_.ap)
        inst.then_inc(sem.h, 16)
        sem.count += 16
        self.ninst += 1
        st = (sem, sem.count)
        self._mark(st, [in_], [out])
        if is_output:
            self.out_stamps.append(st)

    def gather(self, out, table_ap, idx):
        if self.dry:
            return
        q = self.POOL
        sem = self._dsem(out.buf)
        self._deps(q, [idx], [out])
        inst = self.nc.gpsimd.indirect_dma_start(
            out=out.ap, out_offset=None, in_=table_ap,
            in_offset=bass.IndirectOffsetOnAxis(ap=idx.ap, axis=0))
        inst.then_inc(sem.h, 16)
        sem.count += 16
        self.ninst += 1
        self._mark((sem, sem.count), [idx], [out])

    def finish(self):
        if self.dry:
            return
        assert not self.pe_open
        sp = self.SP
        for sem, val in self.out_stamps:
            if sp.waited.get(sem, 0) < val:
                sp.h.wait_ge(sem.h, val)
                sp.waited[sem] = val


def weight_blocks(dr):
    blocks = []

    def pview(W2d, r0, kc):
        return W2d[r0:r0 + kc * 128, :].rearrange("(k p) n -> p k n", p=128)

    for f in (1, 2):
        gu = dr[f"w_ffn{f}_gu"][0]
        dn = dr[f"w_ffn{f}_down"][0]
        gv = pview(gu, 0, 8)
        for c in range(NCH):
            blocks.append(((f"f{f}gu", c), 8, 256,
                           [(gv[:, :, c * 128:(c + 1) * 128], 0, 128),
                            (gv[:, :, DFF + c * 128:DFF + (c + 1) * 128], 128, 128)]))
        for g in range(NCH // 2):
            dv = pview(dn, g * 256, 2)
            blocks.append(((f"f{f}dn", g), 2, 1024, [(dv[:, :, :], 0, 1024)]))
    win = pview(dr["w_in"][0], 0, 8)
    for name, nblk in (("q_sb", 2), ("k_sb", 2), ("v_sb", 2), ("q_r", 2), ("k_r", 2), ("v_r", 4), ("g_r", 4),
                       ("a_sb", 4), ("a_r", 4)):
        c0 = W_IN_COLS[name]
        for j in range(nblk):
            blocks.append(((name, j), 8, 256, [(win[:, :, c0 + j * 256:c0 + (j + 1) * 256], 0, 256)]))
    for name in ("q_r", "k_r"):
        c0 = W_IN_COLS[name]
        for j in range(2):
            pcs = []
            for cc in range(2):
                h0 = c0 + (2 * j + cc) * 128
                pcs.append((win[:, :, h0 + 64:h0 + 128], cc * 128, 64))
                pcs.append((win[:, :, h0:h0 + 64], cc * 128 + 64, 64))
            blocks.append(((name + "_sw", j), 8, 256, pcs))
    sbo = pview(dr["w_sb_out"][0], 0, 4)
    for j in range(2):
        blocks.append((("sbo", j), 4, 512, [(sbo[:, :, j * 512:(j + 1) * 512], 0, 512)]))
    reto = pview(dr["w_ret_out"][0], 0, 8)
    wo = pview(dr["w_o"][0], 0, 8)
    for j in range(4):
        blocks.append((("reto", j), 8, 256, [(reto[:, :, j * 256:(j + 1) * 256], 0, 256)]))
        blocks.append((("wo", j), 8, 256, [(wo[:, :, j * 256:(j + 1) * 256], 0, 256)]))
    return blocks


class WStream:
    def __init__(self, T, slots, scr_views, plan):
        self.T = T
        self.slots = slots
        self.scr = scr_views
        self.plan = plan
        self.rec = []
        self.i = 0
        self.loaded = 0

    def get(self, key, hold=0):
        T = self.T
        if T.dry:
            self.rec.append(key)
            _, kc, bw = self.scr[key]
            sl = self.slots[0]
            return V(sl.t[:, 0:kc * bw].rearrange("p (k n) -> p k n", k=kc), sl().buf)
        i = self.i
        assert self.plan[i] == key, (i, self.plan[i], key)
        upto = min(len(self.plan) - 1, i + NSLOT - 1 - hold)
        upto = max(upto, i)
        while self.loaded <= upto:
            j = self.loaded
            k = self.plan[j]
            src, kc, bw = self.scr[k]
            sl = self.slots[j % NSLOT]
            T.dma(sl(slice(0, kc * bw)), src)
            self.loaded += 1
        self.i += 1
        _, kc, bw = self.scr[key]
        sl = self.slots[i % NSLOT]
        v = sl(slice(0, kc * bw))
        return V(v.ap.rearrange("p (k n) -> p k n", k=kc), v.buf)


def gammas():
    h = np.arange(RET_H, dtype=np.float64)
    return np.exp(np.log1p(-np.exp2(-5.0 - h)))


def host_constants():
    c = {}
    c["ident_bf"] = np.eye(128, dtype=np.float32).astype(ml_dtypes.bfloat16)
    c["ident_f"] = np.eye(128, dtype=np.float32)
    j = np.arange(128)[:, None]
    k = np.arange(128)[None, :]
    c["triS"] = (j > k).astype(np.float32).astype(ml_dtypes.bfloat16)
    c["triC"] = (j <= k).astype(np.float32).astype(ml_dtypes.bfloat16)
    q = np.arange(512)[None, :]
    m = np.stack([((np.arange(128)[:, None] + 128 * o) < q) for o in range(4)], axis=1)
    c["masks"] = m.astype(np.float32).astype(ml_dtypes.bfloat16)
    m8 = np.zeros((128, 64), np.float32)
    for hh in range(8):
        for qq in range(8):
            m8[:8, hh * 8 + qq] = (np.arange(8) < qq)
    c["mask8"] = m8
    g = gammas()
    s = np.arange(128, dtype=np.float64)
    sc = 128.0 ** -0.5
    din = np.zeros((128, 4, 128), np.float64)
    for hh in range(4):
        din[:, hh, :] = (g[hh] ** (-(s[:, None] + 1.0))) * (s[None, :] >= s[:, None]) * sc
    c["din"] = din.astype(np.float32)
    din8 = np.zeros((128, 4, 8), np.float64)
    s8 = np.arange(8, dtype=np.float64)
    for hh in range(4):
        din8[:8, hh, :] = (g[hh] ** (-(s8[:, None] + 1.0))) * (s8[None, :] >= s8[:, None]) * sc
    c["din8"] = din8.astype(np.float32)
    rt = np.zeros((128, 12), np.float64)
    rt8 = np.zeros((128, 12), np.float64)
    for hh in range(4):
        rt[:, hh] = g[hh] ** (127.0 - s) * sc
        rt[:, 4 + hh] = g[hh] ** (s + 1.0)
        rt[:, 8 + hh] = g[hh] ** (2.0 * (s + 1.0)) / 256.0
        rt8[:8, hh] = g[hh] ** (7.0 - s8) * sc
        rt8[:8, 4 + hh] = g[hh] ** (s8 + 1.0)
        rt8[:8, 8 + hh] = g[hh] ** (2.0 * (s8 + 1.0)) / 256.0
    c["rtab"] = rt.astype(np.float32)
    c["rtab8"] = rt8.astype(np.float32)
    c["pidx"] = np.arange(128, dtype=np.float32)[:, None]
    return c


def rope_tables(pos):
    half = 64
    freq = (10000.0 ** (-np.arange(half, dtype=np.float32) / half)).astype(np.float32)
    ang = pos.astype(np.float32)[None, :] * freq[:, None]
    cos = np.cos(ang).astype(np.float32)
    sin = np.sin(ang).astype(np.float32)
    return (np.concatenate([cos, cos], 0).astype(np.float32),
            np.concatenate([-sin, sin], 0).astype(np.float32))


_DRAM_IN = [
    ("x_own", [HALF, D], F32), ("x_pre", [HALF, D], F32), ("x_smp", [SEQ_PER_CORE * DEC, D], F32),
    ("cache_kv", [NPOOL * PAGE, 1024], F32),
    ("state_in", [SEQ_PER_CORE, RET_H, 128, 256], F32), ("ptab", [SEQ_PER_CORE, NPAGES], I32),
    ("flag", [128, 1], F32),
    ("g_ffn1_pre", [1, D], F32), ("w_ffn1_gu", [1, D, 2 * DFF], F32), ("w_ffn1_down", [1, DFF, D], F32),
    ("g_ffn1_post", [1, D], F32), ("g_mix_pre", [1, D], F32), ("w_in", [1, D, 6656], F32),
    ("sb_bias", [1, SB_H], F32), ("ret_gn_g", [1, D], F32), ("w_sb_out", [1, 512, D], F32),
    ("w_ret_out", [1, D, D], F32), ("w_o", [1, D, D], F32), ("g_mix_post", [1, D], F32),
    ("g_ffn2_pre", [1, D], F32), ("w_ffn2_gu", [1, D, 2 * DFF], F32), ("w_ffn2_down", [1, DFF, D], F32),
    ("g_ffn2_post", [1, D], F32),
    ("ident_bf", [128, 128], BF16), ("ident_f", [128, 128], F32), ("triS", [128, 128], BF16),
    ("triC", [128, 128], BF16), ("masks", [128, 4, 512], BF16), ("mask8", [128, 64], F32),
    ("din", [128, 4, 128], F32), ("din8", [128, 4, 8], F32), ("rtab", [128, 12], F32), ("rtab8", [128, 12], F32),
    ("pidx", [128, 1], F32),
    ("rope_cos_own", [128, HALF], F32), ("rope_sin_own", [128, HALF], F32),
    ("rope_cos_pre", [128, HALF], F32), ("rope_sin_pre", [128, HALF], F32),
    ("rope_cos_smp", [128, 32], F32), ("rope_sin_smp", [128, 32], F32),
]
_DRAM_OUT = [
    ("y_own", [HALF, D], F32), ("y_smp", [32, D], F32), ("k_rows", [HALF, 512], F32), ("v_rows", [HALF, 512], F32),
    ("ret_state", [RET_H, 128, 256], F32), ("ks_rows", [32, 512], F32), ("vs_rows", [32, 512], F32),
    ("ret_state_s", [SEQ_PER_CORE, RET_H, 128, 256], F32),
]


def build(stage=99):
    nc = bass.Bass("TRN2", target_bir_lowering=False)
    dr = {}
    for name, shape, dt in _DRAM_IN:
        if name == "cache_kv" and stage < 9:
            continue
        dr[name] = nc.dram_tensor(name, shape, dt, kind="ExternalInput").ap()
    for name, shape, dt in _DRAM_OUT:
        dr[name] = nc.dram_tensor(name, shape, dt, kind="ExternalOutput").ap()
    T = Tracker(nc)
    blocks = weight_blocks(dr)
    tot = sum(128 * kc * bw for _, kc, bw, _ in blocks)
    scr = nc.dram_tensor("wscr", [tot], BF16, kind="Internal").ap()
    scr_views = {}
    off = 0
    for key, kc, bw, _ in blocks:
        n = 128 * kc * bw
        scr_views[key] = (T.dram_v(scr[off:off + n].rearrange("(p m) -> p m", p=128), f"scr{key}"), kc, bw)
        off += n

    PSb = [nc.alloc_psum_tensor(f"ps{i}", [128, 512], F32) for i in range(8)]
    PS = [V(PSb[i][:], Buf(f"ps{i}", "psum")) for i in range(8)]
    dv = lambda name: T.dram_v(dr[name], name)

    base0 = T.cursor
    stg = [T.alloc(f"stg{i}", [128, SLOT_E], F32) for i in range(3)]
    wb = [T.alloc(f"wb{i}", [128, SLOT_E], BF16) for i in range(3)]
    cast_engs = ["act", "dve", "dve"]
    def p0_load(bi):
        key, kc, bw, pieces = blocks[bi]
        sv = stg[bi % 3](slice(0, kc * bw))
        s3 = V(sv.ap.rearrange("p (k n) -> p k n", k=kc), sv.buf)
        for src, c0, wd in pieces:
            T.dma(s3[:, :, c0:c0 + wd], T.dram_v(src, "w"))

    p0_load(0)
    p0_load(1)
    for bi, (key, kc, bw, pieces) in enumerate(blocks):
        if bi + 2 < len(blocks):
            p0_load(bi + 2)
        sv = stg[bi % 3](slice(0, kc * bw))
        wv = wb[bi % 3](slice(0, kc * bw))
        T.copy(cast_engs[bi % 3], wv, sv)
        T.dma(scr_views[key][0], wv)
    T.barrier()

    T.cursor = base0
    A = T.alloc
    X = A("X", [128, 4, D], F32)
    xnb = [A(f"xnb{i}", [128, D], BF16) for i in range(2)]
    xT = A("xT", [128, 8, TN], BF16)
    uT = A("uT", [128, 8, TN], BF16)
    QsT = A("QsT", [128, 4, TN], BF16, at=xT.off)
    osT = A("osT", [128, 4, TN], BF16, at=xT.off + 4 * TN * 2)
    hT = A("hT", [128, NCH, TN], BF16)
    T.cursor += 24 * 1024 - hT.nbytes
    orT = A("orT", [128, 8, TN], BF16, at=hT.off)
    mT = A("mT", [128, 8, TN], BF16, at=hT.off + 8192)
    Vr = A("Vr", [128, 4, D], BF16, at=hT.off + 16384)
    ring = [A(f"wr{i}", [128, SLOT_E], BF16) for i in range(NSLOT)]
    KsT = A("KsT", [128, 4, SEQ], BF16)
    Vsb = A("Vsb", [128, 32, 512], BF16)
    S = A("S", [128, 4, 256], F32)
    Sbf = A("Sbf", [128, 4, 256], BF16)
    Kdec = [A(f"Kdec{i}", [128, 4, 128], BF16) for i in range(2)]
    ropec = A("ropec", [128, TN], F32)
    ropes = A("ropes", [128, TN], F32)
    gcur = A("gcur", [128, D], F32)
    gng = A("gng", [128, D], F32)
    gpT = A("gpT", [128, 3, 8], F32)
    ident_bf = A("ident_bf", [128, 128], BF16)
    ident_f = A("ident_f", [128, 128], F32)
    triS = A("triS", [128, 128], BF16)
    triC = A("triC", [128, 128], BF16)
    triSP = A("triSP", [128, 128], BF16)
    triCP = A("triCP", [128, 128], BF16)
    masks = A("masks", [128, 4, 512], BF16)
    mask8 = A("mask8", [128, 64], F32)
    din = A("din", [128, 4, 128], F32)
    din8 = A("din8", [128, 4, 8], F32)
    rtab = A("rtab", [128, 12], F32)
    rtab8 = A("rtab8", [128, 12], F32)
    pidx = A("pidx", [128, 1], F32)
    flag = A("flag", [128, 1], F32)
    nb = A("nb", [128, 8], F32)
    nb64 = A("nb64", [128, 8, 8], F32)
    sbb = A("sbb", [128, 8], F32)
    small = A("small", [128, 64], F32)
    ptt = A("ptt", [128, SEQ_PER_CORE * NPAGES], I32)
    pgi = A("pgi", [128, SEQ_PER_CORE * NPAGES], I32)
    KsTs = A("KsTs", [128, 4, 32], BF16)
    TEMP0 = T.cursor
    TEMP_SZ = T.top - TEMP0
    assert TEMP_SZ >= 26 * 1024, TEMP_SZ

    _tt = {}

    def TT(name, off, shape, dt):
        key = (name, off, tuple(shape), dt == BF16)
        t = _tt.get(key)
        if t is None:
            isz = 2 if dt == BF16 else 4
            nbts = int(np.prod(shape[1:])) * isz
            assert TEMP0 + off + nbts <= T.top, ("TEMP overflow", name, TEMP0 + off + nbts - T.top)
            t = Tile(T, f"tmp_{name}_{off}", shape, dt, TEMP0 + off)
            _tt[key] = t
        return t
    K1 = 1024

    for tl, name in ((ident_bf, "ident_bf"), (ident_f, "ident_f"), (triS, "triS"), (triC, "triC"), (masks, "masks"),
                     (mask8, "mask8"), (din, "din"), (din8, "din8"), (rtab, "rtab"), (rtab8, "rtab8"),
                     (pidx, "pidx"), (flag, "flag")):
        T.dma(tl(), dv(name))
    T.dma(gng(), V(dr["ret_gn_g"][0:1, :].partition_broadcast(128), Buf("g", "dram")))
    for i, name in enumerate(("g_ffn1_pre", "g_mix_pre", "g_ffn2_pre")):
        T.dma(gpT(i), V(dr[name].rearrange("o (c p) -> p (o c)", p=128), Buf("g", "dram")), nonctg=True)
    T.dma(sbb(), V(dr["sb_bias"][0:1, :].partition_broadcast(128), Buf("g", "dram")))
    T.ts("dve", nb(), sbb(), -1.0, None, ALU.mult)
    T.copy("dve", nb64(), V(nb().ap.unsqueeze(2).to_broadcast([128, 8, 8]), nb().buf))
    T.ts("dve", triSP(), triS(), flag(), None, ALU.mult)
    T.ts("dve", triCP(), triC(), flag(), None, ALU.mult)
    T.dma(ptt(), V(dr["ptab"].rearrange("s j -> (s j)").rearrange("(o n) -> o n", o=1).partition_broadcast(128),
                   Buf("g", "dram")))
    T.ts("dve", pgi(), ptt(), 128.0, pidx(), ALU.mult, ALU.add)
    T.memset(S(), 0.0)
    T.memset(Sbf(), 0.0)

    W = WStream(T, ring, scr_views, None)
    small_i = [0]

    def sm(n=1):
        i = small_i[0]
        if i + n > 64:
            i = 0
        small_i[0] = i + n
        return small(slice(i, i + n))

    def rstd_of(ssv, rows, scale, n=1):
        rt = sm(n)
        T.act(rt[:rows], ssv[:rows], AF.Sqrt, bias=EPS, scale=scale)
        rs = sm(n)
        T.recip(rs[:rows], rt[:rows])
        return rs

    def norm_transpose(blocks_, gidx, outT, psbank, joff):
        junk = TT("junkN", joff * K1, [128, D], BF16)
        for bi, (tb, col0, rows) in enumerate(blocks_):
            ss = sm()
            T.act(junk()[:rows], X(tb)[:rows], AF.Square, accum=ss[:rows])
            rs = rstd_of(ss, rows, 1.0 / D)
            xb = xnb[bi % 2]
            T.ts("dve", xb()[:rows], X(tb)[:rows], rs[:rows], None, ALU.mult)
            ps = PS[psbank + bi % 2]
            psb = V(ps.ap.bitcast(BF16).rearrange("p (c t) -> p c t", c=8), ps.buf)
            for c in range(8):
                T.tr(psb[:, c, 0:rows], xb(slice(c * 128, (c + 1) * 128))[:rows], ident_bf()[:rows, :rows],
                     last=(c == 7))
            ov = outT(slice(0, 8), slice(col0, col0 + rows))
            T.tt("dve", ov, psb[:, :, 0:rows],
                 V(gpT(gidx).ap.unsqueeze(2).to_broadcast([128, 8, rows]), gpT(gidx).buf), ALU.mult)

    def postnorm_residual(psA, psB, tb, rows, half_scale, base, slot):
        junk = TT("pjunk", base * K1, [128, 512], BF16)
        ss = sm(2)
        T.act(junk()[:rows], psA[:rows], AF.Square, accum=ss[:rows, 0:1])
        T.act(junk()[:rows], psB[:rows], AF.Square, accum=ss[:rows, 1:2])
        st = sm()
        T.tt("dve", st[:rows], ss[:rows, 0:1], ss[:rows, 1:2], ALU.add)
        rs = rstd_of(st, rows, 1.0 / D)
        t = TT("pt", (base + 2 + 4 * slot) * K1, [128, D], F32)
        T.stt(t(slice(0, 512))[:rows], psA[:rows], rs[:rows], gcur(slice(0, 512))[:rows], ALU.mult, ALU.mult)
        T.stt(t(slice(512, 1024))[:rows], psB[:rows], rs[:rows], gcur(slice(512, 1024))[:rows], ALU.mult, ALU.mult)
        if half_scale:
            T.stt(X(tb)[:rows], t()[:rows], 0.5, X(tb)[:rows], ALU.mult, ALU.add)
        else:
            T.tt("pool", X(tb)[:rows], t()[:rows], X(tb)[:rows], ALU.add)

    def ffn(f, N, tokblocks, gidx, gname):
        T.dma(gcur(), V(dr[gname][0:1, :].partition_broadcast(128), Buf("g", "dram")))
        norm_transpose(tokblocks, gidx, xT, 6, 0)
        sg = [TT("sg", (2 + 2 * i) * K1, [128, TN], F32) for i in range(2)]
        for c in range(NCH):
            w = W.get((f"f{f}gu", c))
            b0 = (c % 3) * 2
            psg, psu = PS[b0], PS[b0 + 1]
            for kc in range(8):
                T.mm(psg[:, :N], w[:, kc, 0:128], xT(kc, slice(0, N)), kc == 0, kc == 7, kc == 7)
            for kc in range(8):
                T.mm(psu[:, :N], w[:, kc, 128:256], xT(kc, slice(0, N)), kc == 0, kc == 7, kc == 7)
            s = sg[c % 2]
            T.act(s(slice(0, N)), psg[:, :N], AF.Silu)
            T.tt("dve", hT(c, slice(0, N)), s(slice(0, N)), psu[:, :N], ALU.mult)
        passes = [tokblocks[i:i + 2] for i in range(0, len(tokblocks), 2)]
        for pi, pb in enumerate(passes):
            bank0 = (pi % 2) * 4
            for g in range(NCH // 2):
                w = W.get((f"f{f}dn", g))
                for kk in range(2):
                    kc = 2 * g + kk
                    for bi, (tb, col0, rows) in enumerate(pb):
                        for j in range(2):
                            T.mm(PS[bank0 + 2 * bi + j][:rows, :], hT(kc, slice(col0, col0 + rows)),
                                 w[:, kk, j * 512:(j + 1) * 512], kc == 0, kc == NCH - 1,
                                 kk == 1 and bi == len(pb) - 1 and j == 1)
            for bi, (tb, col0, rows) in enumerate(pb):
                postnorm_residual(PS[bank0 + 2 * bi], PS[bank0 + 2 * bi + 1], tb, rows, True, 6, bi)

    def proj_fm(wname, nblk, N, src, evac, bank0=0, sw=None):
        for j in range(nblk):
            w = W.get((wname, j))
            w2 = W.get((sw, j), hold=1) if sw else None
            for cc in range(2):
                c = 2 * j + cc
                ps = PS[bank0 + (c % 2) * 2]
                for kc in range(8):
                    T.mm(ps[:, :N], w[:, kc, cc * 128:(cc + 1) * 128], src(kc), kc == 0, kc == 7, kc == 7)
                ps2 = None
                if sw:
                    ps2 = PS[bank0 + (c % 2) * 2 + 1]
                    for kc in range(8):
                        T.mm(ps2[:, :N], w2[:, kc, cc * 128:(cc + 1) * 128], src(kc), kc == 0, kc == 7, kc == 7)
                evac(c, ps, ps2)

    def proj_tm(wname, jlist, tokblocks, src_cols, banks, col_of):
        for j in jlist:
            w = W.get((wname, j))
            for bi, (tb, col0, rows) in enumerate(tokblocks):
                ps = banks(bi, j)
                c0 = col_of(j)
                for kc in range(8):
                    T.mm(ps[:rows, c0:c0 + 256], src_cols(kc, col0, rows), w[:, kc, :], kc == 0, kc == 7, kc == 7)

    def rope_evac(dst):
        def f(c, ps, ps2, N):
            t1 = TT("rope1", 18 * K1, [128, TN], F32)
            t2 = TT("rope2", 20 * K1, [128, TN], F32)
            T.tt("dve", t1(slice(0, N)), ps[:, :N], ropec(slice(0, N)), ALU.mult)
            T.tt("dve", t2(slice(0, N)), ps2[:, :N], ropes(slice(0, N)), ALU.mult)
            T.tt("pool", dst(c, slice(0, N)), t1(slice(0, N)), t2(slice(0, N)), ALU.add)
        return f

    def retention_block(rows, qv, kv, vr, din_t, rt_t, dc, do_out, sg_v, or_dst, psbase):
        C = rows
        if do_out:
            psI = PS[psbase]
            for h in range(4):
                T.mm(psI[:C, h * C:(h + 1) * C], kv(h), qv(h), True, True, h == 3)
            inT = TT("inT", 10 * K1, [128, 4, 128], BF16)
            T.tt("dve", inT(slice(0, 4), slice(0, C))[:C],
                 V(psI.ap[:C, 0:4 * C].rearrange("p (h c) -> p h c", h=4), psI.buf),
                 din_t(slice(0, 4), slice(0, C))[:C], ALU.mult)
            pso = [PS[psbase + 1], PS[psbase + 2]]
            for h in range(4):
                o = pso[h // 2][:C, (h % 2) * 256:(h % 2 + 1) * 256]
                T.mm(o, inT(h, slice(0, C))[:C], vr[:C, h * 256:(h + 1) * 256], True, False, False)
                T.mm(o, qv(h), Sbf(h), False, True, h % 2 == 1)
        kd = Kdec[0]
        psk = PS[psbase + 3]
        pskb = V(psk.ap.bitcast(BF16).rearrange("p (c t) -> p c t", c=8), psk.buf)
        for h in range(4):
            T.tr(pskb[:C, h, :], kv(h), ident_bf(), last=(h == 3))
        for h in range(4):
            T.ts("dve", kd(h)[:C], pskb[:C, h, :], rt_t(slice(h, h + 1))[:C], None, ALU.mult)
        pss = [PS[psbase + 4], PS[psbase + 5]]
        for h in range(4):
            T.mm(pss[h // 2][:, (h % 2) * 256:(h % 2 + 1) * 256], kd(h)[:C], vr[:C, h * 256:(h + 1) * 256],
                 True, True, h % 2 == 1)
        if do_out:
            junk = TT("rjunk", 11 * K1, [128, 256], BF16)
            ss = sm(4)
            for h in range(4):
                T.act(junk()[:C], pso[h // 2][:C, (h % 2) * 256:(h % 2 + 1) * 256], AF.Square,
                      accum=ss[:C, h:h + 1])
            s2 = sm(4)
            T.tt("dve", s2[:C], ss[:C], rt_t(slice(8, 12))[:C], ALU.mult)
            rs = rstd_of(s2, C, 1.0, n=4)
            fac = sm(4)
            T.tt("dve", fac[:C], rs[:C], rt_t(slice(4, 8))[:C], ALU.mult)
            tn = TT("tn", 12 * K1, [128, D], F32)
            for h in range(4):
                T.stt(tn(slice(h * 256, (h + 1) * 256))[:C], pso[h // 2][:C, (h % 2) * 256:(h % 2 + 1) * 256],
                      fac[:C, h:h + 1], gng(slice(h * 256, (h + 1) * 256))[:C], ALU.mult, ALU.mult)
            orb = TT("orb", 16 * K1, [128, D], BF16)
            T.tt("pool", orb()[:C], tn()[:C], sg_v[:C], ALU.mult)
            pst = PS[psbase]
            pstb = V(pst.ap.bitcast(BF16).rearrange("p (c t) -> p c t", c=8), pst.buf)
            for c in range(8):
                T.tr(pstb[:, c, 0:C], orb(slice(c * 128, (c + 1) * 128))[:C], ident_bf()[:C, :C], last=(c == 7))
            or_dst(pstb[:, :, 0:C])
        for h in range(4):
            T.stt(S(h), S(h), float(dc[h]), pss[h // 2][:, (h % 2) * 256:(h % 2 + 1) * 256], ALU.mult, ALU.add)
        T.copy("pool", Sbf(), S())

    g = gammas()
    dc128 = g ** 128.0
    dc8 = g ** 8.0

    aL = [Tile(T, f"aL{i}", [128, TN], BF16, mT.off + i * 1024) for i in range(6)]
    aA = [Tile(T, f"aA{i}", [128, TN], BF16, mT.off + 6144 + i * 1024) for i in range(2)]

    def sb_attention_prompt(t_own, core_blocks_before):
        R = 6
        E = [TT("aE", (2 * i) * K1, [128, TN], F32) for i in range(R)]
        Xt = [TT("aX", (12 + 2 * i) * K1, [128, TN], F32) for i in range(R)]
        L = aL
        Aa = aA
        qblk0 = core_blocks_before + t_own * 4
        kb_last = qblk0 + 3
        units = [(pair, h, kb) for pair in range(4) for kb in range(kb_last, -1, -1)
                 for h in (2 * pair, 2 * pair + 1)]
        n = len(units)

        def stA(u):
            pair, h, kb = units[u]
            r0 = (h % 2) * 64
            psz = PS[4 + u % 4]
            T.mm(psz, KsT(pair, slice(kb * 128, (kb + 1) * 128))[r0:r0 + 64],
                 QsT(pair)[r0:r0 + 64], True, True, True)
            e = E[u % R]
            T.ts("dve", Xt[u % R](), psz, -0.125, nb(slice(h, h + 1)), ALU.mult, ALU.add)
            T.act(e(), Xt[u % R](), AF.Exp)
            T.act(e(), e(), AF.Ln, bias=1.0)
            T.tt("pool", L[u % R](), Xt[u % R](), e(), ALU.subtract)
            if kb >= qblk0:
                T.tt("pool", L[u % R](), L[u % R](), masks(kb - qblk0), ALU.mult)

        def stB(u):
            pair, h, kb = units[u]
            pst = PS[2 + h % 2]
            pre = kb < 16
            T.mm(pst, (triSP if pre else triS)(), L[u % R](), kb == kb_last, True, True)
            T.tt("dve", Xt[u % R](), pst, E[u % R](), ALU.subtract)

        def stC(u):
            pair, h, kb = units[u]
            pst = PS[2 + h % 2]
            pre = kb < 16
            if kb > 0:
                T.mm(pst, (triCP if pre else triC)(), L[u % R](), False, True, True)
            a = Aa[u % 2]
            T.act(a(), Xt[u % R](), AF.Exp)
            if kb >= qblk0:
                T.tt("pool", a(), a(), masks(kb - qblk0), ALU.mult)

        def stD(u):
            pair, h, kb = units[u]
            r0 = (h % 2) * 64
            pso = PS[pair % 2]
            T.mm(pso[r0:r0 + 64, :], Vsb(kb, slice(h * 64, (h + 1) * 64)), Aa[u % 2](), kb == kb_last, kb == 0, True)
            if kb == 0 and h % 2 == 1:
                T.copy("act", osT(pair), pso)

        for step in range(n + 5):
            if step < n:
                stA(step)
            if 0 <= step - 3 < n:
                stB(step - 3)
            if 0 <= step - 4 < n:
                stC(step - 4)
            if 0 <= step - 5 < n:
                stD(step - 5)

    def load_rope(cos_name, sin_name, c0, N):
        T.dma(ropec(slice(0, N)), V(dr[cos_name][:, c0:c0 + N], Buf("g", "dram")))
        T.dma(ropes(slice(0, N)), V(dr[sin_name][:, c0:c0 + N], Buf("g", "dram")))

    FULL = [(tb, tb * 128, 128) for tb in range(4)]

    def prefix_tile(t):
        T.dma(V(X.t[:], X().buf), V(dr["x_pre"][t * TN:(t + 1) * TN, :].rearrange("(b p) d -> p b d", p=128),
                                    Buf("g", "dram")))
        load_rope("rope_cos_pre", "rope_sin_pre", t * TN, TN)
        ffn(1, TN, FULL, 0, "g_ffn1_post")
        norm_transpose(FULL, 1, uT, 6, 8)
        src = lambda kc: uT(kc)
        proj_fm("k_sb", 2, TN, src,
                lambda c, ps, ps2: T.copy("act", KsT(c, slice(t * TN, (t + 1) * TN)), ps))
        proj_tm("v_sb", [0, 1], FULL, lambda kc, col0, rows: uT(kc, slice(col0, col0 + rows)),
                lambda bi, j: PS[4 + bi], lambda j: j * 256)
        for bi in range(4):
            T.ts("dve", Vsb(t * 4 + bi), PS[4 + bi], flag(), None, ALU.mult)
        re = rope_evac(lambda c, sl: KrT(c, sl))
        proj_fm("k_r", 2, TN, src, lambda c, ps, ps2: re(c, ps, ps2, TN), sw="k_r_sw")
        for hf in range(2):
            proj_tm("v_r", [2 * hf, 2 * hf + 1], FULL, lambda kc, col0, rows: uT(kc, slice(col0, col0 + rows)),
                    lambda bi, j: PS[4 * (hf % 2) + bi], lambda j: (j % 2) * 256)
            for bi in range(4):
                T.ts("dve", Vr(bi, slice(hf * 512, (hf + 1) * 512)), PS[4 * (hf % 2) + bi], flag(), None, ALU.mult)
        for tb in range(4):
            retention_block(128, None, lambda h: KrT(h, slice(tb * 128, (tb + 1) * 128)), Vr(tb),
                            din, rtab, dc128, False, None, None, 0)

    def own_tile(t, core_blocks_before, N=TN, tokblocks=FULL, sample=False):
        if not sample:
            T.dma(V(X.t[:], X().buf), V(dr["x_own"][t * TN:(t + 1) * TN, :].rearrange("(b p) d -> p b d", p=128),
                                        Buf("g", "dram")))
            load_rope("rope_cos_own", "rope_sin_own", t * TN, TN)
        else:
            T.dma(X(0)[:32], dv("x_smp"))
            load_rope("rope_cos_smp", "rope_sin_smp", 0, 32)
        ffn(1, N, tokblocks, 0, "g_ffn1_post")
        norm_transpose(tokblocks, 1, uT, 6, 8)
        src = lambda kc: uT(kc, slice(0, N))
        srcc = lambda kc, col0, rows: uT(kc, slice(col0, col0 + rows))
        kpos = (core_blocks_before * 128 + t * TN) if not sample else 0
        proj_fm("q_sb", 2, N, src, lambda c, ps, ps2: T.copy("act", QsT(c, slice(0, N)), ps[:, :N]))
        if not sample:
            kdst = lambda c: KsT(c, slice(kpos, kpos + N))
        else:
            kdst = lambda c: KsTs(c, slice(0, N))
        for j in range(2):
            w = W.get(("k_sb", j))
            for cc in range(2):
                c = 2 * j + cc
                ps = PS[(c % 2)]
                for kc in range(8):
                    T.mm(ps[:, :N], w[:, kc, cc * 128:(cc + 1) * 128], src(kc), kc == 0, kc == 7, kc == 7)
                T.copy("act", kdst(c), ps[:, :N])
            for bi, (tb, col0, rows) in enumerate(tokblocks):
                ps = PS[4 + bi]
                for kc in range(8):
                    T.mm(ps[:rows, j * 256:(j + 1) * 256], srcc(kc, col0, rows), w[:, kc, :], kc == 0, kc == 7, kc == 7)
        for bi, (tb, col0, rows) in enumerate(tokblocks):
            ko = TT("ko", (10 + 2 * (bi % 2)) * K1, [128, 512], F32)
            T.copy("dve", ko()[:rows], PS[4 + bi][:rows])
            if not sample:
                T.dma(T.dram_v(dr["k_rows"][t * TN + col0:t * TN + col0 + rows, :], "o"), ko()[:rows], is_output=True)
            else:
                T.dma(T.dram_v(dr["ks_rows"][:, :], "o"), ko()[:rows], is_output=True)
        vblocks = tokblocks if not sample else [(0, s * 8, 8) for s in range(4)]
        proj_tm("v_sb", [0, 1], vblocks, srcc, lambda bi, j: PS[bi], lambda j: j * 256)
        for bi, (tb, col0, rows) in enumerate(vblocks):
            if not sample:
                vo = TT("vo", (14 + 2 * (bi % 2)) * K1, [128, 512], F32)
                T.copy("dve", vo()[:rows], PS[bi][:rows])
                T.dma(T.dram_v(dr["v_rows"][t * TN + col0:t * TN + col0 + rows, :], "o"), vo()[:rows], is_output=True)
                T.copy("act", Vsb((kpos // 128) + bi), vo())
            else:
                T.copy("dve", Vnew[bi]()[:8], PS[bi][:8])
                T.dma(T.dram_v(dr["vs_rows"][bi * 8:(bi + 1) * 8, :], "o"), Vnew[bi]()[:8], is_output=True)
                T.copy("act", Vnb[bi]()[:8], Vnew[bi]()[:8])
        re_q = rope_evac(lambda c, sl: QrT(c, sl))
        proj_fm("q_r", 2, N, src, lambda c, ps, ps2: re_q(c, ps, ps2, N), bank0=4, sw="q_r_sw")
        re_k = rope_evac(lambda c, sl: KrT(c, sl))
        proj_fm("k_r", 2, N, src, lambda c, ps, ps2: re_k(c, ps, ps2, N), bank0=0, sw="k_r_sw")
        for hf in range(2):
            proj_tm("v_r", [2 * hf, 2 * hf + 1], vblocks, srcc,
                    lambda bi, j: PS[4 * (hf % 2) + bi], lambda j: (j % 2) * 256)
            for bi, (tb, col0, rows) in enumerate(vblocks):
                T.copy("act", Vr(bi, slice(hf * 512, (hf + 1) * 512))[:rows], PS[4 * (hf % 2) + bi][:rows])
        wg = [W.get(("g_r", j), hold=j) for j in range(4)]
        for bi, (tb, col0, rows) in enumerate(vblocks):
            psg = [PS[6], PS[7]]
            for j in range(4):
                for kc in range(8):
                    T.mm(psg[j // 2][:rows, (j % 2) * 256:(j % 2 + 1) * 256], srcc(kc, col0, rows), wg[j][:, kc, :],
                         kc == 0, kc == 7, kc == 7)
            sgr = TT("sgr", 18 * K1, [128, D], F32)
            T.act(sgr(slice(0, 512))[:rows], psg[0][:rows], AF.Silu)
            T.act(sgr(slice(512, 1024))[:rows], psg[1][:rows], AF.Silu)
            if sample:
                T.dma(V(S.t[:], S().buf), V(dr["state_in"][bi].rearrange("h d v -> d h v"), Buf("g", "dram")))
                T.copy("pool", Sbf(), S())
            ordst = lambda src_v, col0=col0, rows=rows: T.copy(
                "dve", orT(slice(0, 8), slice(col0, col0 + rows)), src_v)
            retention_block(rows, lambda h: QrT(h, slice(col0, col0 + rows)),
                            lambda h: KrT(h, slice(col0, col0 + rows)), Vr(bi),
                            din8 if sample else din, rtab8 if sample else rtab, dc8 if sample else dc128,
                            True, sgr(), ordst, 0)
            if sample:
                T.dma(T.dram_v(dr["ret_state_s"][bi].rearrange("h d v -> d h v"), "o"), V(S.t[:], S().buf),
                      is_output=True)
        if not sample and t == NT - 1:
            T.dma(T.dram_v(dr["ret_state"].rearrange("h d v -> d h v"), "o"), V(S.t[:], S().buf), is_output=True)
        if not sample:
            sb_attention_prompt(t, core_blocks_before)
        else:
            sb_attention_sample()
        st_ = [[TT("mg", ((a * 4 + b) * 2) * K1, [128, TN], F32) for b in range(4)] for a in range(2)]
        for p in range(4):
            wa = W.get(("a_sb", p))
            wr_ = W.get(("a_r", p), hold=1)
            wro = W.get(("reto", p), hold=2)
            wso = W.get(("sbo", p // 2), hold=3)
            for cc in range(2):
                fc = 2 * p + cc
                b0 = (fc % 2) * 4
                for kc in range(8):
                    T.mm(PS[b0][:, :N], wa[:, kc, cc * 128:(cc + 1) * 128], src(kc), kc == 0, kc == 7, kc == 7)
                for kc in range(8):
                    T.mm(PS[b0 + 1][:, :N], wr_[:, kc, cc * 128:(cc + 1) * 128], src(kc), kc == 0, kc == 7, kc == 7)
                for kc in range(4):
                    T.mm(PS[b0 + 2][:, :N], wso[:, kc, (fc % 4) * 128:(fc % 4 + 1) * 128], osT(kc, slice(0, N)),
                         kc == 0, kc == 3, kc == 3)
                for kc in range(8):
                    T.mm(PS[b0 + 3][:, :N], wro[:, kc, cc * 128:(cc + 1) * 128], orT(kc, slice(0, N)),
                         kc == 0, kc == 7, kc == 7)
                s1, t1, s2, t2 = [x(slice(0, N)) for x in st_[fc % 2]]
                T.act(s1, PS[b0][:, :N], AF.Sigmoid)
                T.tt("dve", t1, s1, PS[b0 + 2][:, :N], ALU.mult)
                T.act(s2, PS[b0 + 1][:, :N], AF.Sigmoid)
                T.tt("dve", t2, s2, PS[b0 + 3][:, :N], ALU.mult)
                T.tt("pool", mT(fc, slice(0, N)), t1, t2, ALU.add)
        T.dma(gcur(), V(dr["g_mix_post"][0:1, :].partition_broadcast(128), Buf("g", "dram")))
        passes = [tokblocks[i:i + 2] for i in range(0, len(tokblocks), 2)]
        for pi, pb in enumerate(passes):
            bank0 = (pi % 2) * 4
            for j in range(4):
                w = W.get(("wo", j))
                for bi, (tb, col0, rows) in enumerate(pb):
                    for kc in range(8):
                        T.mm(PS[bank0 + 2 * bi + j // 2][:rows, (j % 2) * 256:(j % 2 + 1) * 256],
                             mT(kc, slice(col0, col0 + rows)), w[:, kc, :], kc == 0, kc == 7, kc == 7)
            for bi, (tb, col0, rows) in enumerate(pb):
                postnorm_residual(PS[bank0 + 2 * bi], PS[bank0 + 2 * bi + 1], tb, rows, False, 16, bi)
        ffn(2, N, tokblocks, 2, "g_ffn2_post")
        if not sample:
            T.dma(T.dram_v(dr["y_own"][t * TN:(t + 1) * TN, :].rearrange("(b p) d -> p b d", p=128), "o"),
                  V(X.t[:], X().buf), is_output=True)
        else:
            T.dma(T.dram_v(dr["y_smp"][:, :], "o"), X(0)[:32], is_output=True)

    QrT = TT("QrT", 0, [128, 4, TN], BF16)
    KrT = TT("KrT", 4 * K1, [128, 4, TN], BF16)
    SB0 = KsT.off
    Vnew = [Tile(T, f"Vnew{i}", [128, 512], F32, SB0 + i * 2048) for i in range(4)]
    KVpg = [Tile(T, f"KVpg{i}", [128, 1024], F32, SB0 + 8192 + i * 4096) for i in range(6)]
    KTp = [Tile(T, f"KTp{i}", [128, 4, 128], BF16, SB0 + 32768 + i * 1024) for i in range(3)]
    Kpb = [Tile(T, f"Kpb{i}", [128, 512], BF16, SB0 + 44032 + i * 1024) for i in range(4)]
    Vpb = [Tile(T, f"Vpb{i}", [128, 512], BF16, SB0 + 48128 + i * 1024) for i in range(8)]
    Vnb = [Tile(T, f"Vnb{i}", [128, 512], BF16, SB0 + 56320 + i * 1024) for i in range(4)]
    SR = 6
    sE = [Tile(T, f"sE{i}", [128, 64], F32, SB0 + 36864 + i * 1024) for i in range(SR)]
    sX = [Tile(T, f"sX{i}", [128, 64], F32, SB0 + 36864 + i * 1024 + 256) for i in range(SR)]
    sA = [Tile(T, f"sA{i}", [128, 64], BF16, SB0 + 36864 + i * 1024 + 512) for i in range(SR)]
    sL = [Tile(T, f"sL{i}", [128, 64], BF16, SB0 + 36864 + i * 1024 + 768) for i in range(SR)]

    def sb_attention_sample():
        ckv_ap = dr["cache_kv"]
        nb64v = V(nb64().ap.rearrange("p a b -> p (a b)"), nb64().buf)
        for s_ in range(SEQ_PER_CORE):
            units = ["new"] + list(range(NPAGES - 1, -1, -1))
            n = len(units)
            psO = PS[0]
            psT = PS[6]
            qcols = slice(s_ * 8, s_ * 8 + 8)

            def gath(u):
                if u >= n or units[u] == "new":
                    return
                j = units[u]
                col = s_ * NPAGES + j
                T.gather(KVpg[u % 6](), ckv_ap, pgi(slice(col, col + 1)))

            def cast(u):
                if u >= n or units[u] == "new":
                    return
                T.copy("act", Kpb[u % 4](), KVpg[u % 6](slice(0, 512)))
                T.copy("dve", Vpb[u % 8](), KVpg[u % 6](slice(512, 1024)))

            def S0(u):
                if units[u] == "new":
                    return
                psK = PS[2 + u % 2]
                pkb = V(psK.ap.bitcast(BF16).rearrange("p (c t) -> p c t", c=8), psK.buf)
                for c in range(4):
                    T.tr(pkb[:, c, :], Kpb[u % 4](slice(c * 128, (c + 1) * 128)), ident_bf(), c == 3)

            def S1(u):
                new = units[u] == "new"
                R_ = 8 if new else 128
                psZe = PS[1] if u % 2 == 0 else PS[4]
                psZo = PS[5 + 2 * (u % 2)]
                if not new:
                    psK = PS[2 + u % 2]
                    kt = KTp[u % 3]
                    pkb = V(psK.ap.bitcast(BF16).rearrange("p (c t) -> p c t", c=8), psK.buf)
                    T.copy("dve", kt(), pkb[:, 0:4, :])
                for h in range(8):
                    r0 = (h % 2) * 64
                    if new:
                        lhs = KsTs(h // 2, qcols)[r0:r0 + 64]
                    else:
                        lhs = kt(h // 2)[r0:r0 + 64]
                    pz = psZe if h % 2 == 0 else psZo
                    T.mm(pz[:R_, (h // 2) * 8:(h // 2 + 1) * 8], lhs, QsT(h // 2, qcols)[r0:r0 + 64],
                         True, True, h >= 6)

            def S2(u):
                new = units[u] == "new"
                R_ = 8 if new else 128
                psZe = PS[1] if u % 2 == 0 else PS[4]
                psZo = PS[5 + 2 * (u % 2)]
                i = u % SR
                sx4 = V(sX[i]().ap.rearrange("p (c r q) -> p c r q", c=4, r=2), sX[i]().buf)
                nb4 = V(nb64().ap.rearrange("p (c r) q -> p c r q", c=4), nb64().buf)
                for par, pz in ((0, psZe), (1, psZo)):
                    T.stt(sx4[:R_, :, par, :], V(pz.ap[:R_, 0:32].rearrange("p (c q) -> p c q", c=4), pz.buf),
                          -0.125, nb4[:R_, :, par, :], ALU.mult, ALU.add)
                T.act(sE[i]()[:R_], sX[i]()[:R_], AF.Exp)
                T.act(sE[i]()[:R_], sE[i]()[:R_], AF.Ln, bias=1.0)
                T.tt("dve", sL[i]()[:R_], sX[i]()[:R_], sE[i]()[:R_], ALU.subtract)
                if new:
                    T.tt("dve", sL[i]()[:R_], sL[i]()[:R_], mask8()[:R_], ALU.mult)

            def S3(u):
                new = units[u] == "new"
                R_ = 8 if new else 128
                i = u % SR
                T.mm(psT[:, 0:64], triS()[:R_, :], sL[i]()[:R_], u == 0, True, True)
                T.tt("dve", sX[i]()[:R_], psT[:R_, 0:64], sE[i]()[:R_], ALU.subtract)

            def S4(u):
                new = units[u] == "new"
                R_ = 8 if new else 128
                i = u % SR
                if u < n - 1:
                    T.mm(psT[:, 0:64], triC()[:R_, :], sL[i]()[:R_], False, True, True)
                T.act(sA[i]()[:R_], sX[i]()[:R_], AF.Exp)
                if new:
                    T.tt("dve", sA[i]()[:R_], sA[i]()[:R_], mask8()[:R_], ALU.mult)
                vt = Vnb[s_] if new else Vpb[u % 8]
                for h in range(8):
                    r0 = (h % 2) * 64
                    T.mm(psO[r0:r0 + 64, h * 8:(h + 1) * 8], vt(slice(h * 64, (h + 1) * 64))[:R_],
                         sA[i](slice(h * 8, (h + 1) * 8))[:R_], u == 0 and h < 2, u == n - 1, h == 7)

            gath(1)
            gath(2)
            gath(3)
            for step in range(n + 4):
                gath(step + 4)
                cast(step + 1)
                if step < n:
                    S0(step)
                if 0 <= step - 1 < n:
                    S1(step - 1)
                if 0 <= step - 2 < n:
                    S2(step - 2)
                if 0 <= step - 3 < n:
                    S3(step - 3)
                if 0 <= step - 4 < n:
                    S4(step - 4)
            po = V(psO.ap[:, 0:64].rearrange("p (c r q) -> p c r q", c=4, r=2), psO.buf)
            T.copy("dve", osT(slice(0, 4), qcols)[0:64], po[0:64, :, 0, :])
            T.copy("act", osT(slice(0, 4), qcols)[64:128], po[64:128, :, 1, :])

    def run(Tdry):
        T.dry = Tdry
        W.i = 0
        W.loaded = 0
        for t in range(min(NT, stage)):
            prefix_tile(t)
        for t in range(min(NT, stage - 4)):
            own_tile(t, 16)
        if stage >= 9:
            own_tile(0, 0, N=32, tokblocks=[(0, 0, 32)], sample=True)

    run(True)
    W.plan = list(W.rec)
    small_i[0] = 0
    run(False)
    T.finish()
    return nc, T


_CACHE = {}
STAGE = 9


def kernel(**inputs):
    inputs = {k: np.asarray(v) for k, v in inputs.items()}
    if "nc" not in _CACHE:
        _CACHE["nc"] = build(STAGE)[0]
    nc = _CACHE["nc"]
    consts = host_constants()
    xp = inputs["x_prompt"]
    xs = inputs["x_sample"]
    ckv = None
    if STAGE >= 9:
        ckv = np.concatenate([np.asarray(inputs["cache_k"][0]).reshape(NPOOL * PAGE, 512),
                              np.asarray(inputs["cache_v"][0]).reshape(NPOOL * PAGE, 512)], axis=1)
    pos_first = np.arange(0, HALF)
    cos_s, sin_s = rope_tables(np.tile(8192 + np.arange(DEC), SEQ_PER_CORE))
    in_maps = []
    for c in range(8):
        b, half = c // 2, c % 2
        m = {}
        m["x_own"] = np.ascontiguousarray(xp[b, half * HALF:(half + 1) * HALF])
        m["x_pre"] = np.ascontiguousarray(xp[b, 0:HALF])
        m["x_smp"] = np.ascontiguousarray(xs[4 * c:4 * c + 4].reshape(32, D))
        if STAGE >= 9:
            m["cache_kv"] = ckv
        m["state_in"] = np.ascontiguousarray(inputs["state_ret"][0, 4 * c:4 * c + 4])
        m["ptab"] = np.ascontiguousarray(inputs["page_table"][4 * c:4 * c + 4]).astype(np.int32)
        m["flag"] = np.full((128, 1), float(half), np.float32)
        for k in ("g_ffn1_pre", "w_ffn1_gu", "w_ffn1_down", "g_ffn1_post", "g_mix_pre", "w_in", "sb_bias", "ret_gn_g",
                  "w_sb_out", "w_ret_out", "w_o", "g_mix_post", "g_ffn2_pre", "w_ffn2_gu", "w_ffn2_down",
                  "g_ffn2_post"):
            m[k] = inputs[k]
        m.update(consts)
        co, so = rope_tables(half * HALF + pos_first)
        cp, sp_ = rope_tables(pos_first)
        m["rope_cos_own"], m["rope_sin_own"] = co, so
        m["rope_cos_pre"], m["rope_sin_pre"] = cp, sp_
        m["rope_cos_smp"], m["rope_sin_smp"] = cos_s, sin_s
        in_maps.append(m)
    res = run_bass_kernel_spmd(nc, in_maps, core_ids=list(range(8)))
    r = res.results
    y_prompt = np.stack([np.concatenate([r[2 * b]["y_own"], r[2 * b + 1]["y_own"]], 0) for b in range(4)], 0)
    y_sample = np.concatenate([r[c]["y_smp"].reshape(4, DEC, D) for c in range(8)], 0)
    k_rows = np.stack([np.concatenate([r[2 * b]["k_rows"], r[2 * b + 1]["k_rows"]], 0) for b in range(4)], 0)
    v_rows = np.stack([np.concatenate([r[2 * b]["v_rows"], r[2 * b + 1]["v_rows"]], 0) for b in range(4)], 0)
    k_rows = k_rows.reshape(1, 4, SEQ, SB_H, 64)
    v_rows = v_rows.reshape(1, 4, SEQ, SB_H, 64)
    ret_p = np.stack([r[2 * b + 1]["ret_state"] for b in range(4)], 0)[None]
    ks = np.concatenate([r[c]["ks_rows"].reshape(4, DEC, SB_H, 64) for c in range(8)], 0)[None]
    vs = np.concatenate([r[c]["vs_rows"].reshape(4, DEC, SB_H, 64) for c in range(8)], 0)[None]
    ret_s = np.concatenate([r[c]["ret_state_s"] for c in range(8)], 0)[None]
    f = lambda a: np.ascontiguousarray(a, dtype=np.float32)
    return (f(y_prompt), f(y_sample), f(k_rows), f(v_rows), f(ret_p), f(ks), f(vs), f(ret_s))
```
